# Optimizing a Trainium2 kernel written in Bass

```python
import math
import numpy as np
import jax
import jax.numpy as jnp
from jax import lax

D_MODEL = 1024
BATCH = 16
SEQ = 4096
DEPTH = 1

GDN_HEADS = 4
GDN_HEAD_DIM = 128
GDN_WIDTH = GDN_HEADS * GDN_HEAD_DIM
GDN_CONV = 4
GDN_CHUNK = 64

NSA_Q_HEADS = 8
NSA_KV_HEADS = 2
NSA_HEAD_DIM = 64
NSA_WIDTH = NSA_Q_HEADS * NSA_HEAD_DIM
NSA_KV_WIDTH = NSA_KV_HEADS * NSA_HEAD_DIM
NSA_CMP_LEN = 32
NSA_CMP_STRIDE = 16
NSA_CMP_HIDDEN = 256
NSA_SEL_LEN = 64
NSA_TOPN = 16
NSA_WINDOW = 512
NSA_QBLOCK = 64
NSA_FORCE_SCORE = 1e4
NEG_INF = -1e30

MIX_WIDTH = GDN_WIDTH + NSA_WIDTH
IN_SIZES = (GDN_WIDTH, GDN_WIDTH, GDN_WIDTH, GDN_WIDTH, GDN_HEADS, GDN_HEADS,
            NSA_WIDTH, NSA_KV_WIDTH, NSA_KV_WIDTH, NSA_KV_WIDTH, NSA_KV_WIDTH,
            NSA_KV_WIDTH, NSA_KV_WIDTH, 3 * NSA_Q_HEADS)
IN_WIDTH = 4 * GDN_WIDTH + 2 * GDN_HEADS + NSA_WIDTH + 6 * NSA_KV_WIDTH + 3 * NSA_Q_HEADS

D_FF = 2816
LN_EPS = 1e-5
RMS_EPS = 1e-6
L2_EPS = 1e-6
DN_ALPHA = (2 * DEPTH) ** 0.25
DN_BETA = (8 * DEPTH) ** -0.25

kernel_name = 'hybrid_gdn_nsa_macaron_deepnorm'


def layer_norm(x, g, b):
    xf = x.astype(jnp.float32)
    mu = xf.mean(-1, keepdims=True)
    var = jnp.square(xf - mu).mean(-1, keepdims=True)
    return ((xf - mu) * lax.rsqrt(var + LN_EPS) * g.astype(jnp.float32) + b.astype(jnp.float32)).astype(x.dtype)


def swiglu(x, wg, wu, wd):
    return (jax.nn.silu(x @ wg) * (x @ wu)) @ wd


def causal_dwconv(x, w):
    k = w.shape[0]
    return lax.conv_general_dilated(x, w[:, None, :], window_strides=(1,), padding=[(k - 1, 0)],
                                    dimension_numbers=('NWC', 'WIO', 'NWC'),
                                    feature_group_count=x.shape[-1])


def l2norm(x):
    return x * lax.rsqrt(jnp.sum(x * x, -1, keepdims=True) + L2_EPS)


def masked_softmax(s, mask):
    p = jax.nn.softmax(jnp.where(mask, s, NEG_INF), axis=-1)
    return jnp.where(mask, p, 0.0)


def gated_delta_chunked(q, k, v, beta, g):
    B, T, H, Dk = q.shape
    Dv = v.shape[-1]
    C = GDN_CHUNK
    N = T // C

    def chunks(t):
        return jnp.moveaxis(t.reshape((B, N, C, H) + t.shape[3:]), 3, 1)

    q, k, v, beta = chunks(q), chunks(k), chunks(v), chunks(beta)
    gc = jnp.cumsum(chunks(g), axis=-1)
    k_beta = k * beta[..., None]
    v_beta = v * beta[..., None]
    causal = jnp.tril(jnp.ones((C, C), dtype=bool))
    strict = jnp.tril(jnp.ones((C, C), dtype=bool), -1)
    decay = jnp.exp(jnp.where(causal, gc[..., :, None] - gc[..., None, :], -jnp.inf))
    lmat = jnp.where(strict, jnp.einsum('bhncd,bhnsd->bhncs', k_beta, k) * decay, 0.0)
    amat = lmat + jnp.eye(C, dtype=jnp.float32)
    u = lax.linalg.triangular_solve(amat, v_beta, left_side=True, lower=True, unit_diagonal=True)
    w = lax.linalg.triangular_solve(amat, k_beta * jnp.exp(gc)[..., None], left_side=True,
                                    lower=True, unit_diagonal=True)
    qk = jnp.einsum('bhncd,bhnsd->bhncs', q, k) * decay

    def step(S, inp):
        q_i, k_i, u_i, w_i, gc_i, qk_i = inp
        v_new = u_i - jnp.einsum('bhck,bhkv->bhcv', w_i, S)
        o_i = (jnp.einsum('bhck,bhkv->bhcv', q_i * jnp.exp(gc_i)[..., None], S)
               + jnp.einsum('bhcs,bhsv->bhcv', qk_i, v_new))
        g_last = gc_i[..., -1:]
        S = (S * jnp.exp(g_last)[..., None]
             + jnp.einsum('bhck,bhcv->bhkv', k_i * jnp.exp(g_last - gc_i)[..., None], v_new))
        return S, o_i

    xs = tuple(jnp.moveaxis(t, 2, 0) for t in (q, k, u, w, gc, qk))
    S0 = jnp.zeros((B, H, Dk, Dv), jnp.float32)
    _, o = lax.scan(step, S0, xs)
    o = jnp.transpose(o, (1, 0, 3, 2, 4))
    return o.reshape(B, T, H, Dv)


def gdn_mixer(q, k, v, z, b, a, conv_w, a_log, dt_bias, norm_w):
    B, T, _ = q.shape
    f32 = jnp.float32
    qkv = jax.nn.silu(causal_dwconv(jnp.concatenate([q, k, v], -1), conv_w))
    q, k, v = jnp.split(qkv, 3, axis=-1)

    def heads(t):
        return t.reshape(B, T, GDN_HEADS, GDN_HEAD_DIM).astype(f32)

    q = l2norm(heads(q)) * (GDN_HEAD_DIM ** -0.5)
    k = l2norm(heads(k))
    v = heads(v)
    beta = jax.nn.sigmoid(b.astype(f32))
    g = -jnp.exp(a_log.astype(f32)) * jax.nn.softplus(a.astype(f32) + dt_bias.astype(f32))
    o = gated_delta_chunked(q, k, v, beta, g)
    o = (o * lax.rsqrt(jnp.mean(o * o, -1, keepdims=True) + RMS_EPS) * norm_w.astype(f32)
         * jax.nn.silu(heads(z)))
    return o.reshape(B, T, GDN_WIDTH).astype(q.dtype if False else z.dtype)


def nsa_mixer(q, kc, vc, ks, vs, kw, vw, gates, pos_k, pos_v, ck_w1, ck_w2, cv_w1, cv_w2):
    B, T, _ = q.shape
    out_dtype = q.dtype
    f32 = jnp.float32
    Hk, G, dh = NSA_KV_HEADS, NSA_Q_HEADS // NSA_KV_HEADS, NSA_HEAD_DIM
    l, d, ls, W, QB = NSA_CMP_LEN, NSA_CMP_STRIDE, NSA_SEL_LEN, NSA_WINDOW, NSA_QBLOCK
    scale = dh ** -0.5

    qh = q.astype(f32).reshape(B, T, Hk, G, dh).transpose(0, 2, 3, 1, 4)

    def kv_heads(t):
        return t.astype(f32).reshape(B, T, Hk, dh).transpose(0, 2, 1, 3)

    Nc = (T - l) // d + 1
    c_start = np.arange(Nc) * d
    blk_idx = c_start[:, None] + np.arange(l)[None, :]

    def compress(t, pos, w1, w2):
        blocks = kv_heads(t)[:, :, blk_idx] + pos.astype(f32)
        hid = jax.nn.gelu(blocks.reshape(B, Hk, Nc, l * dh) @ w1.astype(f32))
        return hid @ w2.astype(f32)

    k_cmp = compress(kc, pos_k, ck_w1, ck_w2)
    v_cmp = compress(vc, pos_v, cv_w1, cv_w2)
    c_end = jnp.asarray(c_start + l - 1)

    Ns = T // ls
    n_sel = min(NSA_TOPN, Ns)
    k_blocks = kv_heads(ks).reshape(B, Hk, Ns, ls, dh)
    v_blocks = kv_heads(vs).reshape(B, Hk, Ns, ls, dh)
    s_start = np.arange(Ns) * ls
    overlap = np.clip(np.minimum(c_start[:, None] + l, s_start[None, :] + ls)
                      - np.maximum(c_start[:, None], s_start[None, :]), 0, None) / l
    cmp_to_sel = jnp.asarray(overlap, f32)
    blk_ids = jnp.arange(Ns)
    gather_blocks = jax.vmap(jax.vmap(lambda blocks, ix: blocks[ix]))

    pad = ((0, 0), (0, 0), (W, 0), (0, 0))
    k_win = jnp.pad(kv_heads(kw), pad)
    v_win = jnp.pad(kv_heads(vw), pad)

    gate_h = jax.nn.sigmoid(gates.astype(f32)).reshape(B, T, Hk, G, 3).transpose(0, 2, 3, 1, 4)

    def block(s):
        t = s + jnp.arange(QB)
        q_b = lax.dynamic_slice_in_dim(qh, s, QB, axis=3) * scale
        valid_c = c_end[None, :] <= t[:, None]
        p_c = masked_softmax(jnp.einsum('bhgqd,bhnd->bhgqn', q_b, k_cmp), valid_c)
        o_c = jnp.einsum('bhgqn,bhnd->bhgqd', p_c, v_cmp)
        imp = jnp.einsum('bhgqn,nj->bhqj', p_c, cmp_to_sel)
        cur = t // ls
        visible = blk_ids[None, :] * ls <= t[:, None]
        forced = ((blk_ids[None, :] == 0) | (blk_ids[None, :] == cur[:, None])
                  | (blk_ids[None, :] == cur[:, None] - 1))
        imp = jnp.where(visible, jnp.where(forced, NSA_FORCE_SCORE, imp), -1.0)
        _, idx = lax.top_k(imp, n_sel)
        k_sel = gather_blocks(k_blocks, idx).reshape(B, Hk, QB, n_sel * ls, dh)
        v_sel = gather_blocks(v_blocks, idx).reshape(B, Hk, QB, n_sel * ls, dh)
        kpos = (idx[..., None] * ls + jnp.arange(ls)).reshape(B, Hk, QB, n_sel * ls)
        valid_s = (kpos <= t[:, None])[:, :, None]
        p_s = masked_softmax(jnp.einsum('bhgqd,bhqkd->bhgqk', q_b, k_sel), valid_s)
        o_s = jnp.einsum('bhgqk,bhqkd->bhgqd', p_s, v_sel)
        k_w = lax.dynamic_slice_in_dim(k_win, s, W + QB, axis=2)
        v_w = lax.dynamic_slice_in_dim(v_win, s, W + QB, axis=2)
        wpos = s - W + jnp.arange(W + QB)
        rel = t[:, None] - wpos[None, :]
        valid_w = (rel >= 0) & (rel < W) & (wpos[None, :] >= 0)
        p_w = masked_softmax(jnp.einsum('bhgqd,bhkd->bhgqk', q_b, k_w), valid_w)
        o_w = jnp.einsum('bhgqk,bhkd->bhgqd', p_w, v_w)
        g_b = lax.dynamic_slice_in_dim(gate_h, s, QB, axis=3)
        return g_b[..., 0:1] * o_c + g_b[..., 1:2] * o_s + g_b[..., 2:3] * o_w

    starts = jnp.arange(T // QB, dtype=jnp.int32) * QB
    out = lax.map(block, starts)
    out = out.transpose(1, 0, 4, 2, 3, 5).reshape(B, T, NSA_WIDTH)
    return out.astype(out_dtype)


def setup_inputs(seed: int = 0) -> dict:
    key = jax.random.key(seed)
    ks = jax.random.split(key, 26)
    f32 = jnp.float32
    L = DEPTH

    def nrm(k, shape, scale):
        return jax.random.normal(k, shape, f32) * scale

    dt = jnp.exp(jax.random.uniform(ks[9], (L, GDN_HEADS), f32, math.log(1e-3), math.log(1e-1)))
    return {
        'x': nrm(ks[0], (BATCH, SEQ, D_MODEL), 1.0),
        'ln1_g': 1.0 + nrm(ks[1], (L, D_MODEL), 0.02),
        'ln1_b': nrm(ks[2], (L, D_MODEL), 0.02),
        'ffn1_wg': nrm(ks[3], (L, D_MODEL, D_FF), D_MODEL ** -0.5),
        'ffn1_wu': nrm(ks[4], (L, D_MODEL, D_FF), D_MODEL ** -0.5),
        'ffn1_wd': nrm(ks[5], (L, D_FF, D_MODEL), D_FF ** -0.5 * DN_BETA),
        'w_in': nrm(ks[6], (L, D_MODEL, IN_WIDTH), D_MODEL ** -0.5),
        'gdn_conv_w': nrm(ks[7], (L, GDN_CONV, 3 * GDN_WIDTH), GDN_CONV ** -0.5),
        'gdn_a_log': jnp.log(jax.random.uniform(ks[8], (L, GDN_HEADS), f32, 1.0, 16.0)),
        'gdn_dt_bias': jnp.log(jnp.expm1(dt)),
        'gdn_norm_w': 1.0 + nrm(ks[10], (L, GDN_HEAD_DIM), 0.02),
        'nsa_cmp_pos_k': nrm(ks[11], (L, NSA_CMP_LEN, NSA_HEAD_DIM), 0.1),
        'nsa_cmp_pos_v': nrm(ks[12], (L, NSA_CMP_LEN, NSA_HEAD_DIM), 0.1),
        'nsa_cmp_k_w1': nrm(ks[13], (L, NSA_CMP_LEN * NSA_HEAD_DIM, NSA_CMP_HIDDEN), (NSA_CMP_LEN * NSA_HEAD_DIM) ** -0.5),
        'nsa_cmp_k_w2': nrm(ks[14], (L, NSA_CMP_HIDDEN, NSA_HEAD_DIM), NSA_CMP_HIDDEN ** -0.5),
        'nsa_cmp_v_w1': nrm(ks[15], (L, NSA_CMP_LEN * NSA_HEAD_DIM, NSA_CMP_HIDDEN), (NSA_CMP_LEN * NSA_HEAD_DIM) ** -0.5),
        'nsa_cmp_v_w2': nrm(ks[16], (L, NSA_CMP_HIDDEN, NSA_HEAD_DIM), NSA_CMP_HIDDEN ** -0.5),
        'w_out': nrm(ks[17], (L, MIX_WIDTH, D_MODEL), MIX_WIDTH ** -0.5 * DN_BETA),
        'ln2_g': 1.0 + nrm(ks[18], (L, D_MODEL), 0.02),
        'ln2_b': nrm(ks[19], (L, D_MODEL), 0.02),
        'ffn2_wg': nrm(ks[20], (L, D_MODEL, D_FF), D_MODEL ** -0.5),
        'ffn2_wu': nrm(ks[21], (L, D_MODEL, D_FF), D_MODEL ** -0.5),
        'ffn2_wd': nrm(ks[22], (L, D_FF, D_MODEL), D_FF ** -0.5 * DN_BETA),
        'ln3_g': 1.0 + nrm(ks[23], (L, D_MODEL), 0.02),
        'ln3_b': nrm(ks[24], (L, D_MODEL), 0.02),
    }


def reference(x, ln1_g, ln1_b, ffn1_wg, ffn1_wu, ffn1_wd, w_in, gdn_conv_w, gdn_a_log, gdn_dt_bias,
              gdn_norm_w, nsa_cmp_pos_k, nsa_cmp_pos_v, nsa_cmp_k_w1, nsa_cmp_k_w2, nsa_cmp_v_w1,
              nsa_cmp_v_w2, w_out, ln2_g, ln2_b, ffn2_wg, ffn2_wu, ffn2_wd, ln3_g, ln3_b):
    offsets = [int(o) for o in np.cumsum(IN_SIZES)[:-1]]
    for i in range(DEPTH):
        x = layer_norm(DN_ALPHA * x + 0.5 * swiglu(x, ffn1_wg[i], ffn1_wu[i], ffn1_wd[i]), ln1_g[i], ln1_b[i])
        proj = x @ w_in[i]
        (g_q, g_k, g_v, g_z, g_b, g_a, n_q, n_kc, n_vc, n_ks, n_vs, n_kw, n_vw,
         n_gate) = jnp.split(proj, offsets, axis=-1)
        o_gdn = gdn_mixer(g_q, g_k, g_v, g_z, g_b, g_a, gdn_conv_w[i], gdn_a_log[i],
                          gdn_dt_bias[i], gdn_norm_w[i])
        o_nsa = nsa_mixer(n_q, n_kc, n_vc, n_ks, n_vs, n_kw, n_vw, n_gate, nsa_cmp_pos_k[i],
                          nsa_cmp_pos_v[i], nsa_cmp_k_w1[i], nsa_cmp_k_w2[i], nsa_cmp_v_w1[i],
                          nsa_cmp_v_w2[i])
        mix = jnp.concatenate([o_gdn, o_nsa], axis=-1) @ w_out[i]
        x = layer_norm(DN_ALPHA * x + mix, ln2_g[i], ln2_b[i])
        x = layer_norm(DN_ALPHA * x + 0.5 * swiglu(x, ffn2_wg[i], ffn2_wu[i], ffn2_wd[i]), ln3_g[i], ln3_b[i])
    return x
```

```python
import numpy as np
from contextlib import ExitStack
import concourse.bass as bass
import concourse.mybir as mybir
from concourse.bass_utils import run_bass_kernel_spmd

F32 = mybir.dt.float32
BF16 = mybir.dt.bfloat16
AF = mybir.ActivationFunctionType
ALU = mybir.AluOpType
AX = mybir.AxisListType

D_MODEL = 1024
D_FF = 2816
SEQ = 4096
N_CORES = 8
SEQ_PER_CORE = 2
LN_EPS = 1e-5
DN_ALPHA = 2.0 ** 0.25


class Buf:
    __slots__ = ("name", "last_w", "readers", "psum")

    def __init__(self, name, psum=False):
        self.name = name
        self.last_w = None
        self.readers = {}
        self.psum = psum


class Op:
    __slots__ = ("eng", "fn", "is_dma", "waits", "inc", "count", "sem", "val", "idx", "presem")

    def __init__(self, eng, fn, is_dma):
        self.eng = eng
        self.fn = fn
        self.is_dma = is_dma
        self.waits = []
        self.inc = False
        self.count = 0
        self.sem = None
        self.val = 0
        self.presem = None


class Prog:
    ENGS = ("pe", "dve", "act", "pool", "sp")

    def __init__(self, nc, n_dma_sems=40):
        self.nc = nc
        self.ops = []
        self.n_dma_sems = n_dma_sems
        self.dma_slot = 0
        self.dma_slot_last = [None] * n_dma_sems
        self.last_op = {e: None for e in self.ENGS}
        self.pending_dmas = []
        self.nbar = 0
        self.same_engine_raw = True

    def eng_obj(self, e):
        nc = self.nc
        return {"pe": nc.tensor, "dve": nc.vector, "act": nc.scalar, "pool": nc.gpsimd, "sp": nc.sync}[e]

    def _add_dep(self, op, dep, kind):
        if dep is None or dep is op:
            return
        if not dep.is_dma and not op.is_dma and dep.eng == op.eng:
            if op.eng == "pe" or not self.same_engine_raw:
                return
        op.waits.append(dep)

    def op(self, eng, fn, reads=(), writes=(), dma=False):
        o = Op(eng, fn, dma)
        for r in reads:
            self._add_dep(o, r.last_w, "raw")
            if r.psum:
                for e_, rd in r.readers.items():
                    if e_ != eng:
                        self._add_dep(o, rd, "rar")
        for w in writes:
            self._add_dep(o, w.last_w, "waw")
            for rd in w.readers.values():
                self._add_dep(o, rd, "war")
        for r in reads:
            key = ("dma", len(self.ops)) if dma else eng
            r.readers[key] = o
        for w in writes:
            w.last_w = o
            w.readers = {}
        if dma:
            slot = self.dma_slot
            self.dma_slot = (slot + 1) % self.n_dma_sems
            prev = self.dma_slot_last[slot]
            o.presem = prev
            o.sem = slot
            o.val = (prev.val if prev is not None else 0) + 16
            self.dma_slot_last[slot] = o
            self.pending_dmas.append(o)
        else:
            self.last_op[eng] = o
        self.ops.append(o)
        return o

    def dma(self, out, in_, reads=(), writes=(), eng="sp"):
        e = self.eng_obj(eng)
        return self.op(eng, lambda: e.dma_start(out=out, in_=in_), reads, writes, dma=True)

    def barrier(self):
        self.nbar += 1
        o = Op("all", None, False)
        o.waits = [x for x in self.last_op.values() if x is not None and x.eng != "sp"]
        o.val = list(self.pending_dmas)
        o.count = self.nbar
        self.pending_dmas = []
        self.ops.append(o)

    def emit(self, es):
        nc = self.nc
        csem = {e: es.enter_context(nc.semaphore("c_" + e)) for e in ("pe", "dve", "act", "pool")}
        dsem = [es.enter_context(nc.semaphore("d%d" % i)) for i in range(self.n_dma_sems)]
        bsem = es.enter_context(nc.semaphore("bar"))
        for o in self.ops:
            for d in o.waits:
                if not d.is_dma:
                    d.inc = True
        cnt = {e: 0 for e in csem}
        for o in self.ops:
            if o.eng != "all" and not o.is_dma and o.inc:
                cnt[o.eng] += 1
                o.count = cnt[o.eng]
        waited = {e: {} for e in self.ENGS}

        def do_wait(eng, key, sem, val):
            w = waited[eng]
            if w.get(key, 0) >= val:
                return
            w[key] = val
            self.eng_obj(eng).wait_ge(sem, val)

        for o in self.ops:
            if o.eng == "all":
                for d in o.waits:
                    for e in self.ENGS:
                        if e != d.eng:
                            do_wait(e, d.eng, csem[d.eng], d.count)
                for d in o.val:
                    do_wait("sp", ("d", d.sem), dsem[d.sem], d.val)
                nc.sync.sem_inc(bsem, 1)
                for e in ("pe", "dve", "act", "pool"):
                    do_wait(e, "bar", bsem, o.count)
                continue
            for d in o.waits:
                if d.is_dma:
                    do_wait(o.eng, ("d", d.sem), dsem[d.sem], d.val)
                else:
                    do_wait(o.eng, d.eng, csem[d.eng], d.count)
            if o.is_dma:
                if o.presem is not None:
                    do_wait(o.eng, ("d", o.sem), dsem[o.sem], o.presem.val)
                o.fn().then_inc(dsem[o.sem], 16)
            else:
                ins = o.fn()
                if o.inc:
                    ins.then_inc(csem[o.eng], 1)
        for d in self.pending_dmas:
            do_wait("sp", ("d", d.sem), dsem[d.sem], d.val)


class T:
    def __init__(self, es, nc, name, shape, dtype, psum=False):
        if psum:
            self.t = es.enter_context(nc.psum_tensor(name, shape, dtype))
        else:
            self.t = es.enter_context(nc.sbuf_tensor(name, shape, dtype))
        self.b = Buf(name, psum=psum)

    def __getitem__(self, k):
        return self.t[k]


def ffn_phase(P, x_d, x_b, out_d, out_b, wg_d, wu_d, wd_d, lng_d, lnb_d, ident_d, ntok, banks, TT=256, pfx="f1_", pre_g=None, pre_b=None):
    nc = P.nc
    KC = D_MODEL // 128
    FC = D_FF // 128
    NS = TT // 128
    ntiles = ntok // TT
    with ExitStack() as es:
        wg = T(es, nc, pfx + "wg_sb", [128, KC, D_FF], BF16)
        wu = T(es, nc, pfx + "wu_sb", [128, KC, D_FF], BF16)
        wd = T(es, nc, pfx + "wd_sb", [128, FC, D_MODEL], BF16)
        xt = [T(es, nc, pfx + "x_tok%d" % i, [128, NS * D_MODEL], F32) for i in range(2)]
        xT = T(es, nc, pfx + "xT", [128, KC, TT], BF16)
        hT = T(es, nc, pfx + "hT", [128, FC, TT], BF16)
        sl = [T(es, nc, pfx + "silu%d" % i, [128, TT], F32) for i in range(2)]
        yt = [T(es, nc, pfx + "y_tok%d" % i, [128, D_MODEL], F32) for i in range(NS)]
        lng = T(es, nc, pfx + "lng", [128, D_MODEL], F32)
        lnb = T(es, nc, pfx + "lnb", [128, D_MODEL], F32)
        ident = T(es, nc, pfx + "identf", [128, 128], F32)
        st = T(es, nc, pfx + "bnst", [128, NS, 2, 6], F32)
        mv = T(es, nc, pfx + "bnmv", [128, NS, 2], F32)
        rs = T(es, nc, pfx + "rstd", [128, NS], F32)

        P.dma(ident[:], ident_d, writes=[ident.b])
        P.dma(lng[:], lng_d, writes=[lng.b])
        P.dma(lnb[:], lnb_d, writes=[lnb.b])
        if pre_g is not None:
            pg = T(es, nc, pfx + "pre_g", [128, D_MODEL], F32)
            pbt = T(es, nc, pfx + "pre_b", [128, D_MODEL], F32)
            P.dma(pg[:], pre_g, writes=[pg.b])
            P.dma(pbt[:], pre_b, writes=[pbt.b])
        HALF = D_FF // 2
        cast_engs = ["act", "dve", "pool"]
        ci = 0

        def cast(dst, src, rb, wb):
            nonlocal ci
            e = cast_engs[ci % 3]
            ci += 1
            if e == "act":
                P.op("act", lambda: nc.scalar.copy(out=dst, in_=src), [rb], [wb])
            elif e == "dve":
                P.op("dve", lambda: nc.vector.tensor_copy(out=dst, in_=src), [rb], [wb])
            else:
                P.op("pool", lambda: nc.gpsimd.tensor_copy(out=dst, in_=src), [rb], [wb])

        si = 0
        for w_d, w_sb in ((wg_d, wg), (wu_d, wu)):
            for kc in range(KC):
                for h in range(2):
                    s = xt[si % 2]
                    si += 1
                    P.dma(s[:, 0:HALF], w_d[kc * 128:(kc + 1) * 128, h * HALF:(h + 1) * HALF], writes=[s.b])
                    cast(w_sb[:, kc, h * HALF:(h + 1) * HALF], s[:, 0:HALF], s.b, w_sb.b)
        for j in range(FC // 2):
            s = xt[si % 2]
            si += 1
            P.dma(s[:, 0:2 * D_MODEL].rearrange("p (c d) -> p c d", c=2),
                  wd_d[j * 256:(j + 1) * 256, :].rearrange("(c p) d -> p c d", p=128), writes=[s.b])
            cast(wd[:, 2 * j:2 * j + 2, :], s[:, 0:2 * D_MODEL].rearrange("p (c d) -> p c d", c=2), s.b, wd.b)

        bi = 0

        def nb():
            nonlocal bi
            b = banks[bi % len(banks)]
            bi += 1
            return b

        def load(ti):
            x = xt[ti % 2]
            P.dma(x[:, :].rearrange("p (s d) -> p s d", s=NS),
                  x_d[ti * TT:(ti + 1) * TT, :].rearrange("(s p) d -> p s d", p=128),
                  reads=[x_b], writes=[x.b])
            if pre_g is not None:
                for s_i in range(NS):
                    xs = x[:, s_i * D_MODEL:(s_i + 1) * D_MODEL]
                    P.op("pool", lambda xs=xs: nc.gpsimd.tensor_tensor(out=xs, in0=xs, in1=pg[:, :], op=ALU.mult), [x.b, pg.b], [x.b])
                    P.op("pool", lambda xs=xs: nc.gpsimd.tensor_tensor(out=xs, in0=xs, in1=pbt[:, :], op=ALU.add), [x.b, pbt.b], [x.b])

        def transposes(ti):
            x = xt[ti % 2]
            for s in range(NS):
                for q in range(KC // 4):
                    bk = nb()
                    for j in range(4):
                        kc = q * 4 + j
                        P.op("pe", lambda bk=bk, j=j, s=s, kc=kc, x=x: nc.tensor.transpose(
                            out=bk[:, j * 128:(j + 1) * 128],
                            in_=x[:, s * D_MODEL + kc * 128: s * D_MODEL + (kc + 1) * 128],
                            identity=ident[:]), [x.b, ident.b], [bk.b])
                    P.op("act", lambda bk=bk, q=q, s=s: nc.scalar.copy(
                        out=xT[:, q * 4:(q + 1) * 4, s * 128:(s + 1) * 128],
                        in_=bk[:, :].rearrange("p (j c) -> p j c", j=4)), [bk.b], [xT.b])
            P.op("pool", lambda x=x: nc.gpsimd.tensor_scalar_mul(out=x[:, :], in0=x[:, :], scalar1=DN_ALPHA),
                 [x.b], [x.b])

        load(0)
        transposes(0)
        for ti in range(ntiles):
            x = xt[ti % 2]
            if ti + 1 < ntiles:
                load(ti + 1)
            for fc in range(FC):
                bk = nb()
                for wi, w_sb in enumerate((wg, wu)):
                    for kc in range(KC):
                        P.op("pe", lambda bk=bk, wi=wi, w_sb=w_sb, kc=kc, fc=fc: nc.tensor.matmul(
                            out=bk[:, wi * TT:(wi + 1) * TT], lhsT=w_sb[:, kc, fc * 128:(fc + 1) * 128],
                            rhs=xT[:, kc, :], start=(kc == 0), stop=(kc == KC - 1)),
                            [w_sb.b, xT.b], [bk.b])
                s_ = sl[fc % 2]
                P.op("act", lambda bk=bk, s_=s_: nc.scalar.activation(out=s_[:, :], in_=bk[:, 0:TT], func=AF.Silu),
                     [bk.b], [s_.b])
                P.op("dve", lambda bk=bk, s_=s_, fc=fc: nc.vector.tensor_tensor(
                    out=hT[:, fc, :], in0=bk[:, TT:2 * TT], in1=s_[:, :], op=ALU.mult),
                    [bk.b, s_.b], [hT.b])
            if ti + 1 < ntiles:
                transposes(ti + 1)
            for s in range(NS):
                y = yt[s]
                DW = 256
                for dh in range(D_MODEL // DW):
                    bk = nb()
                    for fc in range(FC):
                        P.op("pe", lambda bk=bk, fc=fc, s=s, dh=dh: nc.tensor.matmul(
                            out=bk[:, 0:DW], lhsT=hT[:, fc, s * 128:(s + 1) * 128],
                            rhs=wd[:, fc, dh * DW:(dh + 1) * DW], start=(fc == 0), stop=(fc == FC - 1)),
                            [hT.b, wd.b], [bk.b])
                    P.op("dve", lambda bk=bk, y=y, x=x, s=s, dh=dh: nc.vector.scalar_tensor_tensor(
                        out=y[:, dh * DW:(dh + 1) * DW], in0=bk[:, 0:DW], scalar=0.5,
                        in1=x[:, s * D_MODEL + dh * DW: s * D_MODEL + (dh + 1) * DW],
                        op0=ALU.mult, op1=ALU.add), [bk.b, x.b], [y.b])
            layer_norm(P, yt, st, mv, rs, lng, lnb)
            for s in range(NS):
                t0 = ti * TT + s * 128
                P.dma(out_d[t0:t0 + 128, :], yt[s][:, :], reads=[yt[s].b], writes=[out_b])
    P.barrier()


def layer_norm(P, ys, st, mv, rs, lng, lnb, lnexp=False, affine=True):
    nc = P.nc
    n = len(ys)
    for s, y in enumerate(ys):
        for h in range(2):
            P.op("dve", lambda h=h, s=s, y=y: nc.vector.bn_stats(out=st[:, s, h, :], in_=y[:, h * 512:(h + 1) * 512]),
                 [y.b], [st.b])
    for s in range(n):
        P.op("dve", lambda s=s: nc.vector.bn_aggr(out=mv[:, s, :], in_=st[:, s, :, :].rearrange("p a b -> p (a b)")),
             [st.b], [mv.b])
    P.op("dve", lambda: nc.vector.tensor_scalar(out=rs[:, 0:n], in0=mv[:, 0:n, 1], scalar1=LN_EPS, scalar2=None,
                                                op0=ALU.add), [mv.b], [rs.b])
    if lnexp:
        P.op("act", lambda: nc.scalar.activation(out=rs[:, 0:n], in_=rs[:, 0:n], func=AF.Ln), [rs.b], [rs.b])
        P.op("act", lambda: nc.scalar.activation(out=rs[:, 0:n], in_=rs[:, 0:n], func=AF.Exp, scale=-0.5), [rs.b], [rs.b])
    else:
        P.op("act", lambda: nc.scalar.activation(out=rs[:, 0:n], in_=rs[:, 0:n], func=AF.Sqrt), [rs.b], [rs.b])
        P.op("dve", lambda: nc.vector.reciprocal(out=rs[:, 0:n], in_=rs[:, 0:n]), [rs.b], [rs.b])
    for s, y in enumerate(ys):
        P.op("dve", lambda s=s, y=y: nc.vector.tensor_scalar(out=y[:, :], in0=y[:, :], scalar1=mv[:, s, 0:1],
                                                         scalar2=rs[:, s:s + 1], op0=ALU.subtract, op1=ALU.mult),
             [y.b, mv.b, rs.b], [y.b])
        if not affine:
            continue
        P.op("pool", lambda y=y: nc.gpsimd.tensor_tensor(out=y[:, :], in0=y[:, :], in1=lng[:, :], op=ALU.mult),
             [y.b, lng.b], [y.b])
        P.op("pool", lambda y=y: nc.gpsimd.tensor_tensor(out=y[:, :], in0=y[:, :], in1=lnb[:, :], op=ALU.add),
             [y.b, lnb.b], [y.b])


class K:
    def __init__(self, P):
        self.P = P
        self.nc = P.nc
        self.ei = 0

    def mm(self, bk, out, lhsT, rhs, rb, start=True, stop=True, skip=False):
        nc = self.nc
        self.P.op("pe", lambda: nc.tensor.matmul(out=out, lhsT=lhsT, rhs=rhs, start=start, stop=stop,
                                                 skip_group_check=skip), rb, [bk.b])

    def tr(self, bk, out, in_, ident, rb):
        nc = self.nc
        self.P.op("pe", lambda: nc.tensor.transpose(out=out, in_=in_, identity=ident), rb, [bk.b])

    def act(self, out, in_, func, rb, wb, scale=1.0, bias=0.0, accum=None):
        nc = self.nc
        self.P.op("act", lambda: nc.scalar.activation(out=out, in_=in_, func=func, bias=bias, scale=scale,
                                                      accum_out=accum), rb, wb)

    def ts(self, eng, out, in0, s1, s2, op0, op1, rb, wb):
        e = self.P.eng_obj(eng)
        if s2 is None:
            self.P.op(eng, lambda: e.tensor_scalar(out=out, in0=in0, scalar1=s1, scalar2=None, op0=op0), rb, wb)
        else:
            self.P.op(eng, lambda: e.tensor_scalar(out=out, in0=in0, scalar1=s1, scalar2=s2, op0=op0, op1=op1), rb, wb)

    def tt(self, eng, out, in0, in1, op, rb, wb):
        e = self.P.eng_obj(eng)
        self.P.op(eng, lambda: e.tensor_tensor(out=out, in0=in0, in1=in1, op=op), rb, wb)

    def stt(self, eng, out, in0, s, in1, op0, op1, rb, wb):
        e = self.P.eng_obj(eng)
        self.P.op(eng, lambda: e.scalar_tensor_tensor(out=out, in0=in0, scalar=s, in1=in1, op0=op0, op1=op1), rb, wb)

    def cp(self, out, in_, rb, wb, eng=None):
        nc = self.nc
        if eng is None:
            eng = ("act", "dve")[self.ei % 2]
            self.ei += 1
        if eng == "act":
            self.P.op("act", lambda: nc.scalar.copy(out=out, in_=in_), rb, wb)
        elif eng == "dve":
            self.P.op("dve", lambda: nc.vector.tensor_copy(out=out, in_=in_), rb, wb)
        else:
            self.P.op("pool", lambda: nc.gpsimd.tensor_copy(out=out, in_=in_), rb, wb)

    def memset(self, eng, ap, val, wb):
        e = self.P.eng_obj(eng)
        self.P.op(eng, lambda: e.memset(ap, val), [], wb)


N_IN = 3360
import os as _os
SKIP = set(_os.environ.get('KSKIP', '').split(','))
FM_COLS = 2048
TM_Z = 2048
TM_S = 2560
P1_COLS = 2592


def phase_b1(P, D, banks, nseq, T_):
    nc = P.nc
    k = K(P)
    NT = T_ // 128
    NC = (T_ - 32) // 16 + 1
    bi = 0

    def nb():
        nonlocal bi
        b = banks[bi % 6]
        bi += 1
        return b

    with ExitStack() as es:
        w1p = T(es, nc, "b1_w1p", [128, 8, 768], BF16)
        stg = [T(es, nc, "b1_b1stg%d" % i, [128, 1024], F32) for i in range(2)]
        cw1 = [T(es, nc, "b1_cw1_%d" % i, [128, 32, 256], BF16) for i in range(2)]
        cw2k = T(es, nc, "b1_cw2k", [128, 2, 128], BF16)
        cw2v = T(es, nc, "b1_cw2v", [128, 2, 64], BF16)
        posT = [T(es, nc, "b1_posT%d" % i, [128, 32], BF16) for i in range(2)]
        pb = T(es, nc, "b1_posb", [128, 4], F32)
        ident = T(es, nc, "b1_b1ident", [128, 128], F32)
        xt = [T(es, nc, "b1_b1x%d" % i, [128, D_MODEL], F32) for i in range(2)]
        xT = T(es, nc, "b1_b1xT", [128, 8, 128], BF16)
        kcT = T(es, nc, "b1_kcT", [128, T_], BF16)
        vcT = T(es, nc, "b1_vcT", [128, T_], BF16)
        ksT = T(es, nc, "b1_b1ksT", [128, T_], BF16)
        ks1T = T(es, nc, "b1_b1ks1T", [64, T_], BF16)
        kwT = T(es, nc, "b1_b1kwT", [128, T_], BF16)
        vst = T(es, nc, "b1_b1vs", [128, NT, 2, 65], BF16)
        vwt = T(es, nc, "b1_b1vw", [128, NT, 2, 65], BF16)
        kcmp = T(es, nc, "b1_b1kcmp", [128, 256], BF16)
        vcmp = T(es, nc, "b1_b1vcmp", [128, 2, 2, 129], BF16)
        c2s = T(es, nc, "b1_b1c2s", [128, 2, 64], F32)
        hid = [T(es, nc, "b1_hid%d" % i, [128, 2, 256], BF16) for i in range(2)]
        gx = [T(es, nc, "b1_gx%d" % i, [128, 256], F32) for i in range(2)]
        gu = [T(es, nc, "b1_gu%d" % i, [128, 256], F32) for i in range(2)]

        P.dma(ident[:], D["ident"], writes=[ident.b])
        P.dma(c2s[:], D["c2s"], writes=[c2s.b])
        si = 0
        for kc in range(8):
            s = stg[si % 2]; si += 1
            P.dma(s[:, 0:768], D["w_in"][kc * 128:(kc + 1) * 128, P1_COLS:N_IN], writes=[s.b])
            k.cp(w1p[:, kc, :], s[:, 0:768], [s.b], [w1p.b])
        for kv in range(2):
            for q in range(8):
                s = stg[si % 2]; si += 1
                P.dma(s[:, :].rearrange("p (l j) -> p l j", l=4), D["cw1"][kv, :, q * 4:(q + 1) * 4, :], writes=[s.b])
                k.cp(cw1[kv][:, q * 4:(q + 1) * 4, :], s[:, :].rearrange("p (l j) -> p l j", l=4), [s.b], [cw1[kv].b])
        s = stg[si % 2]; si += 1
        P.dma(s[:, 0:256].rearrange("p (c j) -> p c j", c=2), D["cw2k"], writes=[s.b])
        k.cp(cw2k[:, :, :], s[:, 0:256].rearrange("p (c j) -> p c j", c=2), [s.b], [cw2k.b])
        s = stg[si % 2]; si += 1
        P.dma(s[:, 0:128].rearrange("p (c j) -> p c j", c=2), D["cw2v"], writes=[s.b])
        k.cp(cw2v[:, :, :], s[:, 0:128].rearrange("p (c j) -> p c j", c=2), [s.b], [cw2v.b])
        for kv in range(2):
            s = stg[si % 2]; si += 1
            P.dma(s[:, 0:32], D["posT"][kv], writes=[s.b])
            k.cp(posT[kv][:, :], s[:, 0:32], [s.b], [posT[kv].b])
        bk = nb()
        for kv in range(2 if 'pb' not in SKIP else 0):
            for jc in range(2):
                col = kv * 2 + jc
                for l in range(32):
                    k.mm(bk, bk[:, col:col + 1], cw1[kv][0:64, l, jc * 128:(jc + 1) * 128], posT[kv][0:64, l:l + 1],
                         [cw1[kv].b, posT[kv].b], start=(l == 0), stop=(l == 31), skip=True)
        if 'pb' not in SKIP:
            k.cp(pb[:, :], bk[:, 0:4], [bk.b], [pb.b], eng="dve")
        else:
            k.memset('dve', pb[:, :], 0.0, [pb.b])

        for sq in range(nseq):
            k.memset("pool", vst[:, :, :, 64:65], 1.0, [vst.b])
            k.memset("pool", vwt[:, :, :, 64:65], 1.0, [vwt.b])
            k.memset("pool", vcmp[:, :, :, 0:65], 0.0, [vcmp.b])
            k.memset("pool", kcmp[:, :], 0.0, [kcmp.b])
            k.memset("pool", vcmp[:, :, :, 64:65], 1.0, [vcmp.b])
            for nch in range(2):
                for hk in range(2):
                    k.cp(vcmp[:, nch, hk, 65:129], c2s[:, nch, :], [c2s.b], [vcmp.b], eng="pool")

            def load(ti):
                x = xt[ti % 2]
                t0 = sq * T_ + ti * 128
                P.dma(x[:, :], D["x1"][t0:t0 + 128, :], reads=[D["x1_b"]], writes=[x.b])
            load(0)
            for ti in range(NT if 'proj' not in SKIP else 0):
                x = xt[ti % 2]
                if ti + 1 < NT:
                    load(ti + 1)
                for q in range(2):
                    bk = nb()
                    for j in range(4):
                        kc = q * 4 + j
                        k.tr(bk, bk[:, j * 128:(j + 1) * 128], x[:, kc * 128:(kc + 1) * 128], ident[:], [x.b, ident.b])
                    k.cp(xT[:, q * 4:(q + 1) * 4, :], bk[:, :].rearrange("p (j c) -> p j c", j=4), [bk.b], [xT.b])
                bk = nb()
                for c in range(4):
                    for kc in range(8):
                        k.mm(bk, bk[:, c * 128:(c + 1) * 128], w1p[:, kc, c * 128:(c + 1) * 128], xT[:, kc, :],
                             [w1p.b, xT.b], start=(kc == 0), stop=(kc == 7))
                for c, dst in enumerate((kcT, vcT, ksT, kwT)):
                    k.cp(dst[:, ti * 128:(ti + 1) * 128], bk[:, c * 128:(c + 1) * 128], [bk.b], [dst.b])
                bk = nb()
                for kc in range(8):
                    k.mm(bk, bk[0:64, 0:128], w1p[:, kc, 320:384], xT[:, kc, :], [w1p.b, xT.b], start=(kc == 0), stop=(kc == 7))
                k.cp(ks1T[0:64, ti * 128:(ti + 1) * 128], bk[0:64, 0:128], [bk.b], [ks1T.b])
                bk = nb()
                for kc in range(8):
                    k.mm(bk, bk[:, 0:256], xT[:, kc, :], w1p[:, kc, 512:768], [w1p.b, xT.b], start=(kc == 0), stop=(kc == 7))
                k.cp(vst[:, ti, :, 0:64], bk[:, 0:128].rearrange("p (h d) -> p h d", h=2), [bk.b], [vst.b])
                k.cp(vwt[:, ti, :, 0:64], bk[:, 128:256].rearrange("p (h d) -> p h d", h=2), [bk.b], [vwt.b])
            for kv, src in enumerate((kcT, vcT) if 'cmp' not in SKIP else ()):
                for hk in range(2):
                    h_ = hid[hk]
                    for jc in range(2):
                        bk = nb()
                        for l in range(32):
                            k.mm(bk, bk[:, 0:NC], cw1[kv][hk * 64:(hk + 1) * 64, l, jc * 128:(jc + 1) * 128],
                                 src[hk * 64:(hk + 1) * 64, l:l + 16 * (NC - 1) + 1:16], [cw1[kv].b, src.b],
                                 start=(l == 0), stop=(l == 31))
                        x_ = gx[jc]; u_ = gu[jc]
                        col = kv * 2 + jc
                        k.ts("dve", x_[:, 0:NC], bk[:, 0:NC], pb[:, col:col + 1], None, ALU.add, None, [bk.b, pb.b], [x_.b])
                        k.tt("dve", u_[:, 0:NC], x_[:, 0:NC], x_[:, 0:NC], ALU.mult, [x_.b], [u_.b])
                        k.ts("dve", u_[:, 0:NC], u_[:, 0:NC], 0.044715, 1.0, ALU.mult, ALU.add, [u_.b], [u_.b])
                        k.tt("dve", u_[:, 0:NC], u_[:, 0:NC], x_[:, 0:NC], ALU.mult, [u_.b, x_.b], [u_.b])
                        k.act(u_[:, 0:NC], u_[:, 0:NC], AF.Exp, [u_.b], [u_.b], scale=-1.5957691216)
                        k.ts("dve", u_[:, 0:NC], u_[:, 0:NC], 1.0, None, ALU.add, None, [u_.b], [u_.b])
                        P.op("dve", lambda u_=u_: nc.vector.reciprocal(out=u_[:, 0:NC], in_=u_[:, 0:NC]), [u_.b], [u_.b])
                        k.tt("dve", h_[:, jc, 0:NC], u_[:, 0:NC], x_[:, 0:NC], ALU.mult, [u_.b, x_.b], [h_.b])
                    if kv == 0:
                        bk = nb()
                        for jc in range(2):
                            k.mm(bk, bk[:, 0:NC], cw2k[:, jc, :], h_[:, jc, 0:NC], [cw2k.b, h_.b], start=(jc == 0), stop=(jc == 1))
                        k.cp(kcmp[hk * 64:(hk + 1) * 64, 0:NC], bk[hk * 64:(hk + 1) * 64, 0:NC], [bk.b], [kcmp.b])
                    else:
                        for nch in range(2):
                            rows = min(NC - nch * 128, 128)
                            if rows <= 0:
                                continue
                            bk = nb()
                            for jc in range(2):
                                k.mm(bk, bk[0:rows, 0:64], h_[:, jc, nch * 128:nch * 128 + rows], cw2v[:, jc, :],
                                     [cw2v.b, h_.b], start=(jc == 0), stop=(jc == 1))
                            k.cp(vcmp[0:rows, nch, hk, 0:64], bk[0:rows, 0:64], [bk.b], [vcmp.b])
            if 'state' in SKIP:
                continue
            sb = D["state_b"]
            P.dma(D["ksT"][sq], ksT[:, :], reads=[ksT.b], writes=[sb])
            P.dma(D["kwT"][sq], kwT[:, :], reads=[kwT.b], writes=[sb])
            P.dma(D["ks1T"][sq], ks1T[0:64, :], reads=[ks1T.b], writes=[sb])
            P.dma(D["vs"][sq], vst[:, :, :, :], reads=[vst.b], writes=[sb])
            P.dma(D["vw"][sq], vwt[:, :, :, :], reads=[vwt.b], writes=[sb])
            P.dma(D["kcmp"][sq], kcmp[:, :], reads=[kcmp.b], writes=[sb])
            P.dma(D["vcmp"][sq], vcmp[:, :, :, :], reads=[vcmp.b], writes=[sb])
    P.barrier()


def phase_b2(P, D, banks, nseq, T_, dbg=False):
    nc = P.nc
    k = K(P)
    NT = T_ // 128
    bi = 0

    def nb():
        nonlocal bi
        b = banks[bi % 5]
        bi += 1
        return b
    bselAB, bwin = (banks[5], banks[6]), banks[7]

    with ExitStack() as es:
        def S(name, shape, dt=F32):
            return T(es, nc, "b2" + name, shape, dt)
        win = S("win", [128, 8, P1_COLS], BF16)
        wout = S("wout", [128, 8, D_MODEL], BF16)
        ident, U, NegU, ones, M1, M2 = [S(n, [128, 128]) for n in ("ident", "U", "NegU", "ones", "M1", "M2")]
        ident4 = S("ident4", [128, 512], BF16)
        CBT = S("CBT", [128, 128], BF16)
        WBT = S("WBT", [128, 128], BF16)
        convw = S("convw", [128, 12, 4])
        alog = S("alog", [128, 4]); dtb = S("dtb", [128, 4]); negA = S("negA", [128, 4])
        normw = S("normw", [128, 512])
        ksE = [S("ksE%d" % i, [128, T_], BF16) for i in range(2)]
        NT2 = S("NT2", [128, 128])
        vs = S("vs", [128, NT, 2, 65], BF16)
        kcmp = S("kcmp", [128, 256], BF16); vcmp = S("vcmp", [128, 2, 2, 129], BF16)
        xt = [S("x%d" % i, [128, D_MODEL]) for i in range(2)]
        xT = S("xT", [128, 8, 128], BF16)
        raw = S("raw", [128, 12, 131])
        cacc = [S("cacc%d" % i, [128, 128]) for i in range(4)]
        sil = S("sil", [128, 8, 128])
        silb = [Buf("silb%d" % i) for i in range(8)]
        sq = [S("sq%d" % i, [128, 128]) for i in range(2)]
        rn = [S("rn%d" % i, [128, 128]) for i in range(2)]
        sm = S("sm", [128, 32]); smt = S("smt", [128, 32])
        gcs = S("gcs", [128, 8])
        class NS:
            pass
        PB = []
        for par in range(2):
            pb = NS()
            sfx = "_p%d" % par
            pb.qnb = S("qnb" + sfx, [128, 4, 128], BF16); pb.knb = S("knb" + sfx, [128, 4, 128], BF16)
            pb.kn = S("kn" + sfx, [128, 4, 128]); pb.silv = S("silv" + sfx, [128, 4, 128])
            pb.silvb = [Buf("silvb%d" % i + sfx) for i in range(4)]
            pb.nqT = S("nqT" + sfx, [128, 4, 128], BF16); pb.zt = S("zt" + sfx, [128, 512])
            pb.QN = [S("QN%d" % i + sfx, [128, 512], BF16) for i in range(2)]
            pb.beta = S("beta" + sfx, [128, 4]); pb.nbeta = S("nbeta" + sfx, [128, 4]); pb.g_ = S("g" + sfx, [128, 4])
            pb.egc = S("egc" + sfx, [128, 4]); pb.egl = S("egl" + sfx, [128, 4]); pb.eglm = S("eglm" + sfx, [128, 4])
            pb.gcp = S("gcp" + sfx, [128, 8]); pb.sg = S("sg" + sfx, [128, 24]); pb.cmbtb = S("cmbtb" + sfx, [128, 2, 128], BF16); pb.kf = S("kf" + sfx, [128, 2, 64])
            pb.Gm = [S("Gm%d" % h + sfx, [128, 128]) for h in range(4)]
            pb.kwin = S("kwin" + sfx, [128, 640], BF16); pb.vwin = S("vwin" + sfx, [128, 5, 2, 65], BF16)
            PB.append(pb)
        HB = []
        for h in range(4):
            HB.append((S("DecS%d" % h, [128, 128]), S("DecTi%d" % h, [128, 128]),
                       [S("Lb%d_%d" % (h, i), [128, 3, 128]) for i in range(2)], S("TT%d" % h, [128, 128]),
                       S("QKdT%d" % h, [128, 128], BF16), S("vtok%d" % h, [128, 128]), S("kd%d" % h, [128, 128], BF16),
                       S("t1_%d" % h, [128, 128]), S("t2_%d" % h, [128, 128]), S("vnb%d" % h, [128, 128], BF16)))
        junk = [S("junk%d" % h, [128, 128], BF16) for h in range(4)]
        ogs = [S("og%d" % i, [128, 4, 128]) for i in range(2)]
        mss = [S("ms%d" % i, [128, 4]) for i in range(2)]
        ogbs = [[Buf("ogb%d_%d" % (i, h)) for h in range(4)] for i in range(2)]
        msbs = [[Buf("msb%d_%d" % (i, h)) for h in range(4)] for i in range(2)]
        rstd = S("rstd", [128, 4])
        St = [S("S%d" % h, [128, 128]) for h in range(4)]
        Sb = [S("Sb%d" % h, [128, 128], BF16) for h in range(4)]
        casb = [S("casb%d" % i, [128, 260]) for i in range(2)]
        wsb = [S("wsb%d" % i, [128, 260]) for i in range(2)]
        omixs = [S("omix%d" % i, [128, D_MODEL]) for i in range(2)]; omixT = S("omixT", [128, 8, 128], BF16)
        y = S("y", [128, D_MODEL])
        st = S("bnst", [128, 1, 2, 6]); mv = S("bnmv", [128, 1, 2]); rs = S("rs", [128, 1])
        SKEW = int(_os.environ.get("SKEW", "2"))
        NPT = SKEW + 2
        pT = [S("pT%d" % i, [128, 512], BF16) for i in range(NPT)]
        cmbt = S("cmbt", [128, 2, 128])
        lc = S("lc", [128, 4]); rl = S("rl", [128, 4]); imp = S("imp", [128, 64]); impt = S("impt", [128, 64])
        m8a = S("m8a", [128, 8]); m8b = S("m8b", [128, 8])
        Lg = S("Lg", [128, 4, 3]); coef = S("coef", [128, 4, 3]); tn = S("tn", [128, 64])

        for t_, nm in ((ident, "ident"), (U, "U"), (NegU, "NegU"), (ones, "ones"), (M1, "M1"), (M2, "M2"),
                       (convw, "convw"), (alog, "alog"), (dtb, "dtb"), (normw, "normw")):
            P.dma(t_.t[tuple(slice(None) for _ in t_.t.shape)], D[nm], writes=[t_.b])
        si = 0
        for t_, nm, w_ in ((ident4, "ident4", 512), (CBT, "CBT", 128), (WBT, "WBT", 128)):
            s = xt[si % 2]; si += 1
            P.dma(s[:, 0:w_], D[nm], writes=[s.b])
            k.cp(t_[:, :], s[:, 0:w_], [s.b], [t_.b])
        for kc in range(8):
            for c3 in range(3):
                s = xt[si % 2]; si += 1
                P.dma(s[:, 0:864], D["w_in"][kc * 128:(kc + 1) * 128, c3 * 864:(c3 + 1) * 864], writes=[s.b])
                k.cp(win[:, kc, c3 * 864:(c3 + 1) * 864], s[:, 0:864], [s.b], [win.b])
        for kc in range(8):
            s = xt[si % 2]; si += 1
            P.dma(s[:, :], D["w_out"][kc * 128:(kc + 1) * 128, :], writes=[s.b])
            k.cp(wout[:, kc, :], s[:, :], [s.b], [wout.b])
        for c4 in range(T_ // 1024 if T_ >= 1024 else 1):
            w_ = min(1024, T_)
            s = xt[si % 2]; si += 1
            P.dma(s[64:128, 0:w_], D["Econst"][:, c4 * w_:(c4 + 1) * w_], writes=[s.b])
            for i in range(2):
                k.cp(ksE[i][64:128, c4 * w_:(c4 + 1) * w_], s[64:128, 0:w_], [s.b], [ksE[i].b])
        k.memset("pool", NT2[:, :], 0.0, [NT2.b])
        k.act(negA[:, :], alog[:, :], AF.Exp, [alog.b], [negA.b])
        k.ts("dve", negA[:, :], negA[:, :], -1.0, None, ALU.mult, None, [negA.b], [negA.b])


        def gdn_stream(h, pb, par):
            og = ogs[par]; ogb = ogbs[par]; ms = mss[par]; msb = msbs[par]
            DecS, DecTi, Lb, TT_, QKdT, vtok, kd, t1, t2, vnb = HB[h]
            Gm = pb.Gm[h]
            bd = nb()
            k.mm(bd, bd[:, 0:128], Gm[:, :], NegU[:, :], [NegU.b, Gm.b])
            k.mm(bd, bd[:, 128:256], pb.knb[:, h, :], pb.knb[:, h, :], [pb.knb.b])
            k.mm(bd, bd[:, 256:384], pb.knb[:, h, :], pb.qnb[:, h, :], [pb.knb.b, pb.qnb.b])
            k.tt("dve", DecS[:, :], bd[:, 0:128], M1[:, :], ALU.add, [bd.b, M1.b], [DecS.b])
            k.stt("dve", DecTi[:, :], bd[:, 0:128], -1.0, M2[:, :], ALU.mult, ALU.add, [bd.b, M2.b], [DecTi.b])
            k.act(DecS[:, :], DecS[:, :], AF.Exp, [DecS.b, pb.gcp.b], [DecS.b], bias=pb.gcp[:, h:h + 1])
            k.act(DecTi[:, :], DecTi[:, :], AF.Exp, [DecTi.b, pb.gcp.b], [DecTi.b], bias=pb.gcp[:, 4 + h:5 + h])
            k.stt("dve", Lb[0][:, 0, :], bd[:, 128:256], pb.beta[:, h:h + 1], DecS[:, :], ALU.mult, ALU.mult,
                  [bd.b, pb.beta.b, DecS.b], [Lb[0].b])
            k.tt("dve", QKdT[:, :], bd[:, 256:384], DecTi[:, :], ALU.mult, [bd.b, DecTi.b], [QKdT.b])
            yield
            bt = nb()
            k.tr(bt, bt[:, 0:128], Lb[0][:, 0, :], ident[:, :], [Lb[0].b, ident.b])
            k.cp(Lb[0][:, 1, :], bt[:, 0:128], [bt.b], [Lb[0].b], eng="act")
            k.stt("dve", Lb[1][:, 2, :], bt[:, 0:128], -1.0, ident[:, :], ALU.mult, ALU.add, [bt.b, ident.b], [Lb[1].b])
            yield
            for lvl in range(1, 8):
                cur = Lb[(lvl - 1) % 2]
                nxt = Lb[lvl % 2]
                bk = nb()
                if lvl <= 6:
                    k.mm(bk, bk[:, 0:128], cur[:, 1, :], cur[:, 0, :], [cur.b])
                    k.mm(bk, bk[:, 128:256], cur[:, 0, :], cur[:, 1, :], [cur.b])
                if lvl >= 2:
                    k.mm(bk, bk[:, 256:384], cur[:, 0, :], cur[:, 2, :], [cur.b])
                if lvl <= 6:
                    k.cp(nxt[:, 0:2, :], bk[:, 0:256].rearrange("p (a c) -> p a c", a=2), [bk.b], [nxt.b], eng="act")
                if lvl >= 2:
                    dst = nxt[:, 2, :] if lvl <= 6 else TT_[:, :]
                    dstb = nxt.b if lvl <= 6 else TT_.b
                    k.tt("dve", dst, bk[:, 256:384], cur[:, 2, :], ALU.add, [bk.b, cur.b], [dstb])
                yield
            bq = nb()
            k.mm(bq, bq[:, 0:128], pb.knb[:, h, :], Sb[h][:, :], [pb.knb.b, Sb[h].b])
            k.mm(bq, bq[:, 128:256], pb.qnb[:, h, :], Sb[h][:, :], [pb.qnb.b, Sb[h].b])
            k.tr(bq, bq[:, 256:384], pb.silv[:, h, :], ident[:, :], [pb.silv.b, ident.b])
            k.tr(bq, bq[:, 384:512], pb.kn[:, h, :], ident[:, :], [pb.kn.b, ident.b])
            k.cp(vtok[:, :], bq[:, 256:384], [bq.b], [vtok.b], eng="act")
            k.ts("dve", kd[:, :], bq[:, 384:512], pb.eglm[:, h:h + 1], None, ALU.mult, None, [bq.b, pb.eglm.b], [kd.b])
            k.stt("dve", t1[:, :], bq[:, 0:128], pb.egc[:, h:h + 1], vtok[:, :], ALU.mult, ALU.subtract,
                  [bq.b, pb.egc.b, vtok.b], [t1.b])
            k.ts("dve", t1[:, :], t1[:, :], pb.nbeta[:, h:h + 1], None, ALU.mult, None, [t1.b, pb.nbeta.b], [t1.b])
            k.ts("dve", t2[:, :], bq[:, 128:256], pb.egc[:, h:h + 1], None, ALU.mult, None, [bq.b, pb.egc.b], [t2.b])
            yield
            bv = nb()
            k.mm(bv, bv[:, 0:128], TT_[:, :], t1[:, :], [TT_.b, t1.b])
            k.cp(vnb[:, :], bv[:, 0:128], [bv.b], [vnb.b], eng="act")
            yield
            bw = nb()
            k.mm(bw, bw[:, 128:256], QKdT[:, :], vnb[:, :], [QKdT.b, vnb.b])
            k.mm(bw, bw[:, 256:384], kd[:, :], vnb[:, :], [kd.b, vnb.b])
            k.tt("dve", og[:, h, :], bw[:, 128:256], t2[:, :], ALU.add, [bw.b, t2.b], [ogb[h]])
            k.stt("dve", St[h][:, :], St[h][:, :], pb.egl[:, h:h + 1], bw[:, 256:384], ALU.mult, ALU.add,
                  [St[h].b, pb.egl.b, bw.b], [St[h].b])
            k.cp(Sb[h][:, :], St[h][:, :], [St[h].b], [Sb[h].b], eng="pool")
            k.act(junk[h][:, :], og[:, h, :], AF.Square, [ogb[h]], [junk[h].b, msb[h]], accum=ms[:, h:h + 1])
            yield

        def nsa_stream(ti, pb, par):
            omix = omixs[par]
            nchunks = 1 if ti < 16 else 2
            pi = 0
            k0 = max(0, ti - 4)
            for hk in range(2):
                hs = slice(hk * 64, (hk + 1) * 64)
                qrhs = pb.nqT[hs, :, :].rearrange("p g q -> p (g q)")
                bCa = nb()
                bCb = nb()
                for nch in range(nchunks):
                    bk = nb()
                    k.mm(bk, bk[:, :], kcmp[hs, nch * 128:(nch + 1) * 128], qrhs, [kcmp.b, pb.nqT.b], start=True, stop=False)
                    k.mm(bk, bk[:, :], pb.cmbtb[:, nch, :], ident4[:, :], [pb.cmbtb.b, ident4.b], start=False, stop=True)
                    p_ = pT[pi % NPT]; pi += 1
                    k.act(p_[:, :], bk[:, :], AF.Exp, [bk.b], [p_.b], scale=0.125)
                    for g in range(4):
                        k.mm(bCa, bCa[:, g * 65:(g + 1) * 65], p_[:, g * 128:(g + 1) * 128], vcmp[:, nch, hk, 0:65],
                             [p_.b, vcmp.b], start=(nch == 0 and g == 0), stop=(nch == nchunks - 1), skip=True)
                    for g in range(4):
                        k.mm(bCb, bCb[:, g * 64:(g + 1) * 64], p_[:, g * 128:(g + 1) * 128], vcmp[:, nch, hk, 65:129],
                             [p_.b, vcmp.b], start=(nch == 0 and g == 0), stop=(nch == nchunks - 1), skip=True)
                cs = casb[hk]
                k.cp(cs[:, :], bCa[:, 0:260], [bCa.b], [cs.b], eng="act")
                k.ts("dve", rl[:, :], cs[:, 64:260:65], 1e-30, None, ALU.max, None, [cs.b], [rl.b])
                P.op("dve", lambda: nc.vector.reciprocal(out=rl[:, :], in_=rl[:, :]), [rl.b], [rl.b])
                k.ts("dve", imp[:, :], bCb[:, 0:64], rl[:, 0:1], None, ALU.mult, None, [bCb.b, rl.b], [imp.b])
                for g in range(1, 4):
                    k.stt("dve", imp[:, :], bCb[:, g * 64:(g + 1) * 64], rl[:, g:g + 1], imp[:, :], ALU.mult, ALU.add,
                          [bCb.b, rl.b, imp.b], [imp.b])
                yield
                k.tt("dve", imp[:, :], imp[:, :], pb.kf[:, 0, :], ALU.mult, [imp.b, pb.kf.b], [imp.b])
                k.tt("dve", imp[:, :], imp[:, :], pb.kf[:, 1, :], ALU.add, [imp.b, pb.kf.b], [imp.b])
                P.op("dve", lambda: nc.vector.max(out=m8a[:, :], in_=imp[:, :]), [imp.b], [m8a.b])
                P.op("dve", lambda: nc.vector.match_replace(out=impt[:, :], in_to_replace=m8a[:, :], in_values=imp[:, :],
                                                            imm_value=-1e9), [imp.b, m8a.b], [impt.b])
                P.op("dve", lambda: nc.vector.max(out=m8b[:, :], in_=impt[:, :]), [impt.b], [m8b.b])
                k.ts("dve", NT2[:, 64:128], imp[:, :], m8b[:, 7:8], -30000.0, ALU.is_lt, ALU.mult, [imp.b, m8b.b], [NT2.b])
                bt = nb()
                k.tr(bt, bt[:, 0:128], NT2[:, :], ident[:, :], [NT2.b, ident.b])
                k.cp(pb.QN[hk][64:128, :].rearrange("p (g q) -> p g q", g=4),
                     bt[64:128, 0:128].rearrange("p (o q) -> p o q", o=1).to_broadcast([64, 4, 128]), [bt.b], [pb.QN[hk].b], eng="act")
                yield
            for hk in range(2):
                hs = slice(hk * 64, (hk + 1) * 64)
                qrhs = pb.nqT[hs, :, :].rearrange("p g q -> p (g q)")
                pend = []
                for kc in range(k0, ti + 1):
                    bk = nb()
                    last_extra = (kc == ti) or (kc == ti - 4)
                    k.mm(bk, bk[:, :], pb.kwin[hs, (kc - k0) * 128:(kc - k0 + 1) * 128], qrhs, [pb.kwin.b, pb.nqT.b],
                         start=True, stop=not last_extra)
                    if kc == ti:
                        k.mm(bk, bk[:, :], CBT[:, :], ident4[:, :], [CBT.b, ident4.b], start=False, stop=True)
                    elif kc == ti - 4:
                        k.mm(bk, bk[:, :], WBT[:, :], ident4[:, :], [WBT.b, ident4.b], start=False, stop=True)
                    p_ = pT[pi % NPT]; pi += 1
                    k.act(p_[:, :], bk[:, :], AF.Exp, [bk.b], [p_.b], scale=0.125)
                    if len(pend) >= SKEW:
                        pend.pop(0)()
                    def pv(p_=p_, kc=kc, hk=hk):
                        for g in range(4):
                            k.mm(bwin, bwin[:, g * 65:(g + 1) * 65], p_[:, g * 128:(g + 1) * 128], pb.vwin[:, kc - k0, hk, :],
                                 [p_.b, pb.vwin.b], start=(kc == k0 and g == 0), stop=(kc == ti), skip=True)
                    pend.append(pv)
                    yield
                while pend:
                    pend.pop(0)()
                k.cp(wsb[hk][:, :], bwin[:, 0:260], [bwin.b], [wsb[hk].b], eng="dve")
                yield
            for hk in range(2):
                bsel = bselAB[hk]
                pend = []
                for kc in range(ti + 1):
                    bk = nb()
                    k.mm(bk, bk[:, :], ksE[hk][:, kc * 128:(kc + 1) * 128], pb.QN[hk][:, :], [ksE[hk].b, pb.QN[hk].b],
                         start=True, stop=(kc != ti))
                    if kc == ti:
                        k.mm(bk, bk[:, :], CBT[:, :], ident4[:, :], [CBT.b, ident4.b], start=False, stop=True)
                    p_ = pT[pi % NPT]; pi += 1
                    k.act(p_[:, :], bk[:, :], AF.Exp, [bk.b], [p_.b], scale=0.125)
                    if len(pend) >= SKEW:
                        pend.pop(0)()
                    def pv(p_=p_, kc=kc, hk=hk, bsel=bsel):
                        for g in range(4):
                            k.mm(bsel, bsel[:, g * 65:(g + 1) * 65], p_[:, g * 128:(g + 1) * 128], vs[:, kc, hk, :],
                                 [p_.b, vs.b], start=(kc == 0 and g == 0), stop=(kc == ti), skip=True)
                    pend.append(pv)
                    yield
                while pend:
                    pend.pop(0)()
                cs = casb[hk]; ws = wsb[hk]
                k.cp(Lg[:, :, 0], cs[:, 64:260:65], [cs.b], [Lg.b], eng="dve")
                k.cp(Lg[:, :, 1], bsel[:, 64:260:65], [bsel.b], [Lg.b], eng="dve")
                k.cp(Lg[:, :, 2], ws[:, 64:260:65], [ws.b], [Lg.b], eng="dve")
                k.ts("dve", coef[:, :, :], Lg[:, :, :], 1e-30, None, ALU.max, None, [Lg.b], [coef.b])
                P.op("dve", lambda: nc.vector.reciprocal(out=coef[:, :, :], in_=coef[:, :, :]), [coef.b], [coef.b])
                k.tt("dve", coef[:, :, :], coef[:, :, :], pb.sg[:, hk * 12:(hk + 1) * 12].rearrange("p (g b) -> p g b", b=3),
                     ALU.mult, [coef.b, pb.sg.b], [coef.b])
                for g in range(4):
                    col = 512 + (hk * 4 + g) * 64
                    k.ts("dve", tn[:, :], cs[:, g * 65:g * 65 + 64], coef[:, g, 0:1], None, ALU.mult, None, [cs.b, coef.b], [tn.b])
                    k.stt("dve", tn[:, :], ws[:, g * 65:g * 65 + 64], coef[:, g, 2:3], tn[:, :], ALU.mult, ALU.add,
                          [ws.b, coef.b, tn.b], [tn.b])
                    k.stt("dve", omix[:, col:col + 64], bsel[:, g * 65:g * 65 + 64], coef[:, g, 1:2], tn[:, :], ALU.mult, ALU.add,
                          [bsel.b, coef.b, tn.b], [omix.b])
                yield

        def prologue(sq_i, ti):
            pb = PB[ti % 2]
            x = xt[ti % 2]
            t0 = sq_i * T_ + ti * 128
            P.dma(x[:, :], D["x1"][t0:t0 + 128, :], reads=[D["x1_b"]], writes=[x.b])
            P.dma(pb.kf[:, :, :], D["KF"][ti], writes=[pb.kf.b])
            P.dma(cmbt[:, :, :], D["CMBT"][ti], writes=[cmbt.b])
            k0 = max(0, ti - 4)
            nk = ti + 1 - k0
            P.dma(pb.kwin[:, 0:nk * 128], D["kwT"][sq_i][:, k0 * 128:(ti + 1) * 128], reads=[D["state_b"]], writes=[pb.kwin.b])
            P.dma(pb.vwin[:, 0:nk, :, :], D["vw"][sq_i][:, k0:ti + 1, :, :], reads=[D["state_b"]], writes=[pb.vwin.b])
            k.cp(pb.cmbtb[:, :, :], cmbt[:, :, :], [cmbt.b], [pb.cmbtb.b], eng="pool")
            yield
            yield
            yield
            for q in range(2):
                bk = nb()
                for j in range(4):
                    kc = q * 4 + j
                    k.tr(bk, bk[:, j * 128:(j + 1) * 128], x[:, kc * 128:(kc + 1) * 128], ident[:, :], [x.b, ident.b])
                k.cp(xT[:, q * 4:(q + 1) * 4, :], bk[:, :].rearrange("p (j c) -> p j c", j=4), [bk.b], [xT.b])
                yield
            for q in range(4):
                bk = nb()
                for j in range(4):
                    c = q * 4 + j
                    for kc in range(8):
                        k.mm(bk, bk[:, j * 128:(j + 1) * 128], win[:, kc, c * 128:(c + 1) * 128], xT[:, kc, :],
                             [win.b, xT.b], start=(kc == 0), stop=(kc == 7))
                if q < 3:
                    k.cp(raw[:, q * 4:(q + 1) * 4, 3:131], bk[:, :].rearrange("p (j c) -> p j c", j=4), [bk.b], [raw.b])
                else:
                    k.cp(pb.nqT[:, :, :], bk[:, :].rearrange("p (j c) -> p j c", j=4), [bk.b], [pb.nqT.b])
                    k.cp(pb.QN[0][0:64, :], bk[0:64, :], [bk.b], [pb.QN[0].b])
                yield
            bk = nb()
            for g in range(4):
                for kc in range(8):
                    k.mm(bk, bk[0:64, g * 128:(g + 1) * 128], win[:, kc, 1536 + g * 128 + 64:1536 + (g + 1) * 128], xT[:, kc, :],
                         [win.b, xT.b], start=(kc == 0), stop=(kc == 7))
            k.cp(pb.QN[1][0:64, :], bk[0:64, :], [bk.b], [pb.QN[1].b])
            yield
            bz = nb()
            for kc in range(8):
                k.mm(bz, bz[:, :], xT[:, kc, :], win[:, kc, TM_Z:TM_Z + 512], [win.b, xT.b], start=(kc == 0), stop=(kc == 7))
            k.cp(pb.zt[:, :], bz[:, :], [bz.b], [pb.zt.b], eng="act")
            bs_ = nb()
            for kc in range(8):
                k.mm(bs_, bs_[:, 0:32], xT[:, kc, :], win[:, kc, TM_S:TM_S + 32], [win.b, xT.b], start=(kc == 0), stop=(kc == 7))
            k.cp(sm[:, :], bs_[:, 0:32], [bs_.b], [sm.b], eng="dve")
            yield
            def cdst(c):
                return (sil[:, c, :], silb[c]) if c < 8 else (pb.silv[:, c - 8, :], pb.silvb[c - 8])
            for kk in (3, 2, 1, 0):
                for half in range(2):
                    for c in range(half * 6, half * 6 + 6):
                        a_, ab = cdst(c)
                        if kk == 3:
                            k.ts("dve", a_, raw[:, c, 3:131], convw[:, c, 3:4], None, ALU.mult, None, [raw.b, convw.b],
                                 [ab] if c < 8 else [ab, pb.silv.b])
                        else:
                            k.stt("dve", a_, raw[:, c, kk:kk + 128], convw[:, c, kk:kk + 1], a_, ALU.mult, ALU.add,
                                  [raw.b, convw.b, ab], [ab])
                    yield
            k.cp(raw[:, :, 0:3], raw[:, :, 128:131], [raw.b], [raw.b], eng="pool")
            yield
            yield
            k.act(sil[:, :, :], sil[:, :, :], AF.Silu, silb, silb)
            k.act(pb.silv[:, :, :], pb.silv[:, :, :], AF.Silu, pb.silvb, pb.silvb + [pb.silv.b])
            k.act(pb.zt[:, :], pb.zt[:, :], AF.Silu, [pb.zt.b], [pb.zt.b])
            yield
            yield
            yield
            k.tt("pool", pb.zt[:, :], pb.zt[:, :], normw[:, :], ALU.mult, [pb.zt.b, normw.b], [pb.zt.b])
            k.tt("dve", sq[0][:, :], sil[:, 0, :], sil[:, 0, :], ALU.mult, [silb[0]], [sq[0].b])
            yield
            for c in range(8):
                h = c % 4
                s_ = sq[c % 2]; r_ = rn[c % 2]
                bk = nb()
                k.mm(bk, bk[:, 0:128], ones[:, :], s_[:, :], [ones.b, s_.b])
                k.act(r_[:, :], bk[:, 0:128], AF.Ln, [bk.b], [r_.b], bias=1e-6)
                k.act(r_[:, :], r_[:, :], AF.Exp, [r_.b], [r_.b], scale=-0.5, bias=(-0.5 * float(np.log(128.0)) if c < 4 else 0.0))
                if c + 1 < 8:
                    k.tt("dve", sq[(c + 1) % 2][:, :], sil[:, c + 1, :], sil[:, c + 1, :], ALU.mult, [silb[c + 1]], [sq[(c + 1) % 2].b])
                yield
                if c < 4:
                    k.tt("dve", pb.qnb[:, h, :], sil[:, c, :], r_[:, :], ALU.mult, [silb[c], r_.b], [pb.qnb.b])
                else:
                    k.tt("dve", pb.kn[:, h, :], sil[:, c, :], r_[:, :], ALU.mult, [silb[c], r_.b], [pb.kn.b])
                    k.cp(pb.knb[:, h, :], pb.kn[:, h, :], [pb.kn.b], [pb.knb.b], eng="pool")
            yield
            k.act(smt[:, 0:4], sm[:, 0:4], AF.Exp, [sm.b], [smt.b], scale=-1.0)
            k.act(smt[:, 8:32], sm[:, 8:32], AF.Exp, [sm.b], [smt.b], scale=-1.0)
            k.tt("dve", smt[:, 4:8], sm[:, 4:8], dtb[:, :], ALU.add, [sm.b, dtb.b], [smt.b])
            k.act(smt[:, 4:8], smt[:, 4:8], AF.Exp, [smt.b], [smt.b])
            k.ts("dve", smt[:, :], smt[:, :], 1.0, None, ALU.add, None, [smt.b], [smt.b])
            k.act(pb.g_[:, :], smt[:, 4:8], AF.Ln, [smt.b], [pb.g_.b])
            k.tt("dve", pb.g_[:, :], pb.g_[:, :], negA[:, :], ALU.mult, [pb.g_.b, negA.b], [pb.g_.b])
            P.op("dve", lambda: nc.vector.reciprocal(out=pb.beta[:, :], in_=smt[:, 0:4]), [smt.b], [pb.beta.b])
            k.ts("dve", pb.nbeta[:, :], pb.beta[:, :], -1.0, None, ALU.mult, None, [pb.beta.b], [pb.nbeta.b])
            P.op("dve", lambda: nc.vector.reciprocal(out=pb.sg[:, :], in_=smt[:, 8:32]), [smt.b], [pb.sg.b])
            yield
            bk = nb()
            k.mm(bk, bk[:, 0:4], U[:, :], pb.g_[:, :], [U.b, pb.g_.b])
            k.mm(bk, bk[:, 4:8], ones[:, :], pb.g_[:, :], [ones.b, pb.g_.b])
            k.cp(gcs[:, :], bk[:, 0:8], [bk.b], [gcs.b], eng="dve")
            k.cp(pb.gcp[:, 0:4], gcs[:, 0:4], [gcs.b], [pb.gcp.b], eng="dve")
            k.ts("dve", pb.gcp[:, 4:8], gcs[:, 0:4], -1.0, None, ALU.mult, None, [gcs.b], [pb.gcp.b])
            k.act(pb.egc[:, :], gcs[:, 0:4], AF.Exp, [gcs.b], [pb.egc.b])
            k.act(pb.egl[:, :], gcs[:, 4:8], AF.Exp, [gcs.b], [pb.egl.b])
            k.tt("dve", pb.eglm[:, :], gcs[:, 4:8], gcs[:, 0:4], ALU.subtract, [gcs.b], [pb.eglm.b])
            k.act(pb.eglm[:, :], pb.eglm[:, :], AF.Exp, [pb.eglm.b], [pb.eglm.b])
            for h in range(4):
                k.ts("dve", pb.Gm[h][:, :], ones[:, :], pb.g_[:, h:h + 1], None, ALU.mult, None, [ones.b, pb.g_.b], [pb.Gm[h].b])
            yield

        PRO_W = int(_os.environ.get("PRO_W", "1"))

        def run_all(gens, weights=None):
            gens = list(gens)
            weights = dict(weights or {})
            while gens:
                for s_ in list(gens):
                    for _ in range(weights.get(id(s_), 1)):
                        try:
                            next(s_)
                        except StopIteration:
                            gens.remove(s_)
                            break

        for sq_i in range(nseq):
            sb = D["state_b"]
            P.dma(ksE[0][0:64, :], D["ksT"][sq_i][0:64, :], reads=[sb], writes=[ksE[0].b])
            P.dma(ksE[1][0:64, :], D["ks1T"][sq_i], reads=[sb], writes=[ksE[1].b])
            P.dma(vs[:, :, :, :], D["vs"][sq_i], reads=[sb], writes=[vs.b])
            P.dma(kcmp[:, :], D["kcmp"][sq_i], reads=[sb], writes=[kcmp.b])
            P.dma(vcmp[:, :, :, :], D["vcmp"][sq_i], reads=[sb], writes=[vcmp.b])
            k.memset("pool", raw[:, :, 0:3], 0.0, [raw.b])
            for h in range(4):
                k.memset("pool", St[h][:, :], 0.0, [St[h].b])
                k.memset("pool", Sb[h][:, :], 0.0, [Sb[h].b])
            run_all([prologue(sq_i, 0)])

            def epilogue(ti):
                par = ti % 2
                pb = PB[par]; x = xt[par]; omix = omixs[par]; og = ogs[par]; ogb = ogbs[par]; ms = mss[par]; msb = msbs[par]
                t0 = sq_i * T_ + ti * 128
                k.ts("dve", y[:, :], x[:, :], DN_ALPHA, None, ALU.mult, None, [x.b], [y.b])
                k.act(rstd[:, :], ms[:, :], AF.Ln, msb, [rstd.b], scale=1.0 / 128.0, bias=1e-6)
                k.act(rstd[:, :], rstd[:, :], AF.Exp, [rstd.b], [rstd.b], scale=-0.5)
                yield
                for h in range(4):
                    k.stt("dve", omix[:, h * 128:(h + 1) * 128], og[:, h, :], rstd[:, h:h + 1], pb.zt[:, h * 128:(h + 1) * 128],
                          ALU.mult, ALU.mult, [ogb[h], rstd.b, pb.zt.b], [omix.b])
                if dbg:
                    P.dma(D["dbg"][t0:t0 + 128, :], omix[:, :], reads=[omix.b], writes=[D["dbg_b"]])
                yield
                for q in range(2):
                    bk = nb()
                    for j in range(4):
                        cc = q * 4 + j
                        k.tr(bk, bk[:, j * 128:(j + 1) * 128], omix[:, cc * 128:(cc + 1) * 128], ident[:, :], [omix.b, ident.b])
                    k.cp(omixT[:, q * 4:(q + 1) * 4, :], bk[:, :].rearrange("p (j c) -> p j c", j=4), [bk.b], [omixT.b])
                    yield
                for dh in range(2):
                    bk = nb()
                    for cc in range(8):
                        k.mm(bk, bk[:, :], omixT[:, cc, :], wout[:, cc, dh * 512:(dh + 1) * 512], [omixT.b, wout.b],
                             start=(cc == 0), stop=(cc == 7))
                    k.tt("dve", y[:, dh * 512:(dh + 1) * 512], bk[:, :], y[:, dh * 512:(dh + 1) * 512], ALU.add, [y.b, bk.b], [y.b])
                    yield
                layer_norm(P, [y], st, mv, rs, None, None, lnexp=True, affine=False)
                P.dma(D["x2"][t0:t0 + 128, :], y[:, :], reads=[y.b], writes=[D["x2_b"]])
                yield

            for ti in range(NT):
                pb = PB[ti % 2]
                gens = []
                if ti > 0:
                    gens.append(epilogue(ti - 1))
                gens += [gdn_stream(h, pb, ti % 2) for h in range(4)] + [nsa_stream(ti, pb, ti % 2)]
                wts = {}
                if ti + 1 < NT:
                    pg = prologue(sq_i, ti + 1)
                    gens.append(pg)
                    wts[id(pg)] = PRO_W
                run_all(gens, wts)
            run_all([epilogue(NT - 1)])
    P.barrier()


def _w_in_perm():
    o = {}
    off = 0
    for nm, sz in (("gq", 512), ("gk", 512), ("gv", 512), ("gz", 512), ("gb", 4), ("ga", 4), ("nq", 512), ("kc", 128),
                   ("vc", 128), ("ks", 128), ("vs", 128), ("kw", 128), ("vw", 128), ("gates", 24)):
        o[nm] = np.arange(off, off + sz)
        off += sz
    nq = o["nq"].reshape(2, 4, 64).transpose(1, 0, 2).reshape(-1)
    return np.concatenate([o["gq"], o["gk"], o["gv"], nq, o["gz"], o["gb"], o["ga"], o["gates"],
                           o["kc"], o["vc"], o["ks"], o["kw"], o["vs"], o["vw"]])


def host_consts(T_):
    NT = T_ // 128
    NC = (T_ - 32) // 16 + 1
    f = np.float32
    p = np.arange(128)[:, None]
    q = np.arange(128)[None, :]
    c = {}
    c["ident"] = np.eye(128, dtype=f)
    c["U"] = (p <= q).astype(f)
    c["NegU"] = -c["U"]
    c["ones"] = np.ones((128, 128), f)
    c["M1"] = np.where(q < p, 0.0, -1e30).astype(f)
    c["M2"] = np.where(p <= q, 0.0, -1e30).astype(f)
    c["ident4"] = np.tile(np.eye(128, dtype=f), (1, 4))
    c["CBT"] = np.where(q <= p, 0.0, -30000.0).astype(f)
    c["WBT"] = np.where(q > p, 0.0, -30000.0).astype(f)
    cs = np.arange(256) * 16
    ss = np.arange(64) * 64
    ov = np.clip(np.minimum(cs[:, None] + 32, ss[None, :] + 64) - np.maximum(cs[:, None], ss[None, :]), 0, None) / 32.0
    ov[NC:] = 0.0
    c["c2s"] = np.ascontiguousarray(ov.reshape(2, 128, 64).transpose(1, 0, 2)).astype(f)
    KF = np.zeros((NT, 128, 2, 64), f)
    CM = np.zeros((NT, 128, 2, 128), f)
    j = np.arange(64)[None, :]
    for ti in range(NT):
        t = ti * 128 + np.arange(128)[:, None]
        cur = t // 64
        visible = j <= cur
        forced = (j == 0) | (j == cur) | (j == cur - 1)
        KF[ti, :, 0, :] = (visible & ~forced)
        KF[ti, :, 1, :] = np.where(visible, np.where(forced, 1e4, 0.0), -1.0)
        n = np.arange(256)[None, :]
        valid = (16 * n + 31) <= t
        CM[ti] = np.where(valid, 0.0, -30000.0).reshape(128, 2, 128)
    c["KF"] = KF
    c["CMBT"] = CM
    c["Econst"] = (np.arange(T_)[None, :] // 64 == np.arange(64)[:, None]).astype(f)
    return c


def host_prep(inp, T_):
    f = np.float32
    d = dict(host_consts(T_))
    bc = lambda v, n=128: np.ascontiguousarray(np.broadcast_to(np.asarray(v, f).reshape(1, -1), (n, np.asarray(v).size)))
    d["w_in"] = np.ascontiguousarray(inp["w_in"][0][:, _w_in_perm()])
    d["w_out"] = np.ascontiguousarray(inp["w_out"][0])
    d["convw"] = np.ascontiguousarray(inp["gdn_conv_w"][0].reshape(4, 12, 128).transpose(2, 1, 0))
    d["alog"] = bc(inp["gdn_a_log"][0])
    d["dtb"] = bc(inp["gdn_dt_bias"][0])
    d["normw"] = bc(np.tile(inp["gdn_norm_w"][0], 4))
    for i in (1, 2, 3):
        d["ln%dg" % i] = bc(inp["ln%d_g" % i][0])
        d["ln%db" % i] = bc(inp["ln%d_b" % i][0])
    cw1 = []
    for nm in ("nsa_cmp_k_w1", "nsa_cmp_v_w1"):
        w = inp[nm][0].reshape(32, 64, 256).transpose(1, 0, 2)
        cw1.append(np.concatenate([w, w], 0))
    d["cw1"] = np.ascontiguousarray(np.stack(cw1))
    w2k = inp["nsa_cmp_k_w2"][0].reshape(2, 128, 64).transpose(1, 0, 2)
    d["cw2k"] = np.ascontiguousarray(np.concatenate([w2k, w2k], -1))
    d["cw2v"] = np.ascontiguousarray(inp["nsa_cmp_v_w2"][0].reshape(2, 128, 64).transpose(1, 0, 2))
    d["posT"] = np.ascontiguousarray(np.stack([np.concatenate([inp[nm][0].T] * 2, 0)
                                               for nm in ("nsa_cmp_pos_k", "nsa_cmp_pos_v")]))
    for i in (1, 2):
        d["f%dwg" % i] = np.ascontiguousarray(inp["ffn%d_wg" % i][0])
        d["f%dwu" % i] = np.ascontiguousarray(inp["ffn%d_wu" % i][0])
        d["f%dwd" % i] = np.ascontiguousarray(inp["ffn%d_wd" % i][0])
    return d


CONST_SHAPES = None


def build_program(nseq, T_, phases=("ffn1", "b1", "b2", "ffn2"), dbg=False):
    nc = bass.Bass("TRN2", target_bir_lowering=False)
    ntok = nseq * T_
    NT = T_ // 128
    shp = {k_: v.shape for k_, v in host_consts(T_).items()}
    shp.update({"w_in": (D_MODEL, N_IN), "w_out": (D_MODEL, D_MODEL), "convw": (128, 12, 4), "alog": (128, 4), "dtb": (128, 4),
                "normw": (128, 512), "cw1": (2, 128, 32, 256), "cw2k": (128, 2, 128), "cw2v": (128, 2, 64), "posT": (2, 128, 32)})
    for i in (1, 2, 3):
        shp["ln%dg" % i] = (128, D_MODEL)
        shp["ln%db" % i] = (128, D_MODEL)
    for i in (1, 2):
        shp["f%dwg" % i] = (D_MODEL, D_FF)
        shp["f%dwu" % i] = (D_MODEL, D_FF)
        shp["f%dwd" % i] = (D_FF, D_MODEL)
    D = {}
    for nm, s in shp.items():
        D[nm] = nc.dram_tensor(nm, list(s), F32, kind="ExternalInput").ap()

    def act_tensor(nm, first_in, last_out):
        kind = "ExternalInput" if first_in else ("ExternalOutput" if last_out else "Internal")
        D[nm] = nc.dram_tensor(nm, [ntok, D_MODEL], F32, kind=kind).ap()
        D[nm + "_b"] = Buf(nm)
    act_tensor("x", "ffn1" in phases, False)
    act_tensor("x1", "ffn1" not in phases, phases[-1] == "ffn1")
    act_tensor("x2", False, phases[-1] == "b2")
    act_tensor("out", False, phases[-1] == "ffn2")
    for nm, s in (("ksT", [nseq, 128, T_]), ("ks1T", [nseq, 64, T_]), ("kwT", [nseq, 128, T_]), ("vs", [nseq, 128, NT, 2, 65]),
                  ("vw", [nseq, 128, NT, 2, 65]), ("kcmp", [nseq, 128, 256]), ("vcmp", [nseq, 128, 2, 2, 129])):
        D[nm] = nc.dram_tensor(nm, s, BF16, kind="Internal").ap()
    D["state_b"] = Buf("state")
    if dbg:
        D["dbg"] = nc.dram_tensor("dbg", [ntok, D_MODEL], F32, kind="ExternalOutput").ap()
        D["dbg_b"] = Buf("dbg")
    if phases[-1] == "b1":
        D["dummy"] = nc.dram_tensor("dummyo", [128, 128], F32, kind="ExternalOutput").ap()
    with ExitStack() as es:
        P = Prog(nc)
        banks = [T(es, nc, "bank%d" % i, [128, 512], F32, psum=True) for i in range(8)]
        if "ffn1" in phases:
            ffn_phase(P, D["x"], D["x_b"], D["x1"], D["x1_b"], D["f1wg"], D["f1wu"], D["f1wd"], D["ln1g"], D["ln1b"],
                      D["ident"], ntok, banks, pfx="f1_")
        if "b1" in phases:
            phase_b1(P, D, banks, nseq, T_)
        if phases[-1] == "b1":
            P.dma(D["dummy"], D["ident"])
        if "b2" in phases:
            phase_b2(P, D, banks, nseq, T_, dbg=dbg)
        if "ffn2" in phases:
            ffn_phase(P, D["x2"], D["x2_b"], D["out"], D["out_b"], D["f2wg"], D["f2wu"], D["f2wd"], D["ln3g"], D["ln3b"],
                      D["ident"], ntok, banks, pfx="f2_", pre_g=D["ln2g"], pre_b=D["ln2b"])
        P.emit(es)
    return nc


_NC_CACHE = {}


def kernel(**inputs):
    inputs = {k_: np.asarray(v) for k_, v in inputs.items()}
    T_ = SEQ
    nseq = SEQ_PER_CORE
    if "nc" not in _NC_CACHE:
        _NC_CACHE["nc"] = build_program(nseq, T_)
    nc = _NC_CACHE["nc"]
    d = host_prep(inputs, T_)
    x = inputs["x"].astype(np.float32, copy=False)
    in_maps = []
    for c in range(N_CORES):
        m = dict(d)
        m["x"] = np.ascontiguousarray(x[c * nseq:(c + 1) * nseq].reshape(nseq * T_, D_MODEL))
        in_maps.append(m)
    res = run_bass_kernel_spmd(nc, in_maps, core_ids=list(range(N_CORES)))
    out = np.concatenate([np.asarray(r["out"]).reshape(nseq, T_, D_MODEL) for r in res.results], axis=0)
    return out.astype(np.float32, copy=False)
```

```python
import numpy as np
from contextlib import ExitStack
import concourse.bass as bass
import concourse.mybir as mybir
from concourse.bass_utils import run_bass_kernel_spmd

F32 = mybir.dt.float32
BF16 = mybir.dt.bfloat16
AF = mybir.ActivationFunctionType
ALU = mybir.AluOpType
AX = mybir.AxisListType

D_MODEL = 1024
D_FF = 2816
SEQ = 4096
N_CORES = 8
SEQ_PER_CORE = 2
LN_EPS = 1e-5
DN_ALPHA = 2.0 ** 0.25


class Buf:
    __slots__ = ("name", "last_w", "readers", "psum")

    def __init__(self, name, psum=False):
        self.name = name
        self.last_w = None
        self.readers = {}
        self.psum = psum


class Op:
    __slots__ = ("eng", "fn", "is_dma", "waits", "inc", "count", "sem", "val", "idx", "presem")

    def __init__(self, eng, fn, is_dma):
        self.eng = eng
        self.fn = fn
        self.is_dma = is_dma
        self.waits = []
        self.inc = False
        self.count = 0
        self.sem = None
        self.val = 0
        self.presem = None


class Prog:
    ENGS = ("pe", "dve", "act", "pool", "sp")

    def __init__(self, nc, n_dma_sems=40):
        self.nc = nc
        self.ops = []
        self.n_dma_sems = n_dma_sems
        self.dma_slot = 0
        self.dma_slot_last = [None] * n_dma_sems
        self.last_op = {e: None for e in self.ENGS}
        self.pending_dmas = []
        self.nbar = 0
        self.same_engine_raw = True

    def eng_obj(self, e):
        nc = self.nc
        return {"pe": nc.tensor, "dve": nc.vector, "act": nc.scalar, "pool": nc.gpsimd, "sp": nc.sync}[e]

    def _add_dep(self, op, dep, kind):
        if dep is None or dep is op:
            return
        if not dep.is_dma and not op.is_dma and dep.eng == op.eng:
            if op.eng == "pe" or not self.same_engine_raw:
                return
        op.waits.append(dep)

    def op(self, eng, fn, reads=(), writes=(), dma=False):
        o = Op(eng, fn, dma)
        for r in reads:
            self._add_dep(o, r.last_w, "raw")
            if r.psum:
                for e_, rd in r.readers.items():
                    if e_ != eng:
                        self._add_dep(o, rd, "rar")
        for w in writes:
            self._add_dep(o, w.last_w, "waw")
            for rd in w.readers.values():
                self._add_dep(o, rd, "war")
        for r in reads:
            key = ("dma", len(self.ops)) if dma else eng
            r.readers[key] = o
        for w in writes:
            w.last_w = o
            w.readers = {}
        if dma:
            slot = self.dma_slot
            self.dma_slot = (slot + 1) % self.n_dma_sems
            prev = self.dma_slot_last[slot]
            o.presem = prev
            o.sem = slot
            o.val = (prev.val if prev is not None else 0) + 16
            self.dma_slot_last[slot] = o
            self.pending_dmas.append(o)
        else:
            self.last_op[eng] = o
        self.ops.append(o)
        return o

    def dma(self, out, in_, reads=(), writes=(), eng="sp"):
        e = self.eng_obj(eng)
        return self.op(eng, lambda: e.dma_start(out=out, in_=in_), reads, writes, dma=True)

    def barrier(self):
        self.nbar += 1
        o = Op("all", None, False)
        o.waits = [x for x in self.last_op.values() if x is not None and x.eng != "sp"]
        o.val = list(self.pending_dmas)
        o.count = self.nbar
        self.pending_dmas = []
        self.ops.append(o)

    def emit(self, es):
        nc = self.nc
        csem = {e: es.enter_context(nc.semaphore("c_" + e)) for e in ("pe", "dve", "act", "pool")}
        dsem = [es.enter_context(nc.semaphore("d%d" % i)) for i in range(self.n_dma_sems)]
        bsem = es.enter_context(nc.semaphore("bar"))
        for o in self.ops:
            for d in o.waits:
                if not d.is_dma:
                    d.inc = True
        cnt = {e: 0 for e in csem}
        for o in self.ops:
            if o.eng != "all" and not o.is_dma and o.inc:
                cnt[o.eng] += 1
                o.count = cnt[o.eng]
        waited = {e: {} for e in self.ENGS}

        def do_wait(eng, key, sem, val):
            w = waited[eng]
            if w.get(key, 0) >= val:
                return
            w[key] = val
            self.eng_obj(eng).wait_ge(sem, val)

        for o in self.ops:
            if o.eng == "all":
                for d in o.waits:
                    for e in self.ENGS:
                        if e != d.eng:
                            do_wait(e, d.eng, csem[d.eng], d.count)
                for d in o.val:
                    do_wait("sp", ("d", d.sem), dsem[d.sem], d.val)
                nc.sync.sem_inc(bsem, 1)
                for e in ("pe", "dve", "act", "pool"):
                    do_wait(e, "bar", bsem, o.count)
                continue
            for d in o.waits:
                if d.is_dma:
                    do_wait(o.eng, ("d", d.sem), dsem[d.sem], d.val)
                else:
                    do_wait(o.eng, d.eng, csem[d.eng], d.count)
            if o.is_dma:
                if o.presem is not None:
                    do_wait(o.eng, ("d", o.sem), dsem[o.sem], o.presem.val)
                o.fn().then_inc(dsem[o.sem], 16)
            else:
                ins = o.fn()
                if o.inc:
                    ins.then_inc(csem[o.eng], 1)
        for d in self.pending_dmas:
            do_wait("sp", ("d", d.sem), dsem[d.sem], d.val)


class T:
    def __init__(self, es, nc, name, shape, dtype, psum=False):
        if psum:
            self.t = es.enter_context(nc.psum_tensor(name, shape, dtype))
        else:
            self.t = es.enter_context(nc.sbuf_tensor(name, shape, dtype))
        self.b = Buf(name, psum=psum)

    def __getitem__(self, k):
        return self.t[k]


def ffn_phase(P, x_d, x_b, out_d, out_b, wg_d, wu_d, wd_d, lng_d, lnb_d, ident_d, ntok, banks, TT=256, pfx="f1_", pre_g=None, pre_b=None):
    nc = P.nc
    KC = D_MODEL // 128
    FC = D_FF // 128
    NS = TT // 128
    ntiles = ntok // TT
    with ExitStack() as es:
        wg = T(es, nc, pfx + "wg_sb", [128, KC, D_FF], BF16)
        wu = T(es, nc, pfx + "wu_sb", [128, KC, D_FF], BF16)
        wd = T(es, nc, pfx + "wd_sb", [128, FC, D_MODEL], BF16)
        xt = [T(es, nc, pfx + "x_tok%d" % i, [128, NS * D_MODEL], F32) for i in range(2)]
        xT = T(es, nc, pfx + "xT", [128, KC, TT], BF16)
        hT = T(es, nc, pfx + "hT", [128, FC, TT], BF16)
        sl = [T(es, nc, pfx + "silu%d" % i, [128, TT], F32) for i in range(2)]
        yt = [T(es, nc, pfx + "y_tok%d" % i, [128, D_MODEL], F32) for i in range(NS)]
        lng = T(es, nc, pfx + "lng", [128, D_MODEL], F32)
        lnb = T(es, nc, pfx + "lnb", [128, D_MODEL], F32)
        ident = T(es, nc, pfx + "identf", [128, 128], F32)
        st = T(es, nc, pfx + "bnst", [128, NS, 2, 6], F32)
        mv = T(es, nc, pfx + "bnmv", [128, NS, 2], F32)
        rs = T(es, nc, pfx + "rstd", [128, NS], F32)

        P.dma(ident[:], ident_d, writes=[ident.b])
        P.dma(lng[:], lng_d, writes=[lng.b])
        P.dma(lnb[:], lnb_d, writes=[lnb.b])
        if pre_g is not None:
            pg = T(es, nc, pfx + "pre_g", [128, D_MODEL], F32)
            pbt = T(es, nc, pfx + "pre_b", [128, D_MODEL], F32)
            P.dma(pg[:], pre_g, writes=[pg.b])
            P.dma(pbt[:], pre_b, writes=[pbt.b])
        HALF = D_FF // 2
        cast_engs = ["act", "dve", "pool"]
        ci = 0

        def cast(dst, src, rb, wb):
            nonlocal ci
            e = cast_engs[ci % 3]
            ci += 1
            if e == "act":
                P.op("act", lambda: nc.scalar.copy(out=dst, in_=src), [rb], [wb])
            elif e == "dve":
                P.op("dve", lambda: nc.vector.tensor_copy(out=dst, in_=src), [rb], [wb])
            else:
                P.op("pool", lambda: nc.gpsimd.tensor_copy(out=dst, in_=src), [rb], [wb])

        si = 0
        QW = D_FF // 4

        def stage():
            nonlocal si
            t_ = xt[(si // 2) % 2]
            h_ = si % 2
            si += 1
            return t_, t_[:, h_ * D_MODEL:(h_ + 1) * D_MODEL], stg_b[(si - 1) % 4]
        stg_b = [Buf(pfx + "stg%d" % i) for i in range(4)]
        for w_d, w_sb in ((wg_d, wg), (wu_d, wu)):
            for kc in range(KC):
                for q4 in range(4):
                    t_, sv, sb_ = stage()
                    P.dma(sv[:, 0:QW], w_d[kc * 128:(kc + 1) * 128, q4 * QW:(q4 + 1) * QW], writes=[sb_])
                    cast(w_sb[:, kc, q4 * QW:(q4 + 1) * QW], sv[:, 0:QW], sb_, w_sb.b)
        for j in range(FC):
            t_, sv, sb_ = stage()
            P.dma(sv[:, 0:D_MODEL], wd_d[j * 128:(j + 1) * 128, :], writes=[sb_])
            cast(wd[:, j, :], sv[:, 0:D_MODEL], sb_, wd.b)
        bi = 0

        def nb():
            nonlocal bi
            b = banks[bi % len(banks)]
            bi += 1
            return b

        def load(ti):
            x = xt[ti % 2]
            P.dma(x[:, :].rearrange("p (s d) -> p s d", s=NS),
                  x_d[ti * TT:(ti + 1) * TT, :].rearrange("(s p) d -> p s d", p=128),
                  reads=[x_b], writes=[x.b] + (stg_b[2 * (ti % 2):2 * (ti % 2) + 2] if ti < 2 else []))
            if pre_g is not None:
                for s_i in range(NS):
                    xs = x[:, s_i * D_MODEL:(s_i + 1) * D_MODEL]
                    P.op("pool", lambda xs=xs: nc.gpsimd.tensor_tensor(out=xs, in0=xs, in1=pg[:, :], op=ALU.mult), [x.b, pg.b], [x.b])
                    P.op("pool", lambda xs=xs: nc.gpsimd.tensor_tensor(out=xs, in0=xs, in1=pbt[:, :], op=ALU.add), [x.b, pbt.b], [x.b])

        def transposes(ti):
            x = xt[ti % 2]
            for s in range(NS):
                for q in range(KC // 4):
                    bk = nb()
                    for j in range(4):
                        kc = q * 4 + j
                        P.op("pe", lambda bk=bk, j=j, s=s, kc=kc, x=x: nc.tensor.transpose(
                            out=bk[:, j * 128:(j + 1) * 128],
                            in_=x[:, s * D_MODEL + kc * 128: s * D_MODEL + (kc + 1) * 128],
                            identity=ident[:]), [x.b, ident.b], [bk.b])
                    P.op("act", lambda bk=bk, q=q, s=s: nc.scalar.copy(
                        out=xT[:, q * 4:(q + 1) * 4, s * 128:(s + 1) * 128],
                        in_=bk[:, :].rearrange("p (j c) -> p j c", j=4)), [bk.b], [xT.b])
            P.op("pool", lambda x=x: nc.gpsimd.tensor_scalar_mul(out=x[:, :], in0=x[:, :], scalar1=DN_ALPHA),
                 [x.b], [x.b])

        load(0)
        transposes(0)
        for ti in range(ntiles):
            x = xt[ti % 2]
            if ti + 1 < ntiles:
                load(ti + 1)
            for fc in range(FC):
                bk = nb()
                for wi, w_sb in enumerate((wg, wu)):
                    for kc in range(KC):
                        P.op("pe", lambda bk=bk, wi=wi, w_sb=w_sb, kc=kc, fc=fc: nc.tensor.matmul(
                            out=bk[:, wi * TT:(wi + 1) * TT], lhsT=w_sb[:, kc, fc * 128:(fc + 1) * 128],
                            rhs=xT[:, kc, :], start=(kc == 0), stop=(kc == KC - 1)),
                            [w_sb.b, xT.b], [bk.b])
                s_ = sl[fc % 2]
                P.op("act", lambda bk=bk, s_=s_: nc.scalar.activation(out=s_[:, :], in_=bk[:, 0:TT], func=AF.Silu),
                     [bk.b], [s_.b])
                P.op("dve", lambda bk=bk, s_=s_, fc=fc: nc.vector.tensor_tensor(
                    out=hT[:, fc, :], in0=bk[:, TT:2 * TT], in1=s_[:, :], op=ALU.mult),
                    [bk.b, s_.b], [hT.b])
            if ti + 1 < ntiles:
                transposes(ti + 1)
            for s in range(NS):
                y = yt[s]
                DW = 256
                for dh in range(D_MODEL // DW):
                    bk = nb()
                    for fc in range(FC):
                        P.op("pe", lambda bk=bk, fc=fc, s=s, dh=dh: nc.tensor.matmul(
                            out=bk[:, 0:DW], lhsT=hT[:, fc, s * 128:(s + 1) * 128],
                            rhs=wd[:, fc, dh * DW:(dh + 1) * DW], start=(fc == 0), stop=(fc == FC - 1)),
                            [hT.b, wd.b], [bk.b])
                    P.op("dve", lambda bk=bk, y=y, x=x, s=s, dh=dh: nc.vector.scalar_tensor_tensor(
                        out=y[:, dh * DW:(dh + 1) * DW], in0=bk[:, 0:DW], scalar=0.5,
                        in1=x[:, s * D_MODEL + dh * DW: s * D_MODEL + (dh + 1) * DW],
                        op0=ALU.mult, op1=ALU.add), [bk.b, x.b], [y.b])
            layer_norm(P, yt, st, mv, rs, lng, lnb)
            for s in range(NS):
                t0 = ti * TT + s * 128
                P.dma(out_d[t0:t0 + 128, :], yt[s][:, :], reads=[yt[s].b], writes=[out_b])
    P.barrier()


def layer_norm(P, ys, st, mv, rs, lng, lnb, lnexp=False, affine=True):
    nc = P.nc
    n = len(ys)
    for s, y in enumerate(ys):
        for h in range(2):
            P.op("dve", lambda h=h, s=s, y=y: nc.vector.bn_stats(out=st[:, s, h, :], in_=y[:, h * 512:(h + 1) * 512]),
                 [y.b], [st.b])
    for s in range(n):
        P.op("dve", lambda s=s: nc.vector.bn_aggr(out=mv[:, s, :], in_=st[:, s, :, :].rearrange("p a b -> p (a b)")),
             [st.b], [mv.b])
    P.op("dve", lambda: nc.vector.tensor_scalar(out=rs[:, 0:n], in0=mv[:, 0:n, 1], scalar1=LN_EPS, scalar2=None,
                                                op0=ALU.add), [mv.b], [rs.b])
    if lnexp:
        P.op("act", lambda: nc.scalar.activation(out=rs[:, 0:n], in_=rs[:, 0:n], func=AF.Ln), [rs.b], [rs.b])
        P.op("act", lambda: nc.scalar.activation(out=rs[:, 0:n], in_=rs[:, 0:n], func=AF.Exp, scale=-0.5), [rs.b], [rs.b])
    else:
        P.op("act", lambda: nc.scalar.activation(out=rs[:, 0:n], in_=rs[:, 0:n], func=AF.Sqrt), [rs.b], [rs.b])
        P.op("dve", lambda: nc.vector.reciprocal(out=rs[:, 0:n], in_=rs[:, 0:n]), [rs.b], [rs.b])
    for s, y in enumerate(ys):
        P.op("dve", lambda s=s, y=y: nc.vector.tensor_scalar(out=y[:, :], in0=y[:, :], scalar1=mv[:, s, 0:1],
                                                         scalar2=rs[:, s:s + 1], op0=ALU.subtract, op1=ALU.mult),
             [y.b, mv.b, rs.b], [y.b])
        if not affine:
            continue
        P.op("pool", lambda y=y: nc.gpsimd.tensor_tensor(out=y[:, :], in0=y[:, :], in1=lng[:, :], op=ALU.mult),
             [y.b, lng.b], [y.b])
        P.op("pool", lambda y=y: nc.gpsimd.tensor_tensor(out=y[:, :], in0=y[:, :], in1=lnb[:, :], op=ALU.add),
             [y.b, lnb.b], [y.b])


class K:
    def __init__(self, P):
        self.P = P
        self.nc = P.nc
        self.ei = 0

    def mm(self, bk, out, lhsT, rhs, rb, start=True, stop=True, skip=False):
        nc = self.nc
        self.P.op("pe", lambda: nc.tensor.matmul(out=out, lhsT=lhsT, rhs=rhs, start=start, stop=stop,
                                                 skip_group_check=skip), rb, [bk.b])

    def tr(self, bk, out, in_, ident, rb):
        nc = self.nc
        self.P.op("pe", lambda: nc.tensor.transpose(out=out, in_=in_, identity=ident), rb, [bk.b])

    def act(self, out, in_, func, rb, wb, scale=1.0, bias=0.0, accum=None):
        nc = self.nc
        self.P.op("act", lambda: nc.scalar.activation(out=out, in_=in_, func=func, bias=bias, scale=scale,
                                                      accum_out=accum), rb, wb)

    def ts(self, eng, out, in0, s1, s2, op0, op1, rb, wb):
        e = self.P.eng_obj(eng)
        if s2 is None:
            self.P.op(eng, lambda: e.tensor_scalar(out=out, in0=in0, scalar1=s1, scalar2=None, op0=op0), rb, wb)
        else:
            self.P.op(eng, lambda: e.tensor_scalar(out=out, in0=in0, scalar1=s1, scalar2=s2, op0=op0, op1=op1), rb, wb)

    def tt(self, eng, out, in0, in1, op, rb, wb):
        e = self.P.eng_obj(eng)
        self.P.op(eng, lambda: e.tensor_tensor(out=out, in0=in0, in1=in1, op=op), rb, wb)

    def stt(self, eng, out, in0, s, in1, op0, op1, rb, wb):
        e = self.P.eng_obj(eng)
        self.P.op(eng, lambda: e.scalar_tensor_tensor(out=out, in0=in0, scalar=s, in1=in1, op0=op0, op1=op1), rb, wb)

    def cp(self, out, in_, rb, wb, eng=None):
        nc = self.nc
        if eng is None:
            eng = ("act", "dve")[self.ei % 2]
            self.ei += 1
        if eng == "act":
            self.P.op("act", lambda: nc.scalar.copy(out=out, in_=in_), rb, wb)
        elif eng == "dve":
            self.P.op("dve", lambda: nc.vector.tensor_copy(out=out, in_=in_), rb, wb)
        else:
            self.P.op("pool", lambda: nc.gpsimd.tensor_copy(out=out, in_=in_), rb, wb)

    def memset(self, eng, ap, val, wb):
        e = self.P.eng_obj(eng)
        self.P.op(eng, lambda: e.memset(ap, val), [], wb)


N_IN = 3360
import os as _os
SKIP = set(_os.environ.get('KSKIP', '').split(','))
FM_COLS = 2048
TM_Z = 2048
TM_S = 2560
P1_COLS = 2592


def phase_b1(P, D, banks, nseq, T_):
    nc = P.nc
    k = K(P)
    NT = T_ // 128
    NC = (T_ - 32) // 16 + 1
    bi = 0

    def nb():
        nonlocal bi
        b = banks[bi % 6]
        bi += 1
        return b

    with ExitStack() as es:
        w1p = T(es, nc, "b1_w1p", [128, 8, 768], BF16)
        stg = [T(es, nc, "b1_b1stg%d" % i, [128, 1024], F32) for i in range(2)]
        cw1 = [T(es, nc, "b1_cw1_%d" % i, [128, 32, 256], BF16) for i in range(2)]
        cw2k = T(es, nc, "b1_cw2k", [128, 2, 128], BF16)
        cw2v = T(es, nc, "b1_cw2v", [128, 2, 64], BF16)
        posT = [T(es, nc, "b1_posT%d" % i, [128, 32], BF16) for i in range(2)]
        pb = T(es, nc, "b1_posb", [128, 4], F32)
        ident = T(es, nc, "b1_b1ident", [128, 128], F32)
        xt = [T(es, nc, "b1_b1x%d" % i, [128, D_MODEL], F32) for i in range(2)]
        xT = T(es, nc, "b1_b1xT", [128, 8, 128], BF16)
        kcT = T(es, nc, "b1_kcT", [128, T_], BF16)
        vcT = T(es, nc, "b1_vcT", [128, T_], BF16)
        ksT = T(es, nc, "b1_b1ksT", [128, T_], BF16)
        ks1T = T(es, nc, "b1_b1ks1T", [64, T_], BF16)
        kwT = T(es, nc, "b1_b1kwT", [128, T_], BF16)
        vst = T(es, nc, "b1_b1vs", [128, NT, 2, 65], BF16)
        vwt = T(es, nc, "b1_b1vw", [128, NT, 2, 65], BF16)
        kcmp = T(es, nc, "b1_b1kcmp", [128, 256], BF16)
        vcmp = T(es, nc, "b1_b1vcmp", [128, 2, 2, 129], BF16)
        c2s = T(es, nc, "b1_b1c2s", [128, 2, 64], F32)
        hid = [T(es, nc, "b1_hid%d" % i, [128, 2, 256], BF16) for i in range(2)]
        gx = [T(es, nc, "b1_gx%d" % i, [128, 256], F32) for i in range(2)]
        gu = [T(es, nc, "b1_gu%d" % i, [128, 256], F32) for i in range(2)]

        P.dma(ident[:], D["ident"], writes=[ident.b])
        P.dma(c2s[:], D["c2s"], writes=[c2s.b])
        si = 0
        for kc in range(8):
            s = stg[si % 2]; si += 1
            P.dma(s[:, 0:768], D["w_in"][kc * 128:(kc + 1) * 128, P1_COLS:N_IN], writes=[s.b])
            k.cp(w1p[:, kc, :], s[:, 0:768], [s.b], [w1p.b])
        for sq in range(nseq):
            k.memset("pool", vst[:, :, :, 64:65], 1.0, [vst.b])
            k.memset("pool", vwt[:, :, :, 64:65], 1.0, [vwt.b])
            k.memset("pool", vcmp[:, :, :, 0:65], 0.0, [vcmp.b])
            k.memset("pool", kcmp[:, :], 0.0, [kcmp.b])
            k.memset("pool", vcmp[:, :, :, 64:65], 1.0, [vcmp.b])
            for nch in range(2):
                for hk in range(2):
                    k.cp(vcmp[:, nch, hk, 65:129], c2s[:, nch, :], [c2s.b], [vcmp.b], eng="pool")

            def load(ti):
                x = xt[ti % 2]
                t0 = sq * T_ + ti * 128
                P.dma(x[:, :], D["x1"][t0:t0 + 128, :], reads=[D["x1_b"]], writes=[x.b])
            load(0)
            for ti in range(NT if 'proj' not in SKIP else 0):
                x = xt[ti % 2]
                if ti + 1 < NT:
                    load(ti + 1)
                for q in range(2):
                    bk = nb()
                    for j in range(4):
                        kc = q * 4 + j
                        k.tr(bk, bk[:, j * 128:(j + 1) * 128], x[:, kc * 128:(kc + 1) * 128], ident[:], [x.b, ident.b])
                    k.cp(xT[:, q * 4:(q + 1) * 4, :], bk[:, :].rearrange("p (j c) -> p j c", j=4), [bk.b], [xT.b])
                bk = nb()
                for c in range(4):
                    for kc in range(8):
                        k.mm(bk, bk[:, c * 128:(c + 1) * 128], w1p[:, kc, c * 128:(c + 1) * 128], xT[:, kc, :],
                             [w1p.b, xT.b], start=(kc == 0), stop=(kc == 7))
                for c, dst in enumerate((kcT, vcT, ksT, kwT)):
                    k.cp(dst[:, ti * 128:(ti + 1) * 128], bk[:, c * 128:(c + 1) * 128], [bk.b], [dst.b])
                bk = nb()
                for kc in range(8):
                    k.mm(bk, bk[0:64, 0:128], w1p[:, kc, 320:384], xT[:, kc, :], [w1p.b, xT.b], start=(kc == 0), stop=(kc == 7))
                k.cp(ks1T[0:64, ti * 128:(ti + 1) * 128], bk[0:64, 0:128], [bk.b], [ks1T.b])
                bk = nb()
                for kc in range(8):
                    k.mm(bk, bk[:, 0:256], xT[:, kc, :], w1p[:, kc, 512:768], [w1p.b, xT.b], start=(kc == 0), stop=(kc == 7))
                k.cp(vst[:, ti, :, 0:64], bk[:, 0:128].rearrange("p (h d) -> p h d", h=2), [bk.b], [vst.b])
                k.cp(vwt[:, ti, :, 0:64], bk[:, 128:256].rearrange("p (h d) -> p h d", h=2), [bk.b], [vwt.b])
            if sq == 0:
                dsi = [0]
                for kv in range(2):
                    for q in range(8):
                        s = stg[dsi[0] % 2]; dsi[0] += 1
                        P.dma(s[:, :].rearrange("p (l j) -> p l j", l=4), D["cw1"][kv, :, q * 4:(q + 1) * 4, :], writes=[s.b])
                        k.cp(cw1[kv][:, q * 4:(q + 1) * 4, :], s[:, :].rearrange("p (l j) -> p l j", l=4), [s.b], [cw1[kv].b])
                s = stg[dsi[0] % 2]; dsi[0] += 1
                P.dma(s[:, 0:256].rearrange("p (c j) -> p c j", c=2), D["cw2k"], writes=[s.b])
                k.cp(cw2k[:, :, :], s[:, 0:256].rearrange("p (c j) -> p c j", c=2), [s.b], [cw2k.b])
                s = stg[dsi[0] % 2]; dsi[0] += 1
                P.dma(s[:, 0:128].rearrange("p (c j) -> p c j", c=2), D["cw2v"], writes=[s.b])
                k.cp(cw2v[:, :, :], s[:, 0:128].rearrange("p (c j) -> p c j", c=2), [s.b], [cw2v.b])
                for kv in range(2):
                    s = stg[dsi[0] % 2]; dsi[0] += 1
                    P.dma(s[:, 0:32], D["posT"][kv], writes=[s.b])
                    k.cp(posT[kv][:, :], s[:, 0:32], [s.b], [posT[kv].b])
                bk = nb()
                for kv in range(2 if 'pb' not in SKIP else 0):
                    for jc in range(2):
                        col = kv * 2 + jc
                        for l in range(32):
                            k.mm(bk, bk[:, col:col + 1], cw1[kv][0:64, l, jc * 128:(jc + 1) * 128], posT[kv][0:64, l:l + 1],
                                 [cw1[kv].b, posT[kv].b], start=(l == 0), stop=(l == 31), skip=True)
                if 'pb' not in SKIP:
                    k.cp(pb[:, :], bk[:, 0:4], [bk.b], [pb.b], eng="dve")
                else:
                    k.memset('dve', pb[:, :], 0.0, [pb.b])

            for kv, src in enumerate((kcT, vcT) if 'cmp' not in SKIP else ()):
                for hk in range(2):
                    h_ = hid[hk]
                    for jc in range(2):
                        bk = nb()
                        for l in range(32):
                            k.mm(bk, bk[:, 0:NC], cw1[kv][hk * 64:(hk + 1) * 64, l, jc * 128:(jc + 1) * 128],
                                 src[hk * 64:(hk + 1) * 64, l:l + 16 * (NC - 1) + 1:16], [cw1[kv].b, src.b],
                                 start=(l == 0), stop=(l == 31))
                        x_ = gx[jc]; u_ = gu[jc]
                        col = kv * 2 + jc
                        k.ts("dve", x_[:, 0:NC], bk[:, 0:NC], pb[:, col:col + 1], None, ALU.add, None, [bk.b, pb.b], [x_.b])
                        k.tt("dve", u_[:, 0:NC], x_[:, 0:NC], x_[:, 0:NC], ALU.mult, [x_.b], [u_.b])
                        k.ts("dve", u_[:, 0:NC], u_[:, 0:NC], 0.044715, 1.0, ALU.mult, ALU.add, [u_.b], [u_.b])
                        k.tt("dve", u_[:, 0:NC], u_[:, 0:NC], x_[:, 0:NC], ALU.mult, [u_.b, x_.b], [u_.b])
                        k.act(u_[:, 0:NC], u_[:, 0:NC], AF.Exp, [u_.b], [u_.b], scale=-1.5957691216)
                        k.ts("dve", u_[:, 0:NC], u_[:, 0:NC], 1.0, None, ALU.add, None, [u_.b], [u_.b])
                        P.op("dve", lambda u_=u_: nc.vector.reciprocal(out=u_[:, 0:NC], in_=u_[:, 0:NC]), [u_.b], [u_.b])
                        k.tt("dve", h_[:, jc, 0:NC], u_[:, 0:NC], x_[:, 0:NC], ALU.mult, [u_.b, x_.b], [h_.b])
                    if kv == 0:
                        bk = nb()
                        for jc in range(2):
                            k.mm(bk, bk[:, 0:NC], cw2k[:, jc, :], h_[:, jc, 0:NC], [cw2k.b, h_.b], start=(jc == 0), stop=(jc == 1))
                        k.cp(kcmp[hk * 64:(hk + 1) * 64, 0:NC], bk[hk * 64:(hk + 1) * 64, 0:NC], [bk.b], [kcmp.b])
                    else:
                        for nch in range(2):
                            rows = min(NC - nch * 128, 128)
                            if rows <= 0:
                                continue
                            bk = nb()
                            for jc in range(2):
                                k.mm(bk, bk[0:rows, 0:64], h_[:, jc, nch * 128:nch * 128 + rows], cw2v[:, jc, :],
                                     [cw2v.b, h_.b], start=(jc == 0), stop=(jc == 1))
                            k.cp(vcmp[0:rows, nch, hk, 0:64], bk[0:rows, 0:64], [bk.b], [vcmp.b])
            if 'state' in SKIP:
                continue
            sb = D["state_b"]
            P.dma(D["ksT"][sq], ksT[:, :], reads=[ksT.b], writes=[sb])
            P.dma(D["kwT"][sq], kwT[:, :], reads=[kwT.b], writes=[sb])
            P.dma(D["ks1T"][sq], ks1T[0:64, :], reads=[ks1T.b], writes=[sb])
            P.dma(D["vs"][sq], vst[:, :, :, :], reads=[vst.b], writes=[sb])
            P.dma(D["vw"][sq], vwt[:, :, :, :], reads=[vwt.b], writes=[sb])
            P.dma(D["kcmp"][sq], kcmp[:, :], reads=[kcmp.b], writes=[sb])
            P.dma(D["vcmp"][sq], vcmp[:, :, :, :], reads=[vcmp.b], writes=[sb])
    P.barrier()


def phase_b2(P, D, banks, nseq, T_, dbg=False):
    nc = P.nc
    k = K(P)
    NT = T_ // 128
    bi = 0

    def nb():
        nonlocal bi
        b = banks[bi % 5]
        bi += 1
        return b
    bselAB, bwin = (banks[5], banks[6]), banks[7]

    with ExitStack() as es:
        def S(name, shape, dt=F32):
            return T(es, nc, "b2" + name, shape, dt)
        win = S("win", [128, 8, P1_COLS], BF16)
        wout = S("wout", [128, 8, D_MODEL], BF16)
        ident, U, NegU, ones, M1, M2 = [S(n, [128, 128]) for n in ("ident", "U", "NegU", "ones", "M1", "M2")]
        ident4 = S("ident4", [128, 512], BF16)
        CBT = S("CBT", [128, 128], BF16)
        WBT = S("WBT", [128, 128], BF16)
        convw = S("convw", [128, 12, 4])
        alog = S("alog", [128, 4]); dtb = S("dtb", [128, 4]); negA = S("negA", [128, 4])
        normw = S("normw", [128, 512])
        ksE = [S("ksE%d" % i, [128, T_], BF16) for i in range(2)]
        NT2 = S("NT2", [128, 128])
        vs = S("vs", [128, NT, 2, 65], BF16)
        kcmp = S("kcmp", [128, 256], BF16); vcmp = S("vcmp", [128, 2, 2, 129], BF16)
        xt = [S("x%d" % i, [128, D_MODEL]) for i in range(2)]
        xT = S("xT", [128, 8, 128], BF16)
        raw = S("raw", [128, 12, 131])
        cacc = [S("cacc%d" % i, [128, 128]) for i in range(4)]
        sil = S("sil", [128, 8, 128])
        silb = [Buf("silb%d" % i) for i in range(8)]
        sq = [S("sq%d" % i, [128, 128]) for i in range(2)]
        rn = [S("rn%d" % i, [128, 128]) for i in range(2)]
        sm = S("sm", [128, 32]); smt = S("smt", [128, 32])
        gcs = S("gcs", [128, 8])
        class NS:
            pass
        PB = []
        for par in range(2):
            pb = NS()
            sfx = "_p%d" % par
            pb.qnb = S("qnb" + sfx, [128, 4, 128], BF16); pb.knb = S("knb" + sfx, [128, 4, 128], BF16)
            pb.kn = S("kn" + sfx, [128, 4, 128]); pb.silv = S("silv" + sfx, [128, 4, 128])
            pb.silvb = [Buf("silvb%d" % i + sfx) for i in range(4)]
            pb.nqT = S("nqT" + sfx, [128, 4, 128], BF16); pb.zt = S("zt" + sfx, [128, 512])
            pb.QN = [S("QN%d" % i + sfx, [128, 512], BF16) for i in range(2)]
            pb.beta = S("beta" + sfx, [128, 4]); pb.nbeta = S("nbeta" + sfx, [128, 4]); pb.g_ = S("g" + sfx, [128, 4])
            pb.egc = S("egc" + sfx, [128, 4]); pb.egl = S("egl" + sfx, [128, 4]); pb.eglm = S("eglm" + sfx, [128, 4])
            pb.gcp = S("gcp" + sfx, [128, 8]); pb.sg = S("sg" + sfx, [128, 24]); pb.cmbtb = S("cmbtb" + sfx, [128, 2, 128], BF16); pb.kf = S("kf" + sfx, [128, 2, 64])
            pb.Gm = [S("Gm%d" % h + sfx, [128, 128]) for h in range(4)]
            pb.kwin = S("kwin" + sfx, [128, 640], BF16); pb.vwin = S("vwin" + sfx, [128, 5, 2, 65], BF16)
            PB.append(pb)
        HB = []
        for h in range(4):
            HB.append((S("DecS%d" % h, [128, 128]), S("DecTi%d" % h, [128, 128]),
                       [S("Lb%d_%d" % (h, i), [128, 3, 128]) for i in range(2)], S("TT%d" % h, [128, 128]),
                       S("QKdT%d" % h, [128, 128], BF16), S("vtok%d" % h, [128, 128]), S("kd%d" % h, [128, 128], BF16),
                       S("t1_%d" % h, [128, 128]), S("t2_%d" % h, [128, 128]), S("vnb%d" % h, [128, 128], BF16)))
        junk = [S("junk%d" % h, [128, 128], BF16) for h in range(4)]
        ogs = [S("og%d" % i, [128, 4, 128]) for i in range(2)]
        mss = [S("ms%d" % i, [128, 4]) for i in range(2)]
        ogbs = [[Buf("ogb%d_%d" % (i, h)) for h in range(4)] for i in range(2)]
        msbs = [[Buf("msb%d_%d" % (i, h)) for h in range(4)] for i in range(2)]
        rstd = S("rstd", [128, 4])
        St = [S("S%d" % h, [128, 128]) for h in range(4)]
        Sb = [S("Sb%d" % h, [128, 128], BF16) for h in range(4)]
        casb = [S("casb%d" % i, [128, 260]) for i in range(2)]
        wsb = [S("wsb%d" % i, [128, 260]) for i in range(2)]
        omixs = [S("omix%d" % i, [128, D_MODEL]) for i in range(2)]; omixT = S("omixT", [128, 8, 128], BF16)
        y = S("y", [128, D_MODEL])
        st = S("bnst", [128, 1, 2, 6]); mv = S("bnmv", [128, 1, 2]); rs = S("rs", [128, 1])
        SKEW = int(_os.environ.get("SKEW", "2"))
        NPT = SKEW + 2
        pT = [S("pT%d" % i, [128, 512], BF16) for i in range(NPT)]
        cmbt = S("cmbt", [128, 2, 128])
        lc = S("lc", [128, 4]); rl = S("rl", [128, 4]); imp = S("imp", [128, 64]); impt = S("impt", [128, 64])
        m8a = S("m8a", [128, 8]); m8b = S("m8b", [128, 8])
        Lg = S("Lg", [128, 4, 3]); coef = S("coef", [128, 4, 3]); tn = S("tn", [128, 64])

        for t_, nm in ((ident, "ident"), (U, "U"), (NegU, "NegU"), (ones, "ones"), (M1, "M1"), (M2, "M2"),
                       (convw, "convw"), (alog, "alog"), (dtb, "dtb"), (normw, "normw")):
            P.dma(t_.t[tuple(slice(None) for _ in t_.t.shape)], D[nm], writes=[t_.b])
        si = 0
        for t_, nm, w_ in ((ident4, "ident4", 512), (CBT, "CBT", 128), (WBT, "WBT", 128)):
            s = xt[si % 2]; si += 1
            P.dma(s[:, 0:w_], D[nm], writes=[s.b])
            k.cp(t_[:, :], s[:, 0:w_], [s.b], [t_.b])
        for kc in range(8):
            for c3 in range(3):
                s = xt[si % 2]; si += 1
                P.dma(s[:, 0:864], D["w_in"][kc * 128:(kc + 1) * 128, c3 * 864:(c3 + 1) * 864], writes=[s.b])
                k.cp(win[:, kc, c3 * 864:(c3 + 1) * 864], s[:, 0:864], [s.b], [win.b])
        k.memset("pool", NT2[:, :], 0.0, [NT2.b])
        k.act(negA[:, :], alog[:, :], AF.Exp, [alog.b], [negA.b])
        k.ts("dve", negA[:, :], negA[:, :], -1.0, None, ALU.mult, None, [negA.b], [negA.b])


        def gdn_stream(h, pb, par):
            og = ogs[par]; ogb = ogbs[par]; ms = mss[par]; msb = msbs[par]
            DecS, DecTi, Lb, TT_, QKdT, vtok, kd, t1, t2, vnb = HB[h]
            Gm = pb.Gm[h]
            bd = nb()
            k.mm(bd, bd[:, 0:128], Gm[:, :], NegU[:, :], [NegU.b, Gm.b])
            k.mm(bd, bd[:, 128:256], pb.knb[:, h, :], pb.knb[:, h, :], [pb.knb.b])
            k.mm(bd, bd[:, 256:384], pb.knb[:, h, :], pb.qnb[:, h, :], [pb.knb.b, pb.qnb.b])
            k.tt("dve", DecS[:, :], bd[:, 0:128], M1[:, :], ALU.add, [bd.b, M1.b], [DecS.b])
            k.stt("dve", DecTi[:, :], bd[:, 0:128], -1.0, M2[:, :], ALU.mult, ALU.add, [bd.b, M2.b], [DecTi.b])
            k.act(DecS[:, :], DecS[:, :], AF.Exp, [DecS.b, pb.gcp.b], [DecS.b], bias=pb.gcp[:, h:h + 1])
            k.act(DecTi[:, :], DecTi[:, :], AF.Exp, [DecTi.b, pb.gcp.b], [DecTi.b], bias=pb.gcp[:, 4 + h:5 + h])
            k.stt("dve", Lb[0][:, 0, :], bd[:, 128:256], pb.beta[:, h:h + 1], DecS[:, :], ALU.mult, ALU.mult,
                  [bd.b, pb.beta.b, DecS.b], [Lb[0].b])
            k.tt("dve", QKdT[:, :], bd[:, 256:384], DecTi[:, :], ALU.mult, [bd.b, DecTi.b], [QKdT.b])
            yield
            bt = nb()
            k.tr(bt, bt[:, 0:128], Lb[0][:, 0, :], ident[:, :], [Lb[0].b, ident.b])
            k.cp(Lb[0][:, 1, :], bt[:, 0:128], [bt.b], [Lb[0].b], eng="act")
            k.stt("dve", Lb[1][:, 2, :], bt[:, 0:128], -1.0, ident[:, :], ALU.mult, ALU.add, [bt.b, ident.b], [Lb[1].b])
            yield
            for lvl in range(1, 8):
                cur = Lb[(lvl - 1) % 2]
                nxt = Lb[lvl % 2]
                bk = nb()
                if lvl <= 6:
                    k.mm(bk, bk[:, 0:128], cur[:, 1, :], cur[:, 0, :], [cur.b])
                    k.mm(bk, bk[:, 128:256], cur[:, 0, :], cur[:, 1, :], [cur.b])
                if lvl >= 2:
                    k.mm(bk, bk[:, 256:384], cur[:, 0, :], cur[:, 2, :], [cur.b])
                if lvl <= 6:
                    k.cp(nxt[:, 0:2, :], bk[:, 0:256].rearrange("p (a c) -> p a c", a=2), [bk.b], [nxt.b], eng="act")
                if lvl >= 2:
                    dst = nxt[:, 2, :] if lvl <= 6 else TT_[:, :]
                    dstb = nxt.b if lvl <= 6 else TT_.b
                    k.tt("dve", dst, bk[:, 256:384], cur[:, 2, :], ALU.add, [bk.b, cur.b], [dstb])
                yield
            bq = nb()
            k.mm(bq, bq[:, 0:128], pb.knb[:, h, :], Sb[h][:, :], [pb.knb.b, Sb[h].b])
            k.mm(bq, bq[:, 128:256], pb.qnb[:, h, :], Sb[h][:, :], [pb.qnb.b, Sb[h].b])
            k.tr(bq, bq[:, 256:384], pb.silv[:, h, :], ident[:, :], [pb.silv.b, ident.b])
            k.tr(bq, bq[:, 384:512], pb.kn[:, h, :], ident[:, :], [pb.kn.b, ident.b])
            k.cp(vtok[:, :], bq[:, 256:384], [bq.b], [vtok.b], eng="act")
            k.ts("dve", kd[:, :], bq[:, 384:512], pb.eglm[:, h:h + 1], None, ALU.mult, None, [bq.b, pb.eglm.b], [kd.b])
            k.stt("dve", t1[:, :], bq[:, 0:128], pb.egc[:, h:h + 1], vtok[:, :], ALU.mult, ALU.subtract,
                  [bq.b, pb.egc.b, vtok.b], [t1.b])
            k.ts("dve", t1[:, :], t1[:, :], pb.nbeta[:, h:h + 1], None, ALU.mult, None, [t1.b, pb.nbeta.b], [t1.b])
            k.ts("dve", t2[:, :], bq[:, 128:256], pb.egc[:, h:h + 1], None, ALU.mult, None, [bq.b, pb.egc.b], [t2.b])
            yield
            bv = nb()
            k.mm(bv, bv[:, 0:128], TT_[:, :], t1[:, :], [TT_.b, t1.b])
            k.cp(vnb[:, :], bv[:, 0:128], [bv.b], [vnb.b], eng="act")
            yield
            bw = nb()
            k.mm(bw, bw[:, 128:256], QKdT[:, :], vnb[:, :], [QKdT.b, vnb.b])
            k.mm(bw, bw[:, 256:384], kd[:, :], vnb[:, :], [kd.b, vnb.b])
            k.tt("dve", og[:, h, :], bw[:, 128:256], t2[:, :], ALU.add, [bw.b, t2.b], [ogb[h]])
            k.stt("dve", St[h][:, :], St[h][:, :], pb.egl[:, h:h + 1], bw[:, 256:384], ALU.mult, ALU.add,
                  [St[h].b, pb.egl.b, bw.b], [St[h].b])
            k.cp(Sb[h][:, :], St[h][:, :], [St[h].b], [Sb[h].b], eng="pool")
            k.act(junk[h][:, :], og[:, h, :], AF.Square, [ogb[h]], [junk[h].b, msb[h]], accum=ms[:, h:h + 1])
            yield

        def nsa_stream(ti, pb, par):
            omix = omixs[par]
            nchunks = 1 if ti < 16 else 2
            pi = 0
            k0 = max(0, ti - 4)
            for hk in range(2):
                hs = slice(hk * 64, (hk + 1) * 64)
                qrhs = pb.nqT[hs, :, :].rearrange("p g q -> p (g q)")
                bCa = nb()
                bCb = nb()
                for nch in range(nchunks):
                    bk = nb()
                    k.mm(bk, bk[:, :], kcmp[hs, nch * 128:(nch + 1) * 128], qrhs, [kcmp.b, pb.nqT.b], start=True, stop=False)
                    k.mm(bk, bk[:, :], pb.cmbtb[:, nch, :], ident4[:, :], [pb.cmbtb.b, ident4.b], start=False, stop=True)
                    p_ = pT[pi % NPT]; pi += 1
                    k.act(p_[:, :], bk[:, :], AF.Exp, [bk.b], [p_.b], scale=0.125)
                    for g in range(4):
                        k.mm(bCa, bCa[:, g * 65:(g + 1) * 65], p_[:, g * 128:(g + 1) * 128], vcmp[:, nch, hk, 0:65],
                             [p_.b, vcmp.b], start=(nch == 0 and g == 0), stop=(nch == nchunks - 1), skip=True)
                    for g in range(4):
                        k.mm(bCb, bCb[:, g * 64:(g + 1) * 64], p_[:, g * 128:(g + 1) * 128], vcmp[:, nch, hk, 65:129],
                             [p_.b, vcmp.b], start=(nch == 0 and g == 0), stop=(nch == nchunks - 1), skip=True)
                cs = casb[hk]
                k.cp(cs[:, :], bCa[:, 0:260], [bCa.b], [cs.b], eng="act")
                k.ts("dve", rl[:, :], cs[:, 64:260:65], 1e-30, None, ALU.max, None, [cs.b], [rl.b])
                P.op("dve", lambda: nc.vector.reciprocal(out=rl[:, :], in_=rl[:, :]), [rl.b], [rl.b])
                k.ts("dve", imp[:, :], bCb[:, 0:64], rl[:, 0:1], None, ALU.mult, None, [bCb.b, rl.b], [imp.b])
                for g in range(1, 4):
                    k.stt("dve", imp[:, :], bCb[:, g * 64:(g + 1) * 64], rl[:, g:g + 1], imp[:, :], ALU.mult, ALU.add,
                          [bCb.b, rl.b, imp.b], [imp.b])
                yield
                k.tt("dve", imp[:, :], imp[:, :], pb.kf[:, 0, :], ALU.mult, [imp.b, pb.kf.b], [imp.b])
                k.tt("dve", imp[:, :], imp[:, :], pb.kf[:, 1, :], ALU.add, [imp.b, pb.kf.b], [imp.b])
                P.op("dve", lambda: nc.vector.max(out=m8a[:, :], in_=imp[:, :]), [imp.b], [m8a.b])
                P.op("dve", lambda: nc.vector.match_replace(out=impt[:, :], in_to_replace=m8a[:, :], in_values=imp[:, :],
                                                            imm_value=-1e9), [imp.b, m8a.b], [impt.b])
                P.op("dve", lambda: nc.vector.max(out=m8b[:, :], in_=impt[:, :]), [impt.b], [m8b.b])
                k.ts("dve", NT2[:, 64:128], imp[:, :], m8b[:, 7:8], -30000.0, ALU.is_lt, ALU.mult, [imp.b, m8b.b], [NT2.b])
                bt = nb()
                k.tr(bt, bt[:, 0:128], NT2[:, :], ident[:, :], [NT2.b, ident.b])
                k.cp(pb.QN[hk][64:128, :].rearrange("p (g q) -> p g q", g=4),
                     bt[64:128, 0:128].rearrange("p (o q) -> p o q", o=1).to_broadcast([64, 4, 128]), [bt.b], [pb.QN[hk].b], eng="act")
                yield
            for hk in range(2):
                hs = slice(hk * 64, (hk + 1) * 64)
                qrhs = pb.nqT[hs, :, :].rearrange("p g q -> p (g q)")
                pend = []
                for kc in range(k0, ti + 1):
                    bk = nb()
                    last_extra = (kc == ti) or (kc == ti - 4)
                    k.mm(bk, bk[:, :], pb.kwin[hs, (kc - k0) * 128:(kc - k0 + 1) * 128], qrhs, [pb.kwin.b, pb.nqT.b],
                         start=True, stop=not last_extra)
                    if kc == ti:
                        k.mm(bk, bk[:, :], CBT[:, :], ident4[:, :], [CBT.b, ident4.b], start=False, stop=True)
                    elif kc == ti - 4:
                        k.mm(bk, bk[:, :], WBT[:, :], ident4[:, :], [WBT.b, ident4.b], start=False, stop=True)
                    p_ = pT[pi % NPT]; pi += 1
                    k.act(p_[:, :], bk[:, :], AF.Exp, [bk.b], [p_.b], scale=0.125)
                    if len(pend) >= SKEW:
                        pend.pop(0)()
                    def pv(p_=p_, kc=kc, hk=hk):
                        for g in range(4):
                            k.mm(bwin, bwin[:, g * 65:(g + 1) * 65], p_[:, g * 128:(g + 1) * 128], pb.vwin[:, kc - k0, hk, :],
                                 [p_.b, pb.vwin.b], start=(kc == k0 and g == 0), stop=(kc == ti), skip=True)
                    pend.append(pv)
                    yield
                while pend:
                    pend.pop(0)()
                k.cp(wsb[hk][:, :], bwin[:, 0:260], [bwin.b], [wsb[hk].b], eng="dve")
                yield
            for hk in range(2):
                bsel = bselAB[hk]
                pend = []
                for kc in range(ti + 1):
                    bk = nb()
                    k.mm(bk, bk[:, :], ksE[hk][:, kc * 128:(kc + 1) * 128], pb.QN[hk][:, :], [ksE[hk].b, pb.QN[hk].b],
                         start=True, stop=(kc != ti))
                    if kc == ti:
                        k.mm(bk, bk[:, :], CBT[:, :], ident4[:, :], [CBT.b, ident4.b], start=False, stop=True)
                    p_ = pT[pi % NPT]; pi += 1
                    k.act(p_[:, :], bk[:, :], AF.Exp, [bk.b], [p_.b], scale=0.125)
                    if len(pend) >= SKEW:
                        pend.pop(0)()
                    def pv(p_=p_, kc=kc, hk=hk, bsel=bsel):
                        for g in range(4):
                            k.mm(bsel, bsel[:, g * 65:(g + 1) * 65], p_[:, g * 128:(g + 1) * 128], vs[:, kc, hk, :],
                                 [p_.b, vs.b], start=(kc == 0 and g == 0), stop=(kc == ti), skip=True)
                    pend.append(pv)
                    yield
                while pend:
                    pend.pop(0)()
                cs = casb[hk]; ws = wsb[hk]
                k.cp(Lg[:, :, 0], cs[:, 64:260:65], [cs.b], [Lg.b], eng="dve")
                k.cp(Lg[:, :, 1], bsel[:, 64:260:65], [bsel.b], [Lg.b], eng="dve")
                k.cp(Lg[:, :, 2], ws[:, 64:260:65], [ws.b], [Lg.b], eng="dve")
                k.ts("dve", coef[:, :, :], Lg[:, :, :], 1e-30, None, ALU.max, None, [Lg.b], [coef.b])
                P.op("dve", lambda: nc.vector.reciprocal(out=coef[:, :, :], in_=coef[:, :, :]), [coef.b], [coef.b])
                k.tt("dve", coef[:, :, :], coef[:, :, :], pb.sg[:, hk * 12:(hk + 1) * 12].rearrange("p (g b) -> p g b", b=3),
                     ALU.mult, [coef.b, pb.sg.b], [coef.b])
                for g in range(4):
                    col = 512 + (hk * 4 + g) * 64
                    k.ts("dve", tn[:, :], cs[:, g * 65:g * 65 + 64], coef[:, g, 0:1], None, ALU.mult, None, [cs.b, coef.b], [tn.b])
                    k.stt("dve", tn[:, :], ws[:, g * 65:g * 65 + 64], coef[:, g, 2:3], tn[:, :], ALU.mult, ALU.add,
                          [ws.b, coef.b, tn.b], [tn.b])
                    k.stt("dve", omix[:, col:col + 64], bsel[:, g * 65:g * 65 + 64], coef[:, g, 1:2], tn[:, :], ALU.mult, ALU.add,
                          [bsel.b, coef.b, tn.b], [omix.b])
                yield

        def prologue(sq_i, ti):
            pb = PB[ti % 2]
            x = xt[ti % 2]
            t0 = sq_i * T_ + ti * 128
            P.dma(x[:, :], D["x1"][t0:t0 + 128, :], reads=[D["x1_b"]], writes=[x.b])
            P.dma(pb.kf[:, :, :], D["KF"][ti], writes=[pb.kf.b])
            P.dma(cmbt[:, :, :], D["CMBT"][ti], writes=[cmbt.b])
            k0 = max(0, ti - 4)
            nk = ti + 1 - k0
            P.dma(pb.kwin[:, 0:nk * 128], D["kwT"][sq_i][:, k0 * 128:(ti + 1) * 128], reads=[D["state_b"]], writes=[pb.kwin.b])
            P.dma(pb.vwin[:, 0:nk, :, :], D["vw"][sq_i][:, k0:ti + 1, :, :], reads=[D["state_b"]], writes=[pb.vwin.b])
            k.cp(pb.cmbtb[:, :, :], cmbt[:, :, :], [cmbt.b], [pb.cmbtb.b], eng="pool")
            yield
            yield
            yield
            for q in range(2):
                bk = nb()
                for j in range(4):
                    kc = q * 4 + j
                    k.tr(bk, bk[:, j * 128:(j + 1) * 128], x[:, kc * 128:(kc + 1) * 128], ident[:, :], [x.b, ident.b])
                k.cp(xT[:, q * 4:(q + 1) * 4, :], bk[:, :].rearrange("p (j c) -> p j c", j=4), [bk.b], [xT.b])
                yield
            for q in range(4):
                bk = nb()
                for j in range(4):
                    c = q * 4 + j
                    for kc in range(8):
                        k.mm(bk, bk[:, j * 128:(j + 1) * 128], win[:, kc, c * 128:(c + 1) * 128], xT[:, kc, :],
                             [win.b, xT.b], start=(kc == 0), stop=(kc == 7))
                if q < 3:
                    k.cp(raw[:, q * 4:(q + 1) * 4, 3:131], bk[:, :].rearrange("p (j c) -> p j c", j=4), [bk.b], [raw.b])
                else:
                    k.cp(pb.nqT[:, :, :], bk[:, :].rearrange("p (j c) -> p j c", j=4), [bk.b], [pb.nqT.b])
                    k.cp(pb.QN[0][0:64, :], bk[0:64, :], [bk.b], [pb.QN[0].b])
                yield
            bk = nb()
            for g in range(4):
                for kc in range(8):
                    k.mm(bk, bk[0:64, g * 128:(g + 1) * 128], win[:, kc, 1536 + g * 128 + 64:1536 + (g + 1) * 128], xT[:, kc, :],
                         [win.b, xT.b], start=(kc == 0), stop=(kc == 7))
            k.cp(pb.QN[1][0:64, :], bk[0:64, :], [bk.b], [pb.QN[1].b])
            yield
            bz = nb()
            for kc in range(8):
                k.mm(bz, bz[:, :], xT[:, kc, :], win[:, kc, TM_Z:TM_Z + 512], [win.b, xT.b], start=(kc == 0), stop=(kc == 7))
            k.cp(pb.zt[:, :], bz[:, :], [bz.b], [pb.zt.b], eng="act")
            bs_ = nb()
            for kc in range(8):
                k.mm(bs_, bs_[:, 0:32], xT[:, kc, :], win[:, kc, TM_S:TM_S + 32], [win.b, xT.b], start=(kc == 0), stop=(kc == 7))
            k.cp(sm[:, :], bs_[:, 0:32], [bs_.b], [sm.b], eng="dve")
            yield
            def cdst(c):
                return (sil[:, c, :], silb[c]) if c < 8 else (pb.silv[:, c - 8, :], pb.silvb[c - 8])
            for kk in (3, 2, 1, 0):
                for half in range(2):
                    for c in range(half * 6, half * 6 + 6):
                        a_, ab = cdst(c)
                        if kk == 3:
                            k.ts("dve", a_, raw[:, c, 3:131], convw[:, c, 3:4], None, ALU.mult, None, [raw.b, convw.b],
                                 [ab] if c < 8 else [ab, pb.silv.b])
                        else:
                            k.stt("dve", a_, raw[:, c, kk:kk + 128], convw[:, c, kk:kk + 1], a_, ALU.mult, ALU.add,
                                  [raw.b, convw.b, ab], [ab])
                    yield
            k.cp(raw[:, :, 0:3], raw[:, :, 128:131], [raw.b], [raw.b], eng="pool")
            yield
            yield
            k.act(sil[:, :, :], sil[:, :, :], AF.Silu, silb, silb)
            k.act(pb.silv[:, :, :], pb.silv[:, :, :], AF.Silu, pb.silvb, pb.silvb + [pb.silv.b])
            k.act(pb.zt[:, :], pb.zt[:, :], AF.Silu, [pb.zt.b], [pb.zt.b])
            yield
            yield
            yield
            k.tt("pool", pb.zt[:, :], pb.zt[:, :], normw[:, :], ALU.mult, [pb.zt.b, normw.b], [pb.zt.b])
            k.tt("dve", sq[0][:, :], sil[:, 0, :], sil[:, 0, :], ALU.mult, [silb[0]], [sq[0].b])
            yield
            for c in range(8):
                h = c % 4
                s_ = sq[c % 2]; r_ = rn[c % 2]
                bk = nb()
                k.mm(bk, bk[:, 0:128], ones[:, :], s_[:, :], [ones.b, s_.b])
                k.act(r_[:, :], bk[:, 0:128], AF.Ln, [bk.b], [r_.b], bias=1e-6)
                k.act(r_[:, :], r_[:, :], AF.Exp, [r_.b], [r_.b], scale=-0.5, bias=(-0.5 * float(np.log(128.0)) if c < 4 else 0.0))
                if c + 1 < 8:
                    k.tt("dve", sq[(c + 1) % 2][:, :], sil[:, c + 1, :], sil[:, c + 1, :], ALU.mult, [silb[c + 1]], [sq[(c + 1) % 2].b])
                yield
                if c < 4:
                    k.tt("dve", pb.qnb[:, h, :], sil[:, c, :], r_[:, :], ALU.mult, [silb[c], r_.b], [pb.qnb.b])
                else:
                    k.tt("dve", pb.kn[:, h, :], sil[:, c, :], r_[:, :], ALU.mult, [silb[c], r_.b], [pb.kn.b])
                    k.cp(pb.knb[:, h, :], pb.kn[:, h, :], [pb.kn.b], [pb.knb.b], eng="pool")
            yield
            k.act(smt[:, 0:4], sm[:, 0:4], AF.Exp, [sm.b], [smt.b], scale=-1.0)
            k.act(smt[:, 8:32], sm[:, 8:32], AF.Exp, [sm.b], [smt.b], scale=-1.0)
            k.tt("dve", smt[:, 4:8], sm[:, 4:8], dtb[:, :], ALU.add, [sm.b, dtb.b], [smt.b])
            k.act(smt[:, 4:8], smt[:, 4:8], AF.Exp, [smt.b], [smt.b])
            k.ts("dve", smt[:, :], smt[:, :], 1.0, None, ALU.add, None, [smt.b], [smt.b])
            k.act(pb.g_[:, :], smt[:, 4:8], AF.Ln, [smt.b], [pb.g_.b])
            k.tt("dve", pb.g_[:, :], pb.g_[:, :], negA[:, :], ALU.mult, [pb.g_.b, negA.b], [pb.g_.b])
            P.op("dve", lambda: nc.vector.reciprocal(out=pb.beta[:, :], in_=smt[:, 0:4]), [smt.b], [pb.beta.b])
            k.ts("dve", pb.nbeta[:, :], pb.beta[:, :], -1.0, None, ALU.mult, None, [pb.beta.b], [pb.nbeta.b])
            P.op("dve", lambda: nc.vector.reciprocal(out=pb.sg[:, :], in_=smt[:, 8:32]), [smt.b], [pb.sg.b])
            yield
            bk = nb()
            k.mm(bk, bk[:, 0:4], U[:, :], pb.g_[:, :], [U.b, pb.g_.b])
            k.mm(bk, bk[:, 4:8], ones[:, :], pb.g_[:, :], [ones.b, pb.g_.b])
            k.cp(gcs[:, :], bk[:, 0:8], [bk.b], [gcs.b], eng="dve")
            k.cp(pb.gcp[:, 0:4], gcs[:, 0:4], [gcs.b], [pb.gcp.b], eng="dve")
            k.ts("dve", pb.gcp[:, 4:8], gcs[:, 0:4], -1.0, None, ALU.mult, None, [gcs.b], [pb.gcp.b])
            k.act(pb.egc[:, :], gcs[:, 0:4], AF.Exp, [gcs.b], [pb.egc.b])
            k.act(pb.egl[:, :], gcs[:, 4:8], AF.Exp, [gcs.b], [pb.egl.b])
            k.tt("dve", pb.eglm[:, :], gcs[:, 4:8], gcs[:, 0:4], ALU.subtract, [gcs.b], [pb.eglm.b])
            k.act(pb.eglm[:, :], pb.eglm[:, :], AF.Exp, [pb.eglm.b], [pb.eglm.b])
            for h in range(4):
                k.ts("dve", pb.Gm[h][:, :], ones[:, :], pb.g_[:, h:h + 1], None, ALU.mult, None, [ones.b, pb.g_.b], [pb.Gm[h].b])
            yield

        PRO_W = int(_os.environ.get("PRO_W", "1"))

        def run_all(gens, weights=None, periods=None):
            gens = list(gens)
            weights = dict(weights or {})
            periods = dict(periods or {})
            r = 0
            while gens:
                long_alive = any(id(g) not in periods for g in gens)
                for s_ in list(gens):
                    per, ph = periods.get(id(s_), (1, 0))
                    if long_alive and per > 1 and r % per != ph % per:
                        continue
                    for _ in range(weights.get(id(s_), 1)):
                        try:
                            next(s_)
                        except StopIteration:
                            gens.remove(s_)
                            break
                r += 1

        GDN_SPREAD = int(_os.environ.get("GDN_SPREAD", "1"))

        for sq_i in range(nseq):
            sb = D["state_b"]
            P.dma(ksE[0][0:64, :], D["ksT"][sq_i][0:64, :], reads=[sb], writes=[ksE[0].b])
            P.dma(ksE[1][0:64, :], D["ks1T"][sq_i], reads=[sb], writes=[ksE[1].b])
            P.dma(vs[:, :, :, :], D["vs"][sq_i], reads=[sb], writes=[vs.b])
            P.dma(kcmp[:, :], D["kcmp"][sq_i], reads=[sb], writes=[kcmp.b])
            P.dma(vcmp[:, :, :, :], D["vcmp"][sq_i], reads=[sb], writes=[vcmp.b])
            k.memset("pool", raw[:, :, 0:3], 0.0, [raw.b])
            for h in range(4):
                k.memset("pool", St[h][:, :], 0.0, [St[h].b])
                k.memset("pool", Sb[h][:, :], 0.0, [Sb[h].b])
            run_all([prologue(sq_i, 0)])
            if sq_i == 0:
                for kc in range(8):
                    s = xt[1]
                    P.dma(s[:, :], D["w_out"][kc * 128:(kc + 1) * 128, :], writes=[s.b])
                    k.cp(wout[:, kc, :], s[:, :], [s.b], [wout.b])
                for c4 in range(T_ // 1024 if T_ >= 1024 else 1):
                    w_ = min(1024, T_)
                    s = xt[1]
                    P.dma(s[64:128, 0:w_], D["Econst"][:, c4 * w_:(c4 + 1) * w_], writes=[s.b])
                    for i in range(2):
                        k.cp(ksE[i][64:128, c4 * w_:(c4 + 1) * w_], s[64:128, 0:w_], [s.b], [ksE[i].b])

            def epilogue(ti):
                par = ti % 2
                pb = PB[par]; x = xt[par]; omix = omixs[par]; og = ogs[par]; ogb = ogbs[par]; ms = mss[par]; msb = msbs[par]
                t0 = sq_i * T_ + ti * 128
                k.ts("dve", y[:, :], x[:, :], DN_ALPHA, None, ALU.mult, None, [x.b], [y.b])
                k.act(rstd[:, :], ms[:, :], AF.Ln, msb, [rstd.b], scale=1.0 / 128.0, bias=1e-6)
                k.act(rstd[:, :], rstd[:, :], AF.Exp, [rstd.b], [rstd.b], scale=-0.5)
                yield
                for h in range(4):
                    k.stt("dve", omix[:, h * 128:(h + 1) * 128], og[:, h, :], rstd[:, h:h + 1], pb.zt[:, h * 128:(h + 1) * 128],
                          ALU.mult, ALU.mult, [ogb[h], rstd.b, pb.zt.b], [omix.b])
                if dbg:
                    P.dma(D["dbg"][t0:t0 + 128, :], omix[:, :], reads=[omix.b], writes=[D["dbg_b"]])
                yield
                for q in range(2):
                    bk = nb()
                    for j in range(4):
                        cc = q * 4 + j
                        k.tr(bk, bk[:, j * 128:(j + 1) * 128], omix[:, cc * 128:(cc + 1) * 128], ident[:, :], [omix.b, ident.b])
                    k.cp(omixT[:, q * 4:(q + 1) * 4, :], bk[:, :].rearrange("p (j c) -> p j c", j=4), [bk.b], [omixT.b])
                    yield
                for dh in range(2):
                    bk = nb()
                    for cc in range(8):
                        k.mm(bk, bk[:, :], omixT[:, cc, :], wout[:, cc, dh * 512:(dh + 1) * 512], [omixT.b, wout.b],
                             start=(cc == 0), stop=(cc == 7))
                    k.tt("dve", y[:, dh * 512:(dh + 1) * 512], bk[:, :], y[:, dh * 512:(dh + 1) * 512], ALU.add, [y.b, bk.b], [y.b])
                    yield
                layer_norm(P, [y], st, mv, rs, None, None, lnexp=True, affine=False)
                P.dma(D["x2"][t0:t0 + 128, :], y[:, :], reads=[y.b], writes=[D["x2_b"]])
                yield

            for ti in range(NT):
                pb = PB[ti % 2]
                gens = []
                if ti > 0:
                    gens.append(epilogue(ti - 1))
                gg = [gdn_stream(h, pb, ti % 2) for h in range(4)]
                gens += gg + [nsa_stream(ti, pb, ti % 2)]
                wts = {}
                pers = {}
                if GDN_SPREAD:
                    n_rounds = 2 * (ti + 1) + 2 * min(5, ti + 1) + 8
                    per = max(1, min(int(_os.environ.get('GDN_MAXP', '3')), n_rounds // int(_os.environ.get('GDN_DIV', '15'))))
                    for h, g in enumerate(gg):
                        pers[id(g)] = (per, h)
                if ti + 1 < NT:
                    pg = prologue(sq_i, ti + 1)
                    gens.append(pg)
                    wts[id(pg)] = PRO_W
                run_all(gens, wts, pers)
            run_all([epilogue(NT - 1)])
    P.barrier()


def _w_in_perm():
    o = {}
    off = 0
    for nm, sz in (("gq", 512), ("gk", 512), ("gv", 512), ("gz", 512), ("gb", 4), ("ga", 4), ("nq", 512), ("kc", 128),
                   ("vc", 128), ("ks", 128), ("vs", 128), ("kw", 128), ("vw", 128), ("gates", 24)):
        o[nm] = np.arange(off, off + sz)
        off += sz
    nq = o["nq"].reshape(2, 4, 64).transpose(1, 0, 2).reshape(-1)
    return np.concatenate([o["gq"], o["gk"], o["gv"], nq, o["gz"], o["gb"], o["ga"], o["gates"],
                           o["kc"], o["vc"], o["ks"], o["kw"], o["vs"], o["vw"]])


def host_consts(T_):
    NT = T_ // 128
    NC = (T_ - 32) // 16 + 1
    f = np.float32
    p = np.arange(128)[:, None]
    q = np.arange(128)[None, :]
    c = {}
    c["ident"] = np.eye(128, dtype=f)
    c["U"] = (p <= q).astype(f)
    c["NegU"] = -c["U"]
    c["ones"] = np.ones((128, 128), f)
    c["M1"] = np.where(q < p, 0.0, -1e30).astype(f)
    c["M2"] = np.where(p <= q, 0.0, -1e30).astype(f)
    c["ident4"] = np.tile(np.eye(128, dtype=f), (1, 4))
    c["CBT"] = np.where(q <= p, 0.0, -30000.0).astype(f)
    c["WBT"] = np.where(q > p, 0.0, -30000.0).astype(f)
    cs = np.arange(256) * 16
    ss = np.arange(64) * 64
    ov = np.clip(np.minimum(cs[:, None] + 32, ss[None, :] + 64) - np.maximum(cs[:, None], ss[None, :]), 0, None) / 32.0
    ov[NC:] = 0.0
    c["c2s"] = np.ascontiguousarray(ov.reshape(2, 128, 64).transpose(1, 0, 2)).astype(f)
    KF = np.zeros((NT, 128, 2, 64), f)
    CM = np.zeros((NT, 128, 2, 128), f)
    j = np.arange(64)[None, :]
    for ti in range(NT):
        t = ti * 128 + np.arange(128)[:, None]
        cur = t // 64
        visible = j <= cur
        forced = (j == 0) | (j == cur) | (j == cur - 1)
        KF[ti, :, 0, :] = (visible & ~forced)
        KF[ti, :, 1, :] = np.where(visible, np.where(forced, 1e4, 0.0), -1.0)
        n = np.arange(256)[None, :]
        valid = (16 * n + 31) <= t
        CM[ti] = np.where(valid, 0.0, -30000.0).reshape(128, 2, 128)
    c["KF"] = KF
    c["CMBT"] = CM
    c["Econst"] = (np.arange(T_)[None, :] // 64 == np.arange(64)[:, None]).astype(f)
    return c


def host_prep(inp, T_):
    f = np.float32
    d = dict(host_consts(T_))
    bc = lambda v, n=128: np.ascontiguousarray(np.broadcast_to(np.asarray(v, f).reshape(1, -1), (n, np.asarray(v).size)))
    d["w_in"] = np.ascontiguousarray(inp["w_in"][0][:, _w_in_perm()])
    d["w_out"] = np.ascontiguousarray(inp["w_out"][0])
    d["convw"] = np.ascontiguousarray(inp["gdn_conv_w"][0].reshape(4, 12, 128).transpose(2, 1, 0))
    d["alog"] = bc(inp["gdn_a_log"][0])
    d["dtb"] = bc(inp["gdn_dt_bias"][0])
    d["normw"] = bc(np.tile(inp["gdn_norm_w"][0], 4))
    for i in (1, 2, 3):
        d["ln%dg" % i] = bc(inp["ln%d_g" % i][0])
        d["ln%db" % i] = bc(inp["ln%d_b" % i][0])
    cw1 = []
    for nm in ("nsa_cmp_k_w1", "nsa_cmp_v_w1"):
        w = inp[nm][0].reshape(32, 64, 256).transpose(1, 0, 2)
        cw1.append(np.concatenate([w, w], 0))
    d["cw1"] = np.ascontiguousarray(np.stack(cw1))
    w2k = inp["nsa_cmp_k_w2"][0].reshape(2, 128, 64).transpose(1, 0, 2)
    d["cw2k"] = np.ascontiguousarray(np.concatenate([w2k, w2k], -1))
    d["cw2v"] = np.ascontiguousarray(inp["nsa_cmp_v_w2"][0].reshape(2, 128, 64).transpose(1, 0, 2))
    d["posT"] = np.ascontiguousarray(np.stack([np.concatenate([inp[nm][0].T] * 2, 0)
                                               for nm in ("nsa_cmp_pos_k", "nsa_cmp_pos_v")]))
    for i in (1, 2):
        d["f%dwg" % i] = np.ascontiguousarray(inp["ffn%d_wg" % i][0])
        d["f%dwu" % i] = np.ascontiguousarray(inp["ffn%d_wu" % i][0])
        d["f%dwd" % i] = np.ascontiguousarray(inp["ffn%d_wd" % i][0])
    return d


CONST_SHAPES = None


def build_program(nseq, T_, phases=("ffn1", "b1", "b2", "ffn2"), dbg=False):
    nc = bass.Bass("TRN2", target_bir_lowering=False)
    ntok = nseq * T_
    NT = T_ // 128
    shp = {k_: v.shape for k_, v in host_consts(T_).items()}
    shp.update({"w_in": (D_MODEL, N_IN), "w_out": (D_MODEL, D_MODEL), "convw": (128, 12, 4), "alog": (128, 4), "dtb": (128, 4),
                "normw": (128, 512), "cw1": (2, 128, 32, 256), "cw2k": (128, 2, 128), "cw2v": (128, 2, 64), "posT": (2, 128, 32)})
    for i in (1, 2, 3):
        shp["ln%dg" % i] = (128, D_MODEL)
        shp["ln%db" % i] = (128, D_MODEL)
    for i in (1, 2):
        shp["f%dwg" % i] = (D_MODEL, D_FF)
        shp["f%dwu" % i] = (D_MODEL, D_FF)
        shp["f%dwd" % i] = (D_FF, D_MODEL)
    D = {}
    for nm, s in shp.items():
        D[nm] = nc.dram_tensor(nm, list(s), F32, kind="ExternalInput").ap()

    def act_tensor(nm, first_in, last_out):
        kind = "ExternalInput" if first_in else ("ExternalOutput" if last_out else "Internal")
        D[nm] = nc.dram_tensor(nm, [ntok, D_MODEL], F32, kind=kind).ap()
        D[nm + "_b"] = Buf(nm)
    act_tensor("x", "ffn1" in phases, False)
    act_tensor("x1", "ffn1" not in phases, phases[-1] == "ffn1")
    act_tensor("x2", False, phases[-1] == "b2")
    act_tensor("out", False, phases[-1] == "ffn2")
    for nm, s in (("ksT", [nseq, 128, T_]), ("ks1T", [nseq, 64, T_]), ("kwT", [nseq, 128, T_]), ("vs", [nseq, 128, NT, 2, 65]),
                  ("vw", [nseq, 128, NT, 2, 65]), ("kcmp", [nseq, 128, 256]), ("vcmp", [nseq, 128, 2, 2, 129])):
        D[nm] = nc.dram_tensor(nm, s, BF16, kind="Internal").ap()
    D["state_b"] = Buf("state")
    if dbg:
        D["dbg"] = nc.dram_tensor("dbg", [ntok, D_MODEL], F32, kind="ExternalOutput").ap()
        D["dbg_b"] = Buf("dbg")
    if phases[-1] == "b1":
        D["dummy"] = nc.dram_tensor("dummyo", [128, 128], F32, kind="ExternalOutput").ap()
    with ExitStack() as es:
        P = Prog(nc)
        banks = [T(es, nc, "bank%d" % i, [128, 512], F32, psum=True) for i in range(8)]
        if "ffn1" in phases:
            ffn_phase(P, D["x"], D["x_b"], D["x1"], D["x1_b"], D["f1wg"], D["f1wu"], D["f1wd"], D["ln1g"], D["ln1b"],
                      D["ident"], ntok, banks, pfx="f1_")
        if "b1" in phases:
            phase_b1(P, D, banks, nseq, T_)
        if phases[-1] == "b1":
            P.dma(D["dummy"], D["ident"])
        if "b2" in phases:
            phase_b2(P, D, banks, nseq, T_, dbg=dbg)
        if "ffn2" in phases:
            ffn_phase(P, D["x2"], D["x2_b"], D["out"], D["out_b"], D["f2wg"], D["f2wu"], D["f2wd"], D["ln3g"], D["ln3b"],
                      D["ident"], ntok, banks, pfx="f2_", pre_g=D["ln2g"], pre_b=D["ln2b"])
        P.emit(es)
    return nc


_NC_CACHE = {}


def kernel(**inputs):
    inputs = {k_: np.asarray(v) for k_, v in inputs.items()}
    T_ = SEQ
    nseq = SEQ_PER_CORE
    if "nc" not in _NC_CACHE:
        _NC_CACHE["nc"] = build_program(nseq, T_)
    nc = _NC_CACHE["nc"]
    d = host_prep(inputs, T_)
    x = inputs["x"].astype(np.float32, copy=False)
    in_maps = []
    for c in range(N_CORES):
        m = dict(d)
        m["x"] = np.ascontiguousarray(x[c * nseq:(c + 1) * nseq].reshape(nseq * T_, D_MODEL))
        in_maps.append(m)
    res = run_bass_kernel_spmd(nc, in_maps, core_ids=list(range(N_CORES)))
    out = np.concatenate([np.asarray(r["out"]).reshape(nseq, T_, D_MODEL) for r in res.results], axis=0)
    return out.astype(np.float32, copy=False)
```

```python
import numpy as np
from contextlib import ExitStack
import concourse.bass as bass
import concourse.mybir as mybir
from concourse.bass_utils import run_bass_kernel_spmd

F32 = mybir.dt.float32
BF16 = mybir.dt.bfloat16
AF = mybir.ActivationFunctionType
ALU = mybir.AluOpType
AX = mybir.AxisListType

D_MODEL = 1024
D_FF = 2816
SEQ = 4096
N_CORES = 8
SEQ_PER_CORE = 2
LN_EPS = 1e-5
DN_ALPHA = 2.0 ** 0.25


class Buf:
    __slots__ = ("name", "last_w", "readers", "psum")

    def __init__(self, name, psum=False):
        self.name = name
        self.last_w = None
        self.readers = {}
        self.psum = psum


class Op:
    __slots__ = ("eng", "fn", "is_dma", "waits", "inc", "count", "sem", "val", "idx", "presem")

    def __init__(self, eng, fn, is_dma):
        self.eng = eng
        self.fn = fn
        self.is_dma = is_dma
        self.waits = []
        self.inc = False
        self.count = 0
        self.sem = None
        self.val = 0
        self.presem = None


class Prog:
    ENGS = ("pe", "dve", "act", "pool", "sp")

    def __init__(self, nc, n_dma_sems=40):
        self.nc = nc
        self.ops = []
        self.n_dma_sems = n_dma_sems
        self.dma_slot = 0
        self.dma_slot_last = [None] * n_dma_sems
        self.last_op = {e: None for e in self.ENGS}
        self.pending_dmas = []
        self.nbar = 0
        self.same_engine_raw = True

    def eng_obj(self, e):
        nc = self.nc
        return {"pe": nc.tensor, "dve": nc.vector, "act": nc.scalar, "pool": nc.gpsimd, "sp": nc.sync}[e]

    def _add_dep(self, op, dep, kind):
        if dep is None or dep is op:
            return
        if not dep.is_dma and not op.is_dma and dep.eng == op.eng:
            if op.eng == "pe" or not self.same_engine_raw:
                return
        op.waits.append(dep)

    def op(self, eng, fn, reads=(), writes=(), dma=False):
        o = Op(eng, fn, dma)
        for r in reads:
            self._add_dep(o, r.last_w, "raw")
            if r.psum:
                for e_, rd in r.readers.items():
                    if e_ != eng:
                        self._add_dep(o, rd, "rar")
        for w in writes:
            self._add_dep(o, w.last_w, "waw")
            for rd in w.readers.values():
                self._add_dep(o, rd, "war")
        for r in reads:
            key = ("dma", len(self.ops)) if dma else eng
            r.readers[key] = o
        for w in writes:
            w.last_w = o
            w.readers = {}
        if dma:
            slot = self.dma_slot
            self.dma_slot = (slot + 1) % self.n_dma_sems
            prev = self.dma_slot_last[slot]
            o.presem = prev
            o.sem = slot
            o.val = (prev.val if prev is not None else 0) + 16
            self.dma_slot_last[slot] = o
            self.pending_dmas.append(o)
        else:
            self.last_op[eng] = o
        self.ops.append(o)
        return o

    def dma(self, out, in_, reads=(), writes=(), eng="sp"):
        e = self.eng_obj(eng)
        return self.op(eng, lambda: e.dma_start(out=out, in_=in_), reads, writes, dma=True)

    def barrier(self):
        self.nbar += 1
        o = Op("all", None, False)
        o.waits = [x for x in self.last_op.values() if x is not None and x.eng != "sp"]
        o.val = list(self.pending_dmas)
        o.count = self.nbar
        self.pending_dmas = []
        self.ops.append(o)

    def emit(self, es):
        nc = self.nc
        csem = {e: es.enter_context(nc.semaphore("c_" + e)) for e in ("pe", "dve", "act", "pool")}
        dsem = [es.enter_context(nc.semaphore("d%d" % i)) for i in range(self.n_dma_sems)]
        bsem = es.enter_context(nc.semaphore("bar"))
        for o in self.ops:
            for d in o.waits:
                if not d.is_dma:
                    d.inc = True
        cnt = {e: 0 for e in csem}
        for o in self.ops:
            if o.eng != "all" and not o.is_dma and o.inc:
                cnt[o.eng] += 1
                o.count = cnt[o.eng]
        waited = {e: {} for e in self.ENGS}

        def do_wait(eng, key, sem, val):
            w = waited[eng]
            if w.get(key, 0) >= val:
                return
            w[key] = val
            self.eng_obj(eng).wait_ge(sem, val)

        for o in self.ops:
            if o.eng == "all":
                for d in o.waits:
                    for e in self.ENGS:
                        if e != d.eng:
                            do_wait(e, d.eng, csem[d.eng], d.count)
                for d in o.val:
                    do_wait("sp", ("d", d.sem), dsem[d.sem], d.val)
                nc.sync.sem_inc(bsem, 1)
                for e in ("pe", "dve", "act", "pool"):
                    do_wait(e, "bar", bsem, o.count)
                continue
            for d in o.waits:
                if d.is_dma:
                    do_wait(o.eng, ("d", d.sem), dsem[d.sem], d.val)
                else:
                    do_wait(o.eng, d.eng, csem[d.eng], d.count)
            if o.is_dma:
                if o.presem is not None:
                    do_wait(o.eng, ("d", o.sem), dsem[o.sem], o.presem.val)
                o.fn().then_inc(dsem[o.sem], 16)
            else:
                ins = o.fn()
                if o.inc:
                    ins.then_inc(csem[o.eng], 1)
        for d in self.pending_dmas:
            do_wait("sp", ("d", d.sem), dsem[d.sem], d.val)


class T:
    def __init__(self, es, nc, name, shape, dtype, psum=False):
        if psum:
            self.t = es.enter_context(nc.psum_tensor(name, shape, dtype))
        else:
            self.t = es.enter_context(nc.sbuf_tensor(name, shape, dtype))
        self.b = Buf(name, psum=psum)

    def __getitem__(self, k):
        return self.t[k]


def ffn_phase(P, x_d, x_b, out_d, out_b, wg_d, wu_d, wd_d, lng_d, lnb_d, ident_d, ntok, banks, TT=256, pfx="f1_", pre_g=None, pre_b=None):
    nc = P.nc
    KC = D_MODEL // 128
    FC = D_FF // 128
    NS = TT // 128
    ntiles = ntok // TT
    with ExitStack() as es:
        wg = T(es, nc, pfx + "wg_sb", [128, KC, D_FF], BF16)
        wu = T(es, nc, pfx + "wu_sb", [128, KC, D_FF], BF16)
        wd = T(es, nc, pfx + "wd_sb", [128, FC, D_MODEL], BF16)
        xt = [T(es, nc, pfx + "x_tok%d" % i, [128, NS * D_MODEL], F32) for i in range(2)]
        xT = T(es, nc, pfx + "xT", [128, KC, TT], BF16)
        hT = T(es, nc, pfx + "hT", [128, FC, TT], BF16)
        sl = [T(es, nc, pfx + "silu%d" % i, [128, TT], F32) for i in range(2)]
        yt = [T(es, nc, pfx + "y_tok%d" % i, [128, D_MODEL], F32) for i in range(NS)]
        lng = T(es, nc, pfx + "lng", [128, D_MODEL], F32)
        lnb = T(es, nc, pfx + "lnb", [128, D_MODEL], F32)
        ident = T(es, nc, pfx + "identf", [128, 128], F32)
        st = T(es, nc, pfx + "bnst", [128, NS, 2, 6], F32)
        mv = T(es, nc, pfx + "bnmv", [128, NS, 2], F32)
        rs = T(es, nc, pfx + "rstd", [128, NS], F32)

        P.dma(ident[:], ident_d, writes=[ident.b])
        P.dma(lng[:], lng_d, writes=[lng.b])
        P.dma(lnb[:], lnb_d, writes=[lnb.b])
        if pre_g is not None:
            pg = T(es, nc, pfx + "pre_g", [128, D_MODEL], F32)
            pbt = T(es, nc, pfx + "pre_b", [128, D_MODEL], F32)
            P.dma(pg[:], pre_g, writes=[pg.b])
            P.dma(pbt[:], pre_b, writes=[pbt.b])
        HALF = D_FF // 2
        cast_engs = ["act", "dve", "pool"]
        ci = 0

        def cast(dst, src, rb, wb):
            nonlocal ci
            e = cast_engs[ci % 3]
            ci += 1
            if e == "act":
                P.op("act", lambda: nc.scalar.copy(out=dst, in_=src), [rb], [wb])
            elif e == "dve":
                P.op("dve", lambda: nc.vector.tensor_copy(out=dst, in_=src), [rb], [wb])
            else:
                P.op("pool", lambda: nc.gpsimd.tensor_copy(out=dst, in_=src), [rb], [wb])

        si = 0
        QW = D_FF // 4

        def stage():
            nonlocal si
            t_ = xt[(si // 2) % 2]
            h_ = si % 2
            si += 1
            return t_, t_[:, h_ * D_MODEL:(h_ + 1) * D_MODEL], stg_b[(si - 1) % 4]
        stg_b = [Buf(pfx + "stg%d" % i) for i in range(4)]
        for w_d, w_sb in ((wg_d, wg), (wu_d, wu)):
            for kc in range(KC):
                for q4 in range(4):
                    t_, sv, sb_ = stage()
                    P.dma(sv[:, 0:QW], w_d[kc * 128:(kc + 1) * 128, q4 * QW:(q4 + 1) * QW], writes=[sb_])
                    cast(w_sb[:, kc, q4 * QW:(q4 + 1) * QW], sv[:, 0:QW], sb_, w_sb.b)
        for j in range(FC):
            t_, sv, sb_ = stage()
            P.dma(sv[:, 0:D_MODEL], wd_d[j * 128:(j + 1) * 128, :], writes=[sb_])
            cast(wd[:, j, :], sv[:, 0:D_MODEL], sb_, wd.b)
        bi = 0

        def nb():
            nonlocal bi
            b = banks[bi % len(banks)]
            bi += 1
            return b

        def load(ti):
            x = xt[ti % 2]
            P.dma(x[:, :].rearrange("p (s d) -> p s d", s=NS),
                  x_d[ti * TT:(ti + 1) * TT, :].rearrange("(s p) d -> p s d", p=128),
                  reads=[x_b], writes=[x.b] + (stg_b[2 * (ti % 2):2 * (ti % 2) + 2] if ti < 2 else []))
            if pre_g is not None:
                for s_i in range(NS):
                    xs = x[:, s_i * D_MODEL:(s_i + 1) * D_MODEL]
                    P.op("pool", lambda xs=xs: nc.gpsimd.tensor_tensor(out=xs, in0=xs, in1=pg[:, :], op=ALU.mult), [x.b, pg.b], [x.b])
                    P.op("pool", lambda xs=xs: nc.gpsimd.tensor_tensor(out=xs, in0=xs, in1=pbt[:, :], op=ALU.add), [x.b, pbt.b], [x.b])

        def transposes(ti):
            x = xt[ti % 2]
            for s in range(NS):
                for q in range(KC // 4):
                    bk = nb()
                    for j in range(4):
                        kc = q * 4 + j
                        P.op("pe", lambda bk=bk, j=j, s=s, kc=kc, x=x: nc.tensor.transpose(
                            out=bk[:, j * 128:(j + 1) * 128],
                            in_=x[:, s * D_MODEL + kc * 128: s * D_MODEL + (kc + 1) * 128],
                            identity=ident[:]), [x.b, ident.b], [bk.b])
                    P.op("act", lambda bk=bk, q=q, s=s: nc.scalar.copy(
                        out=xT[:, q * 4:(q + 1) * 4, s * 128:(s + 1) * 128],
                        in_=bk[:, :].rearrange("p (j c) -> p j c", j=4)), [bk.b], [xT.b])
            P.op("pool", lambda x=x: nc.gpsimd.tensor_scalar_mul(out=x[:, :], in0=x[:, :], scalar1=DN_ALPHA),
                 [x.b], [x.b])

        load(0)
        transposes(0)
        for ti in range(ntiles):
            x = xt[ti % 2]
            if ti + 1 < ntiles:
                load(ti + 1)
            for fc in range(FC):
                bk = nb()
                for wi, w_sb in enumerate((wg, wu)):
                    for kc in range(KC):
                        P.op("pe", lambda bk=bk, wi=wi, w_sb=w_sb, kc=kc, fc=fc: nc.tensor.matmul(
                            out=bk[:, wi * TT:(wi + 1) * TT], lhsT=w_sb[:, kc, fc * 128:(fc + 1) * 128],
                            rhs=xT[:, kc, :], start=(kc == 0), stop=(kc == KC - 1)),
                            [w_sb.b, xT.b], [bk.b])
                s_ = sl[fc % 2]
                P.op("act", lambda bk=bk, s_=s_: nc.scalar.activation(out=s_[:, :], in_=bk[:, 0:TT], func=AF.Silu),
                     [bk.b], [s_.b])
                P.op("dve", lambda bk=bk, s_=s_, fc=fc: nc.vector.tensor_tensor(
                    out=hT[:, fc, :], in0=bk[:, TT:2 * TT], in1=s_[:, :], op=ALU.mult),
                    [bk.b, s_.b], [hT.b])
            if ti + 1 < ntiles:
                transposes(ti + 1)
            for s in range(NS):
                y = yt[s]
                DW = 256
                for dh in range(D_MODEL // DW):
                    bk = nb()
                    for fc in range(FC):
                        P.op("pe", lambda bk=bk, fc=fc, s=s, dh=dh: nc.tensor.matmul(
                            out=bk[:, 0:DW], lhsT=hT[:, fc, s * 128:(s + 1) * 128],
                            rhs=wd[:, fc, dh * DW:(dh + 1) * DW], start=(fc == 0), stop=(fc == FC - 1)),
                            [hT.b, wd.b], [bk.b])
                    P.op("dve", lambda bk=bk, y=y, x=x, s=s, dh=dh: nc.vector.scalar_tensor_tensor(
                        out=y[:, dh * DW:(dh + 1) * DW], in0=bk[:, 0:DW], scalar=0.5,
                        in1=x[:, s * D_MODEL + dh * DW: s * D_MODEL + (dh + 1) * DW],
                        op0=ALU.mult, op1=ALU.add), [bk.b, x.b], [y.b])
            layer_norm(P, yt, st, mv, rs, lng, lnb)
            for s in range(NS):
                t0 = ti * TT + s * 128
                P.dma(out_d[t0:t0 + 128, :], yt[s][:, :], reads=[yt[s].b], writes=[out_b])
    P.barrier()


def layer_norm(P, ys, st, mv, rs, lng, lnb, lnexp=False, affine=True):
    nc = P.nc
    n = len(ys)
    for s, y in enumerate(ys):
        for h in range(2):
            P.op("dve", lambda h=h, s=s, y=y: nc.vector.bn_stats(out=st[:, s, h, :], in_=y[:, h * 512:(h + 1) * 512]),
                 [y.b], [st.b])
    for s in range(n):
        P.op("dve", lambda s=s: nc.vector.bn_aggr(out=mv[:, s, :], in_=st[:, s, :, :].rearrange("p a b -> p (a b)")),
             [st.b], [mv.b])
    P.op("dve", lambda: nc.vector.tensor_scalar(out=rs[:, 0:n], in0=mv[:, 0:n, 1], scalar1=LN_EPS, scalar2=None,
                                                op0=ALU.add), [mv.b], [rs.b])
    if lnexp:
        P.op("act", lambda: nc.scalar.activation(out=rs[:, 0:n], in_=rs[:, 0:n], func=AF.Ln), [rs.b], [rs.b])
        P.op("act", lambda: nc.scalar.activation(out=rs[:, 0:n], in_=rs[:, 0:n], func=AF.Exp, scale=-0.5), [rs.b], [rs.b])
    else:
        P.op("act", lambda: nc.scalar.activation(out=rs[:, 0:n], in_=rs[:, 0:n], func=AF.Sqrt), [rs.b], [rs.b])
        P.op("dve", lambda: nc.vector.reciprocal(out=rs[:, 0:n], in_=rs[:, 0:n]), [rs.b], [rs.b])
    for s, y in enumerate(ys):
        P.op("dve", lambda s=s, y=y: nc.vector.tensor_scalar(out=y[:, :], in0=y[:, :], scalar1=mv[:, s, 0:1],
                                                         scalar2=rs[:, s:s + 1], op0=ALU.subtract, op1=ALU.mult),
             [y.b, mv.b, rs.b], [y.b])
        if not affine:
            continue
        P.op("pool", lambda y=y: nc.gpsimd.tensor_tensor(out=y[:, :], in0=y[:, :], in1=lng[:, :], op=ALU.mult),
             [y.b, lng.b], [y.b])
        P.op("pool", lambda y=y: nc.gpsimd.tensor_tensor(out=y[:, :], in0=y[:, :], in1=lnb[:, :], op=ALU.add),
             [y.b, lnb.b], [y.b])


class K:
    def __init__(self, P):
        self.P = P
        self.nc = P.nc
        self.ei = 0

    def mm(self, bk, out, lhsT, rhs, rb, start=True, stop=True, skip=False):
        nc = self.nc
        self.P.op("pe", lambda: nc.tensor.matmul(out=out, lhsT=lhsT, rhs=rhs, start=start, stop=stop,
                                                 skip_group_check=skip), rb, [bk.b])

    def tr(self, bk, out, in_, ident, rb):
        nc = self.nc
        self.P.op("pe", lambda: nc.tensor.transpose(out=out, in_=in_, identity=ident), rb, [bk.b])

    def act(self, out, in_, func, rb, wb, scale=1.0, bias=0.0, accum=None):
        nc = self.nc
        self.P.op("act", lambda: nc.scalar.activation(out=out, in_=in_, func=func, bias=bias, scale=scale,
                                                      accum_out=accum), rb, wb)

    def ts(self, eng, out, in0, s1, s2, op0, op1, rb, wb):
        e = self.P.eng_obj(eng)
        if s2 is None:
            self.P.op(eng, lambda: e.tensor_scalar(out=out, in0=in0, scalar1=s1, scalar2=None, op0=op0), rb, wb)
        else:
            self.P.op(eng, lambda: e.tensor_scalar(out=out, in0=in0, scalar1=s1, scalar2=s2, op0=op0, op1=op1), rb, wb)

    def tt(self, eng, out, in0, in1, op, rb, wb):
        e = self.P.eng_obj(eng)
        self.P.op(eng, lambda: e.tensor_tensor(out=out, in0=in0, in1=in1, op=op), rb, wb)

    def stt(self, eng, out, in0, s, in1, op0, op1, rb, wb):
        e = self.P.eng_obj(eng)
        self.P.op(eng, lambda: e.scalar_tensor_tensor(out=out, in0=in0, scalar=s, in1=in1, op0=op0, op1=op1), rb, wb)

    def cp(self, out, in_, rb, wb, eng=None):
        nc = self.nc
        if eng is None:
            eng = ("act", "dve")[self.ei % 2]
            self.ei += 1
        if eng == "act":
            self.P.op("act", lambda: nc.scalar.copy(out=out, in_=in_), rb, wb)
        elif eng == "dve":
            self.P.op("dve", lambda: nc.vector.tensor_copy(out=out, in_=in_), rb, wb)
        else:
            self.P.op("pool", lambda: nc.gpsimd.tensor_copy(out=out, in_=in_), rb, wb)

    def memset(self, eng, ap, val, wb):
        e = self.P.eng_obj(eng)
        self.P.op(eng, lambda: e.memset(ap, val), [], wb)


N_IN = 3360
import os as _os
SKIP = set(_os.environ.get('KSKIP', '').split(','))
FM_COLS = 2048
TM_Z = 2048
TM_S = 2560
P1_COLS = 2592


def phase_b1(P, D, banks, nseq, T_):
    nc = P.nc
    k = K(P)
    NT = T_ // 128
    NC = (T_ - 32) // 16 + 1
    bi = 0

    def nb():
        nonlocal bi
        b = banks[bi % 6]
        bi += 1
        return b

    with ExitStack() as es:
        w1p = T(es, nc, "b1_w1p", [128, 8, 768], BF16)
        stg = [T(es, nc, "b1_b1stg%d" % i, [128, 1024], F32) for i in range(2)]
        cw1 = [T(es, nc, "b1_cw1_%d" % i, [128, 32, 256], BF16) for i in range(2)]
        cw2k = T(es, nc, "b1_cw2k", [128, 2, 128], BF16)
        cw2v = T(es, nc, "b1_cw2v", [128, 2, 64], BF16)
        posT = [T(es, nc, "b1_posT%d" % i, [128, 32], BF16) for i in range(2)]
        pb = T(es, nc, "b1_posb", [128, 4], F32)
        ident = T(es, nc, "b1_b1ident", [128, 128], F32)
        xt = [T(es, nc, "b1_b1x%d" % i, [128, D_MODEL], F32) for i in range(3)]
        xTs = [T(es, nc, "b1_b1xT%d" % i, [128, 8, 128], BF16) for i in range(2)]
        kcT = T(es, nc, "b1_kcT", [128, T_], BF16)
        vcT = T(es, nc, "b1_vcT", [128, T_], BF16)
        ksT = T(es, nc, "b1_b1ksT", [128, T_], BF16)
        ks1T = T(es, nc, "b1_b1ks1T", [64, T_], BF16)
        kwT = T(es, nc, "b1_b1kwT", [128, T_], BF16)
        vst = T(es, nc, "b1_b1vs", [128, NT, 2, 65], BF16)
        vwt = T(es, nc, "b1_b1vw", [128, NT, 2, 65], BF16)
        kcmp = T(es, nc, "b1_b1kcmp", [128, 256], BF16)
        vcmp = T(es, nc, "b1_b1vcmp", [128, 2, 2, 129], BF16)
        c2s = T(es, nc, "b1_b1c2s", [128, 2, 64], F32)
        hid = [T(es, nc, "b1_hid%d" % i, [128, 2, 256], BF16) for i in range(2)]
        gx = [T(es, nc, "b1_gx%d" % i, [128, 256], F32) for i in range(2)]
        gu = [T(es, nc, "b1_gu%d" % i, [128, 256], F32) for i in range(2)]

        P.dma(ident[:], D["ident"], writes=[ident.b])
        P.dma(c2s[:], D["c2s"], writes=[c2s.b])
        si = 0
        for kc in range(8):
            s = stg[si % 2]; si += 1
            P.dma(s[:, 0:768], D["w_in"][kc * 128:(kc + 1) * 128, P1_COLS:N_IN], writes=[s.b])
            k.cp(w1p[:, kc, :], s[:, 0:768], [s.b], [w1p.b])
        for sq in range(nseq):
            k.memset("pool", vst[:, :, :, 64:65], 1.0, [vst.b])
            k.memset("pool", vwt[:, :, :, 64:65], 1.0, [vwt.b])
            k.memset("pool", vcmp[:, :, :, 0:65], 0.0, [vcmp.b])
            k.memset("pool", kcmp[:, :], 0.0, [kcmp.b])
            k.memset("pool", vcmp[:, :, :, 64:65], 1.0, [vcmp.b])
            for nch in range(2):
                for hk in range(2):
                    k.cp(vcmp[:, nch, hk, 65:129], c2s[:, nch, :], [c2s.b], [vcmp.b], eng="pool")

            def load(ti):
                x = xt[ti % 3]
                t0 = sq * T_ + ti * 128
                P.dma(x[:, :], D["x1"][t0:t0 + 128, :], reads=[D["x1_b"]], writes=[x.b])

            def tposes(ti):
                x = xt[ti % 3]
                xT_ = xTs[ti % 2]
                for q in range(2):
                    bk = nb()
                    for j in range(4):
                        kc = q * 4 + j
                        k.tr(bk, bk[:, j * 128:(j + 1) * 128], x[:, kc * 128:(kc + 1) * 128], ident[:], [x.b, ident.b])
                    k.cp(xT_[:, q * 4:(q + 1) * 4, :], bk[:, :].rearrange("p (j c) -> p j c", j=4), [bk.b], [xT_.b])
            load(0)
            if NT > 1:
                load(1)
            tposes(0)
            for ti in range(NT if 'proj' not in SKIP else 0):
                xT = xTs[ti % 2]
                if ti + 2 < NT:
                    load(ti + 2)
                if ti + 1 < NT:
                    tposes(ti + 1)
                bk = nb()
                for c in range(4):
                    for kc in range(8):
                        k.mm(bk, bk[:, c * 128:(c + 1) * 128], w1p[:, kc, c * 128:(c + 1) * 128], xT[:, kc, :],
                             [w1p.b, xT.b], start=(kc == 0), stop=(kc == 7))
                for c, dst in enumerate((kcT, vcT, ksT, kwT)):
                    k.cp(dst[:, ti * 128:(ti + 1) * 128], bk[:, c * 128:(c + 1) * 128], [bk.b], [dst.b])
                bk = nb()
                for kc in range(8):
                    k.mm(bk, bk[0:64, 0:128], w1p[:, kc, 320:384], xT[:, kc, :], [w1p.b, xT.b], start=(kc == 0), stop=(kc == 7))
                k.cp(ks1T[0:64, ti * 128:(ti + 1) * 128], bk[0:64, 0:128], [bk.b], [ks1T.b])
                bk = nb()
                for kc in range(8):
                    k.mm(bk, bk[:, 0:256], xT[:, kc, :], w1p[:, kc, 512:768], [w1p.b, xT.b], start=(kc == 0), stop=(kc == 7))
                k.cp(vst[:, ti, :, 0:64], bk[:, 0:128].rearrange("p (h d) -> p h d", h=2), [bk.b], [vst.b])
                k.cp(vwt[:, ti, :, 0:64], bk[:, 128:256].rearrange("p (h d) -> p h d", h=2), [bk.b], [vwt.b])
            if sq == 0:
                dsi = [0]
                for kv in range(2):
                    for q in range(8):
                        s = stg[dsi[0] % 2]; dsi[0] += 1
                        P.dma(s[:, :].rearrange("p (l j) -> p l j", l=4), D["cw1"][kv, :, q * 4:(q + 1) * 4, :], writes=[s.b])
                        k.cp(cw1[kv][:, q * 4:(q + 1) * 4, :], s[:, :].rearrange("p (l j) -> p l j", l=4), [s.b], [cw1[kv].b])
                s = stg[dsi[0] % 2]; dsi[0] += 1
                P.dma(s[:, 0:256].rearrange("p (c j) -> p c j", c=2), D["cw2k"], writes=[s.b])
                k.cp(cw2k[:, :, :], s[:, 0:256].rearrange("p (c j) -> p c j", c=2), [s.b], [cw2k.b])
                s = stg[dsi[0] % 2]; dsi[0] += 1
                P.dma(s[:, 0:128].rearrange("p (c j) -> p c j", c=2), D["cw2v"], writes=[s.b])
                k.cp(cw2v[:, :, :], s[:, 0:128].rearrange("p (c j) -> p c j", c=2), [s.b], [cw2v.b])
                for kv in range(2):
                    s = stg[dsi[0] % 2]; dsi[0] += 1
                    P.dma(s[:, 0:32], D["posT"][kv], writes=[s.b])
                    k.cp(posT[kv][:, :], s[:, 0:32], [s.b], [posT[kv].b])
                bk = nb()
                for kv in range(2 if 'pb' not in SKIP else 0):
                    for jc in range(2):
                        col = kv * 2 + jc
                        for l in range(32):
                            k.mm(bk, bk[:, col:col + 1], cw1[kv][0:64, l, jc * 128:(jc + 1) * 128], posT[kv][0:64, l:l + 1],
                                 [cw1[kv].b, posT[kv].b], start=(l == 0), stop=(l == 31), skip=True)
                if 'pb' not in SKIP:
                    k.cp(pb[:, :], bk[:, 0:4], [bk.b], [pb.b], eng="dve")
                else:
                    k.memset('dve', pb[:, :], 0.0, [pb.b])

            for kv, src in enumerate((kcT, vcT) if 'cmp' not in SKIP else ()):
                for hk in range(2):
                    h_ = hid[hk]
                    for jc in range(2):
                        bk = nb()
                        for l in range(32):
                            k.mm(bk, bk[:, 0:NC], cw1[kv][hk * 64:(hk + 1) * 64, l, jc * 128:(jc + 1) * 128],
                                 src[hk * 64:(hk + 1) * 64, l:l + 16 * (NC - 1) + 1:16], [cw1[kv].b, src.b],
                                 start=(l == 0), stop=(l == 31))
                        x_ = gx[jc]; u_ = gu[jc]
                        col = kv * 2 + jc
                        k.ts("dve", x_[:, 0:NC], bk[:, 0:NC], pb[:, col:col + 1], None, ALU.add, None, [bk.b, pb.b], [x_.b])
                        k.tt("dve", u_[:, 0:NC], x_[:, 0:NC], x_[:, 0:NC], ALU.mult, [x_.b], [u_.b])
                        k.ts("dve", u_[:, 0:NC], u_[:, 0:NC], 0.044715, 1.0, ALU.mult, ALU.add, [u_.b], [u_.b])
                        k.tt("dve", u_[:, 0:NC], u_[:, 0:NC], x_[:, 0:NC], ALU.mult, [u_.b, x_.b], [u_.b])
                        k.act(u_[:, 0:NC], u_[:, 0:NC], AF.Exp, [u_.b], [u_.b], scale=-1.5957691216)
                        k.ts("dve", u_[:, 0:NC], u_[:, 0:NC], 1.0, None, ALU.add, None, [u_.b], [u_.b])
                        P.op("dve", lambda u_=u_: nc.vector.reciprocal(out=u_[:, 0:NC], in_=u_[:, 0:NC]), [u_.b], [u_.b])
                        k.tt("dve", h_[:, jc, 0:NC], u_[:, 0:NC], x_[:, 0:NC], ALU.mult, [u_.b, x_.b], [h_.b])
                    if kv == 0:
                        bk = nb()
                        for jc in range(2):
                            k.mm(bk, bk[:, 0:NC], cw2k[:, jc, :], h_[:, jc, 0:NC], [cw2k.b, h_.b], start=(jc == 0), stop=(jc == 1))
                        k.cp(kcmp[hk * 64:(hk + 1) * 64, 0:NC], bk[hk * 64:(hk + 1) * 64, 0:NC], [bk.b], [kcmp.b])
                    else:
                        for nch in range(2):
                            rows = min(NC - nch * 128, 128)
                            if rows <= 0:
                                continue
                            bk = nb()
                            for jc in range(2):
                                k.mm(bk, bk[0:rows, 0:64], h_[:, jc, nch * 128:nch * 128 + rows], cw2v[:, jc, :],
                                     [cw2v.b, h_.b], start=(jc == 0), stop=(jc == 1))
                            k.cp(vcmp[0:rows, nch, hk, 0:64], bk[0:rows, 0:64], [bk.b], [vcmp.b])
            if 'state' in SKIP:
                continue
            sb = D["state_b"]
            P.dma(D["ksT"][sq], ksT[:, :], reads=[ksT.b], writes=[sb])
            P.dma(D["kwT"][sq], kwT[:, :], reads=[kwT.b], writes=[sb])
            P.dma(D["ks1T"][sq], ks1T[0:64, :], reads=[ks1T.b], writes=[sb])
            P.dma(D["vs"][sq], vst[:, :, :, :], reads=[vst.b], writes=[sb])
            P.dma(D["vw"][sq], vwt[:, :, :, :], reads=[vwt.b], writes=[sb])
            P.dma(D["kcmp"][sq], kcmp[:, :], reads=[kcmp.b], writes=[sb])
            P.dma(D["vcmp"][sq], vcmp[:, :, :, :], reads=[vcmp.b], writes=[sb])
    P.barrier()


def phase_b2(P, D, banks, nseq, T_, dbg=False):
    nc = P.nc
    k = K(P)
    NT = T_ // 128
    bi = 0

    def nb():
        nonlocal bi
        b = banks[bi % 5]
        bi += 1
        return b
    bselAB, bwin = (banks[5], banks[6]), banks[7]

    with ExitStack() as es:
        def S(name, shape, dt=F32):
            return T(es, nc, "b2" + name, shape, dt)
        win = S("win", [128, 8, P1_COLS], BF16)
        wout = S("wout", [128, 8, D_MODEL], BF16)
        ident, U, NegU, ones, M1, M2 = [S(n, [128, 128]) for n in ("ident", "U", "NegU", "ones", "M1", "M2")]
        ident4 = S("ident4", [128, 512], BF16)
        CBT = S("CBT", [128, 128], BF16)
        WBT = S("WBT", [128, 128], BF16)
        convw = S("convw", [128, 12, 4])
        alog = S("alog", [128, 4]); dtb = S("dtb", [128, 4]); negA = S("negA", [128, 4])
        normw = S("normw", [128, 512])
        ksE = [S("ksE%d" % i, [128, T_], BF16) for i in range(2)]
        NT2 = S("NT2", [128, 128])
        vs = S("vs", [128, NT, 2, 65], BF16)
        kcmp = S("kcmp", [128, 256], BF16); vcmp = S("vcmp", [128, 2, 2, 129], BF16)
        xt = [S("x%d" % i, [128, D_MODEL]) for i in range(2)]
        xT = S("xT", [128, 8, 128], BF16)
        raw = S("raw", [128, 12, 131])
        cacc = [S("cacc%d" % i, [128, 128]) for i in range(4)]
        sil = S("sil", [128, 8, 128])
        silb = [Buf("silb%d" % i) for i in range(8)]
        sq = [S("sq%d" % i, [128, 128]) for i in range(2)]
        rn = [S("rn%d" % i, [128, 128]) for i in range(2)]
        sm = S("sm", [128, 32]); smt = S("smt", [128, 32])
        gcs = S("gcs", [128, 8])
        class NS:
            pass
        PB = []
        for par in range(2):
            pb = NS()
            sfx = "_p%d" % par
            pb.qnb = S("qnb" + sfx, [128, 4, 128], BF16); pb.knb = S("knb" + sfx, [128, 4, 128], BF16)
            pb.kn = S("kn" + sfx, [128, 4, 128]); pb.silv = S("silv" + sfx, [128, 4, 128])
            pb.silvb = [Buf("silvb%d" % i + sfx) for i in range(4)]
            pb.nqT = S("nqT" + sfx, [128, 4, 128], BF16); pb.zt = S("zt" + sfx, [128, 512])
            pb.QN = [S("QN%d" % i + sfx, [128, 512], BF16) for i in range(2)]
            pb.beta = S("beta" + sfx, [128, 4]); pb.nbeta = S("nbeta" + sfx, [128, 4]); pb.g_ = S("g" + sfx, [128, 4])
            pb.egc = S("egc" + sfx, [128, 4]); pb.egl = S("egl" + sfx, [128, 4]); pb.eglm = S("eglm" + sfx, [128, 4])
            pb.gcp = S("gcp" + sfx, [128, 8]); pb.sg = S("sg" + sfx, [128, 24]); pb.cmbtb = S("cmbtb" + sfx, [128, 2, 128], BF16); pb.kf = S("kf" + sfx, [128, 2, 64])
            pb.Gm = [S("Gm%d" % h + sfx, [128, 128]) for h in range(4)]
            pb.kwin = S("kwin" + sfx, [128, 640], BF16); pb.vwin = S("vwin" + sfx, [128, 5, 2, 65], BF16)
            PB.append(pb)
        HB = []
        for h in range(4):
            HB.append((S("DecS%d" % h, [128, 128]), S("DecTi%d" % h, [128, 128]),
                       [S("Lb%d_%d" % (h, i), [128, 3, 128]) for i in range(2)], S("TT%d" % h, [128, 128]),
                       S("QKdT%d" % h, [128, 128], BF16), S("vtok%d" % h, [128, 128]), S("kd%d" % h, [128, 128], BF16),
                       S("t1_%d" % h, [128, 128]), S("t2_%d" % h, [128, 128]), S("vnb%d" % h, [128, 128], BF16)))
        junk = [S("junk%d" % h, [128, 128], BF16) for h in range(4)]
        ogs = [S("og%d" % i, [128, 4, 128]) for i in range(2)]
        mss = [S("ms%d" % i, [128, 4]) for i in range(2)]
        ogbs = [[Buf("ogb%d_%d" % (i, h)) for h in range(4)] for i in range(2)]
        msbs = [[Buf("msb%d_%d" % (i, h)) for h in range(4)] for i in range(2)]
        rstd = S("rstd", [128, 4])
        St = [S("S%d" % h, [128, 128]) for h in range(4)]
        Sb = [S("Sb%d" % h, [128, 128], BF16) for h in range(4)]
        casb = [S("casb%d" % i, [128, 260]) for i in range(2)]
        wsb = [S("wsb%d" % i, [128, 260]) for i in range(2)]
        omixs = [S("omix%d" % i, [128, D_MODEL]) for i in range(2)]; omixT = S("omixT", [128, 8, 128], BF16)
        y = S("y", [128, D_MODEL])
        st = S("bnst", [128, 1, 2, 6]); mv = S("bnmv", [128, 1, 2]); rs = S("rs", [128, 1])
        SKEW = int(_os.environ.get("SKEW", "2"))
        NPT = SKEW + 2
        pT = [S("pT%d" % i, [128, 512], BF16) for i in range(NPT)]
        cmbt = S("cmbt", [128, 2, 128])
        lc = S("lc", [128, 4]); rl = S("rl", [128, 4]); imp = S("imp", [128, 64]); impt = S("impt", [128, 64])
        m8a = S("m8a", [128, 8]); m8b = S("m8b", [128, 8])
        Lg = S("Lg", [128, 4, 3]); coef = S("coef", [128, 4, 3]); tn = S("tn", [128, 64])

        for t_, nm in ((ident, "ident"), (U, "U"), (NegU, "NegU"), (ones, "ones"), (M1, "M1"), (M2, "M2"),
                       (convw, "convw"), (alog, "alog"), (dtb, "dtb"), (normw, "normw")):
            P.dma(t_.t[tuple(slice(None) for _ in t_.t.shape)], D[nm], writes=[t_.b])
        si = 0
        for t_, nm, w_ in ((ident4, "ident4", 512), (CBT, "CBT", 128), (WBT, "WBT", 128)):
            s = xt[si % 2]; si += 1
            P.dma(s[:, 0:w_], D[nm], writes=[s.b])
            k.cp(t_[:, :], s[:, 0:w_], [s.b], [t_.b])
        for kc in range(8):
            for c3 in range(3):
                s = xt[si % 2]; si += 1
                P.dma(s[:, 0:864], D["w_in"][kc * 128:(kc + 1) * 128, c3 * 864:(c3 + 1) * 864], writes=[s.b])
                k.cp(win[:, kc, c3 * 864:(c3 + 1) * 864], s[:, 0:864], [s.b], [win.b])
        k.memset("pool", NT2[:, :], 0.0, [NT2.b])
        k.act(negA[:, :], alog[:, :], AF.Exp, [alog.b], [negA.b])
        k.ts("dve", negA[:, :], negA[:, :], -1.0, None, ALU.mult, None, [negA.b], [negA.b])


        def gdn_stream(h, pb, par):
            og = ogs[par]; ogb = ogbs[par]; ms = mss[par]; msb = msbs[par]
            DecS, DecTi, Lb, TT_, QKdT, vtok, kd, t1, t2, vnb = HB[h]
            Gm = pb.Gm[h]
            bd = nb()
            k.mm(bd, bd[:, 0:128], Gm[:, :], NegU[:, :], [NegU.b, Gm.b])
            k.mm(bd, bd[:, 128:256], pb.knb[:, h, :], pb.knb[:, h, :], [pb.knb.b])
            k.mm(bd, bd[:, 256:384], pb.knb[:, h, :], pb.qnb[:, h, :], [pb.knb.b, pb.qnb.b])
            k.tt("dve", DecS[:, :], bd[:, 0:128], M1[:, :], ALU.add, [bd.b, M1.b], [DecS.b])
            k.stt("dve", DecTi[:, :], bd[:, 0:128], -1.0, M2[:, :], ALU.mult, ALU.add, [bd.b, M2.b], [DecTi.b])
            k.act(DecS[:, :], DecS[:, :], AF.Exp, [DecS.b, pb.gcp.b], [DecS.b], bias=pb.gcp[:, h:h + 1])
            k.act(DecTi[:, :], DecTi[:, :], AF.Exp, [DecTi.b, pb.gcp.b], [DecTi.b], bias=pb.gcp[:, 4 + h:5 + h])
            k.stt("dve", Lb[0][:, 0, :], bd[:, 128:256], pb.beta[:, h:h + 1], DecS[:, :], ALU.mult, ALU.mult,
                  [bd.b, pb.beta.b, DecS.b], [Lb[0].b])
            k.tt("dve", QKdT[:, :], bd[:, 256:384], DecTi[:, :], ALU.mult, [bd.b, DecTi.b], [QKdT.b])
            yield
            bt = nb()
            k.tr(bt, bt[:, 0:128], Lb[0][:, 0, :], ident[:, :], [Lb[0].b, ident.b])
            k.cp(Lb[0][:, 1, :], bt[:, 0:128], [bt.b], [Lb[0].b], eng="act")
            k.stt("dve", Lb[1][:, 2, :], bt[:, 0:128], -1.0, ident[:, :], ALU.mult, ALU.add, [bt.b, ident.b], [Lb[1].b])
            yield
            for lvl in range(1, 8):
                cur = Lb[(lvl - 1) % 2]
                nxt = Lb[lvl % 2]
                bk = nb()
                if lvl <= 6:
                    k.mm(bk, bk[:, 0:128], cur[:, 1, :], cur[:, 0, :], [cur.b])
                    k.mm(bk, bk[:, 128:256], cur[:, 0, :], cur[:, 1, :], [cur.b])
                if lvl >= 2:
                    k.mm(bk, bk[:, 256:384], cur[:, 0, :], cur[:, 2, :], [cur.b])
                if lvl <= 6:
                    k.cp(nxt[:, 0:2, :], bk[:, 0:256].rearrange("p (a c) -> p a c", a=2), [bk.b], [nxt.b], eng="act")
                if lvl >= 2:
                    dst = nxt[:, 2, :] if lvl <= 6 else TT_[:, :]
                    dstb = nxt.b if lvl <= 6 else TT_.b
                    k.tt("dve", dst, bk[:, 256:384], cur[:, 2, :], ALU.add, [bk.b, cur.b], [dstb])
                yield
            bq = nb()
            k.mm(bq, bq[:, 0:128], pb.knb[:, h, :], Sb[h][:, :], [pb.knb.b, Sb[h].b])
            k.mm(bq, bq[:, 128:256], pb.qnb[:, h, :], Sb[h][:, :], [pb.qnb.b, Sb[h].b])
            k.tr(bq, bq[:, 256:384], pb.silv[:, h, :], ident[:, :], [pb.silv.b, ident.b])
            k.tr(bq, bq[:, 384:512], pb.kn[:, h, :], ident[:, :], [pb.kn.b, ident.b])
            k.cp(vtok[:, :], bq[:, 256:384], [bq.b], [vtok.b], eng="act")
            k.ts("dve", kd[:, :], bq[:, 384:512], pb.eglm[:, h:h + 1], None, ALU.mult, None, [bq.b, pb.eglm.b], [kd.b])
            k.stt("dve", t1[:, :], bq[:, 0:128], pb.egc[:, h:h + 1], vtok[:, :], ALU.mult, ALU.subtract,
                  [bq.b, pb.egc.b, vtok.b], [t1.b])
            k.ts("dve", t1[:, :], t1[:, :], pb.nbeta[:, h:h + 1], None, ALU.mult, None, [t1.b, pb.nbeta.b], [t1.b])
            k.ts("dve", t2[:, :], bq[:, 128:256], pb.egc[:, h:h + 1], None, ALU.mult, None, [bq.b, pb.egc.b], [t2.b])
            yield
            bv = nb()
            k.mm(bv, bv[:, 0:128], TT_[:, :], t1[:, :], [TT_.b, t1.b])
            k.cp(vnb[:, :], bv[:, 0:128], [bv.b], [vnb.b], eng="act")
            yield
            bw = nb()
            k.mm(bw, bw[:, 128:256], QKdT[:, :], vnb[:, :], [QKdT.b, vnb.b])
            k.mm(bw, bw[:, 256:384], kd[:, :], vnb[:, :], [kd.b, vnb.b])
            k.tt("dve", og[:, h, :], bw[:, 128:256], t2[:, :], ALU.add, [bw.b, t2.b], [ogb[h]])
            k.stt("dve", St[h][:, :], St[h][:, :], pb.egl[:, h:h + 1], bw[:, 256:384], ALU.mult, ALU.add,
                  [St[h].b, pb.egl.b, bw.b], [St[h].b])
            k.cp(Sb[h][:, :], St[h][:, :], [St[h].b], [Sb[h].b], eng="pool")
            k.act(junk[h][:, :], og[:, h, :], AF.Square, [ogb[h]], [junk[h].b, msb[h]], accum=ms[:, h:h + 1])
            yield

        def nsa_stream(ti, pb, par):
            omix = omixs[par]
            nchunks = 1 if ti < 16 else 2
            pi = 0
            k0 = max(0, ti - 4)
            for hk in range(2):
                hs = slice(hk * 64, (hk + 1) * 64)
                qrhs = pb.nqT[hs, :, :].rearrange("p g q -> p (g q)")
                bCa = nb()
                bCb = nb()
                for nch in range(nchunks):
                    bk = nb()
                    k.mm(bk, bk[:, :], kcmp[hs, nch * 128:(nch + 1) * 128], qrhs, [kcmp.b, pb.nqT.b], start=True, stop=False)
                    k.mm(bk, bk[:, :], pb.cmbtb[:, nch, :], ident4[:, :], [pb.cmbtb.b, ident4.b], start=False, stop=True)
                    p_ = pT[pi % NPT]; pi += 1
                    k.act(p_[:, :], bk[:, :], AF.Exp, [bk.b], [p_.b], scale=0.125)
                    for g in range(4):
                        k.mm(bCa, bCa[:, g * 65:(g + 1) * 65], p_[:, g * 128:(g + 1) * 128], vcmp[:, nch, hk, 0:65],
                             [p_.b, vcmp.b], start=(nch == 0 and g == 0), stop=(nch == nchunks - 1), skip=True)
                    for g in range(4):
                        k.mm(bCb, bCb[:, g * 64:(g + 1) * 64], p_[:, g * 128:(g + 1) * 128], vcmp[:, nch, hk, 65:129],
                             [p_.b, vcmp.b], start=(nch == 0 and g == 0), stop=(nch == nchunks - 1), skip=True)
                cs = casb[hk]
                k.cp(cs[:, :], bCa[:, 0:260], [bCa.b], [cs.b], eng="act")
                k.ts("dve", rl[:, :], cs[:, 64:260:65], 1e-30, None, ALU.max, None, [cs.b], [rl.b])
                P.op("dve", lambda: nc.vector.reciprocal(out=rl[:, :], in_=rl[:, :]), [rl.b], [rl.b])
                k.ts("dve", imp[:, :], bCb[:, 0:64], rl[:, 0:1], None, ALU.mult, None, [bCb.b, rl.b], [imp.b])
                for g in range(1, 4):
                    k.stt("dve", imp[:, :], bCb[:, g * 64:(g + 1) * 64], rl[:, g:g + 1], imp[:, :], ALU.mult, ALU.add,
                          [bCb.b, rl.b, imp.b], [imp.b])
                yield
                k.tt("dve", imp[:, :], imp[:, :], pb.kf[:, 0, :], ALU.mult, [imp.b, pb.kf.b], [imp.b])
                k.tt("dve", imp[:, :], imp[:, :], pb.kf[:, 1, :], ALU.add, [imp.b, pb.kf.b], [imp.b])
                P.op("dve", lambda: nc.vector.max(out=m8a[:, :], in_=imp[:, :]), [imp.b], [m8a.b])
                P.op("dve", lambda: nc.vector.match_replace(out=impt[:, :], in_to_replace=m8a[:, :], in_values=imp[:, :],
                                                            imm_value=-1e9), [imp.b, m8a.b], [impt.b])
                P.op("dve", lambda: nc.vector.max(out=m8b[:, :], in_=impt[:, :]), [impt.b], [m8b.b])
                k.ts("dve", NT2[:, 64:128], imp[:, :], m8b[:, 7:8], -30000.0, ALU.is_lt, ALU.mult, [imp.b, m8b.b], [NT2.b])
                bt = nb()
                k.tr(bt, bt[:, 0:128], NT2[:, :], ident[:, :], [NT2.b, ident.b])
                k.cp(pb.QN[hk][64:128, :].rearrange("p (g q) -> p g q", g=4),
                     bt[64:128, 0:128].rearrange("p (o q) -> p o q", o=1).to_broadcast([64, 4, 128]), [bt.b], [pb.QN[hk].b], eng="act")
                yield
            for hk in range(2):
                hs = slice(hk * 64, (hk + 1) * 64)
                qrhs = pb.nqT[hs, :, :].rearrange("p g q -> p (g q)")
                pend = []
                for kc in range(k0, ti + 1):
                    bk = nb()
                    last_extra = (kc == ti) or (kc == ti - 4)
                    k.mm(bk, bk[:, :], pb.kwin[hs, (kc - k0) * 128:(kc - k0 + 1) * 128], qrhs, [pb.kwin.b, pb.nqT.b],
                         start=True, stop=not last_extra)
                    if kc == ti:
                        k.mm(bk, bk[:, :], CBT[:, :], ident4[:, :], [CBT.b, ident4.b], start=False, stop=True)
                    elif kc == ti - 4:
                        k.mm(bk, bk[:, :], WBT[:, :], ident4[:, :], [WBT.b, ident4.b], start=False, stop=True)
                    p_ = pT[pi % NPT]; pi += 1
                    k.act(p_[:, :], bk[:, :], AF.Exp, [bk.b], [p_.b], scale=0.125)
                    if len(pend) >= SKEW:
                        pend.pop(0)()
                    def pv(p_=p_, kc=kc, hk=hk):
                        for g in range(4):
                            k.mm(bwin, bwin[:, g * 65:(g + 1) * 65], p_[:, g * 128:(g + 1) * 128], pb.vwin[:, kc - k0, hk, :],
                                 [p_.b, pb.vwin.b], start=(kc == k0 and g == 0), stop=(kc == ti), skip=True)
                    pend.append(pv)
                    yield
                while pend:
                    pend.pop(0)()
                k.cp(wsb[hk][:, :], bwin[:, 0:260], [bwin.b], [wsb[hk].b], eng="dve")
                yield
            for hk in range(2):
                bsel = bselAB[hk]
                pend = []
                for kc in range(ti + 1):
                    bk = nb()
                    k.mm(bk, bk[:, :], ksE[hk][:, kc * 128:(kc + 1) * 128], pb.QN[hk][:, :], [ksE[hk].b, pb.QN[hk].b],
                         start=True, stop=(kc != ti))
                    if kc == ti:
                        k.mm(bk, bk[:, :], CBT[:, :], ident4[:, :], [CBT.b, ident4.b], start=False, stop=True)
                    p_ = pT[pi % NPT]; pi += 1
                    k.act(p_[:, :], bk[:, :], AF.Exp, [bk.b], [p_.b], scale=0.125)
                    if len(pend) >= SKEW:
                        pend.pop(0)()
                    def pv(p_=p_, kc=kc, hk=hk, bsel=bsel):
                        for g in range(4):
                            k.mm(bsel, bsel[:, g * 65:(g + 1) * 65], p_[:, g * 128:(g + 1) * 128], vs[:, kc, hk, :],
                                 [p_.b, vs.b], start=(kc == 0 and g == 0), stop=(kc == ti), skip=True)
                    pend.append(pv)
                    yield
                while pend:
                    pend.pop(0)()
                cs = casb[hk]; ws = wsb[hk]
                k.cp(Lg[:, :, 0], cs[:, 64:260:65], [cs.b], [Lg.b], eng="dve")
                k.cp(Lg[:, :, 1], bsel[:, 64:260:65], [bsel.b], [Lg.b], eng="dve")
                k.cp(Lg[:, :, 2], ws[:, 64:260:65], [ws.b], [Lg.b], eng="dve")
                k.ts("dve", coef[:, :, :], Lg[:, :, :], 1e-30, None, ALU.max, None, [Lg.b], [coef.b])
                P.op("dve", lambda: nc.vector.reciprocal(out=coef[:, :, :], in_=coef[:, :, :]), [coef.b], [coef.b])
                k.tt("dve", coef[:, :, :], coef[:, :, :], pb.sg[:, hk * 12:(hk + 1) * 12].rearrange("p (g b) -> p g b", b=3),
                     ALU.mult, [coef.b, pb.sg.b], [coef.b])
                for g in range(4):
                    col = 512 + (hk * 4 + g) * 64
                    k.ts("dve", tn[:, :], cs[:, g * 65:g * 65 + 64], coef[:, g, 0:1], None, ALU.mult, None, [cs.b, coef.b], [tn.b])
                    k.stt("dve", tn[:, :], ws[:, g * 65:g * 65 + 64], coef[:, g, 2:3], tn[:, :], ALU.mult, ALU.add,
                          [ws.b, coef.b, tn.b], [tn.b])
                    k.stt("dve", omix[:, col:col + 64], bsel[:, g * 65:g * 65 + 64], coef[:, g, 1:2], tn[:, :], ALU.mult, ALU.add,
                          [bsel.b, coef.b, tn.b], [omix.b])
                yield

        def prologue(sq_i, ti):
            pb = PB[ti % 2]
            x = xt[ti % 2]
            t0 = sq_i * T_ + ti * 128
            P.dma(x[:, :], D["x1"][t0:t0 + 128, :], reads=[D["x1_b"]], writes=[x.b])
            P.dma(pb.kf[:, :, :], D["KF"][ti], writes=[pb.kf.b])
            P.dma(cmbt[:, :, :], D["CMBT"][ti], writes=[cmbt.b])
            k0 = max(0, ti - 4)
            nk = ti + 1 - k0
            P.dma(pb.kwin[:, 0:nk * 128], D["kwT"][sq_i][:, k0 * 128:(ti + 1) * 128], reads=[D["state_b"]], writes=[pb.kwin.b])
            P.dma(pb.vwin[:, 0:nk, :, :], D["vw"][sq_i][:, k0:ti + 1, :, :], reads=[D["state_b"]], writes=[pb.vwin.b])
            k.cp(pb.cmbtb[:, :, :], cmbt[:, :, :], [cmbt.b], [pb.cmbtb.b], eng="pool")
            yield
            yield
            yield
            for q in range(2):
                bk = nb()
                for j in range(4):
                    kc = q * 4 + j
                    k.tr(bk, bk[:, j * 128:(j + 1) * 128], x[:, kc * 128:(kc + 1) * 128], ident[:, :], [x.b, ident.b])
                k.cp(xT[:, q * 4:(q + 1) * 4, :], bk[:, :].rearrange("p (j c) -> p j c", j=4), [bk.b], [xT.b])
                yield
            for q in range(4):
                bk = nb()
                for j in range(4):
                    c = q * 4 + j
                    for kc in range(8):
                        k.mm(bk, bk[:, j * 128:(j + 1) * 128], win[:, kc, c * 128:(c + 1) * 128], xT[:, kc, :],
                             [win.b, xT.b], start=(kc == 0), stop=(kc == 7))
                if q < 3:
                    k.cp(raw[:, q * 4:(q + 1) * 4, 3:131], bk[:, :].rearrange("p (j c) -> p j c", j=4), [bk.b], [raw.b])
                else:
                    k.cp(pb.nqT[:, :, :], bk[:, :].rearrange("p (j c) -> p j c", j=4), [bk.b], [pb.nqT.b])
                    k.cp(pb.QN[0][0:64, :], bk[0:64, :], [bk.b], [pb.QN[0].b])
                yield
            bk = nb()
            for g in range(4):
                for kc in range(8):
                    k.mm(bk, bk[0:64, g * 128:(g + 1) * 128], win[:, kc, 1536 + g * 128 + 64:1536 + (g + 1) * 128], xT[:, kc, :],
                         [win.b, xT.b], start=(kc == 0), stop=(kc == 7))
            k.cp(pb.QN[1][0:64, :], bk[0:64, :], [bk.b], [pb.QN[1].b])
            yield
            bz = nb()
            for kc in range(8):
                k.mm(bz, bz[:, :], xT[:, kc, :], win[:, kc, TM_Z:TM_Z + 512], [win.b, xT.b], start=(kc == 0), stop=(kc == 7))
            k.cp(pb.zt[:, :], bz[:, :], [bz.b], [pb.zt.b], eng="act")
            bs_ = nb()
            for kc in range(8):
                k.mm(bs_, bs_[:, 0:32], xT[:, kc, :], win[:, kc, TM_S:TM_S + 32], [win.b, xT.b], start=(kc == 0), stop=(kc == 7))
            k.cp(sm[:, :], bs_[:, 0:32], [bs_.b], [sm.b], eng="dve")
            yield
            def cdst(c):
                return (sil[:, c, :], silb[c]) if c < 8 else (pb.silv[:, c - 8, :], pb.silvb[c - 8])
            for kk in (3, 2, 1, 0):
                for half in range(2):
                    for c in range(half * 6, half * 6 + 6):
                        a_, ab = cdst(c)
                        if kk == 3:
                            k.ts("dve", a_, raw[:, c, 3:131], convw[:, c, 3:4], None, ALU.mult, None, [raw.b, convw.b],
                                 [ab] if c < 8 else [ab, pb.silv.b])
                        else:
                            k.stt("dve", a_, raw[:, c, kk:kk + 128], convw[:, c, kk:kk + 1], a_, ALU.mult, ALU.add,
                                  [raw.b, convw.b, ab], [ab])
                    yield
            k.cp(raw[:, :, 0:3], raw[:, :, 128:131], [raw.b], [raw.b], eng="pool")
            yield
            yield
            k.act(sil[:, :, :], sil[:, :, :], AF.Silu, silb, silb)
            k.act(pb.silv[:, :, :], pb.silv[:, :, :], AF.Silu, pb.silvb, pb.silvb + [pb.silv.b])
            k.act(pb.zt[:, :], pb.zt[:, :], AF.Silu, [pb.zt.b], [pb.zt.b])
            yield
            yield
            yield
            k.tt("pool", pb.zt[:, :], pb.zt[:, :], normw[:, :], ALU.mult, [pb.zt.b, normw.b], [pb.zt.b])
            k.tt("dve", sq[0][:, :], sil[:, 0, :], sil[:, 0, :], ALU.mult, [silb[0]], [sq[0].b])
            yield
            for c in range(8):
                h = c % 4
                s_ = sq[c % 2]; r_ = rn[c % 2]
                bk = nb()
                k.mm(bk, bk[:, 0:128], ones[:, :], s_[:, :], [ones.b, s_.b])
                k.act(r_[:, :], bk[:, 0:128], AF.Ln, [bk.b], [r_.b], bias=1e-6)
                k.act(r_[:, :], r_[:, :], AF.Exp, [r_.b], [r_.b], scale=-0.5, bias=(-0.5 * float(np.log(128.0)) if c < 4 else 0.0))
                if c + 1 < 8:
                    k.tt("dve", sq[(c + 1) % 2][:, :], sil[:, c + 1, :], sil[:, c + 1, :], ALU.mult, [silb[c + 1]], [sq[(c + 1) % 2].b])
                yield
                if c < 4:
                    k.tt("dve", pb.qnb[:, h, :], sil[:, c, :], r_[:, :], ALU.mult, [silb[c], r_.b], [pb.qnb.b])
                else:
                    k.tt("dve", pb.kn[:, h, :], sil[:, c, :], r_[:, :], ALU.mult, [silb[c], r_.b], [pb.kn.b])
                    k.cp(pb.knb[:, h, :], pb.kn[:, h, :], [pb.kn.b], [pb.knb.b], eng="pool")
            yield
            k.act(smt[:, 0:4], sm[:, 0:4], AF.Exp, [sm.b], [smt.b], scale=-1.0)
            k.act(smt[:, 8:32], sm[:, 8:32], AF.Exp, [sm.b], [smt.b], scale=-1.0)
            k.tt("dve", smt[:, 4:8], sm[:, 4:8], dtb[:, :], ALU.add, [sm.b, dtb.b], [smt.b])
            k.act(smt[:, 4:8], smt[:, 4:8], AF.Exp, [smt.b], [smt.b])
            k.ts("dve", smt[:, :], smt[:, :], 1.0, None, ALU.add, None, [smt.b], [smt.b])
            k.act(pb.g_[:, :], smt[:, 4:8], AF.Ln, [smt.b], [pb.g_.b])
            k.tt("dve", pb.g_[:, :], pb.g_[:, :], negA[:, :], ALU.mult, [pb.g_.b, negA.b], [pb.g_.b])
            P.op("dve", lambda: nc.vector.reciprocal(out=pb.beta[:, :], in_=smt[:, 0:4]), [smt.b], [pb.beta.b])
            k.ts("dve", pb.nbeta[:, :], pb.beta[:, :], -1.0, None, ALU.mult, None, [pb.beta.b], [pb.nbeta.b])
            P.op("dve", lambda: nc.vector.reciprocal(out=pb.sg[:, :], in_=smt[:, 8:32]), [smt.b], [pb.sg.b])
            yield
            bk = nb()
            k.mm(bk, bk[:, 0:4], U[:, :], pb.g_[:, :], [U.b, pb.g_.b])
            k.mm(bk, bk[:, 4:8], ones[:, :], pb.g_[:, :], [ones.b, pb.g_.b])
            k.cp(gcs[:, :], bk[:, 0:8], [bk.b], [gcs.b], eng="dve")
            k.cp(pb.gcp[:, 0:4], gcs[:, 0:4], [gcs.b], [pb.gcp.b], eng="dve")
            k.ts("dve", pb.gcp[:, 4:8], gcs[:, 0:4], -1.0, None, ALU.mult, None, [gcs.b], [pb.gcp.b])
            k.act(pb.egc[:, :], gcs[:, 0:4], AF.Exp, [gcs.b], [pb.egc.b])
            k.act(pb.egl[:, :], gcs[:, 4:8], AF.Exp, [gcs.b], [pb.egl.b])
            k.tt("dve", pb.eglm[:, :], gcs[:, 4:8], gcs[:, 0:4], ALU.subtract, [gcs.b], [pb.eglm.b])
            k.act(pb.eglm[:, :], pb.eglm[:, :], AF.Exp, [pb.eglm.b], [pb.eglm.b])
            for h in range(4):
                k.ts("dve", pb.Gm[h][:, :], ones[:, :], pb.g_[:, h:h + 1], None, ALU.mult, None, [ones.b, pb.g_.b], [pb.Gm[h].b])
            yield

        PRO_W = int(_os.environ.get("PRO_W", "1"))

        def run_all(gens, weights=None, periods=None):
            gens = list(gens)
            weights = dict(weights or {})
            periods = dict(periods or {})
            r = 0
            while gens:
                long_alive = any(id(g) not in periods for g in gens)
                for s_ in list(gens):
                    per, ph = periods.get(id(s_), (1, 0))
                    if long_alive and per > 1 and r % per != ph % per:
                        continue
                    for _ in range(weights.get(id(s_), 1)):
                        try:
                            next(s_)
                        except StopIteration:
                            gens.remove(s_)
                            break
                r += 1

        GDN_SPREAD = int(_os.environ.get("GDN_SPREAD", "1"))

        for sq_i in range(nseq):
            sb = D["state_b"]
            P.dma(ksE[0][0:64, :], D["ksT"][sq_i][0:64, :], reads=[sb], writes=[ksE[0].b])
            P.dma(ksE[1][0:64, :], D["ks1T"][sq_i], reads=[sb], writes=[ksE[1].b])
            P.dma(vs[:, :, :, :], D["vs"][sq_i], reads=[sb], writes=[vs.b])
            P.dma(kcmp[:, :], D["kcmp"][sq_i], reads=[sb], writes=[kcmp.b])
            P.dma(vcmp[:, :, :, :], D["vcmp"][sq_i], reads=[sb], writes=[vcmp.b])
            k.memset("pool", raw[:, :, 0:3], 0.0, [raw.b])
            for h in range(4):
                k.memset("pool", St[h][:, :], 0.0, [St[h].b])
                k.memset("pool", Sb[h][:, :], 0.0, [Sb[h].b])
            run_all([prologue(sq_i, 0)])
            if sq_i == 0:
                for kc in range(8):
                    s = xt[1]
                    P.dma(s[:, :], D["w_out"][kc * 128:(kc + 1) * 128, :], writes=[s.b])
                    k.cp(wout[:, kc, :], s[:, :], [s.b], [wout.b])
                for c4 in range(T_ // 1024 if T_ >= 1024 else 1):
                    w_ = min(1024, T_)
                    s = xt[1]
                    P.dma(s[64:128, 0:w_], D["Econst"][:, c4 * w_:(c4 + 1) * w_], writes=[s.b])
                    for i in range(2):
                        k.cp(ksE[i][64:128, c4 * w_:(c4 + 1) * w_], s[64:128, 0:w_], [s.b], [ksE[i].b])

            def epilogue(ti):
                par = ti % 2
                pb = PB[par]; x = xt[par]; omix = omixs[par]; og = ogs[par]; ogb = ogbs[par]; ms = mss[par]; msb = msbs[par]
                t0 = sq_i * T_ + ti * 128
                k.ts("dve", y[:, :], x[:, :], DN_ALPHA, None, ALU.mult, None, [x.b], [y.b])
                k.act(rstd[:, :], ms[:, :], AF.Ln, msb, [rstd.b], scale=1.0 / 128.0, bias=1e-6)
                k.act(rstd[:, :], rstd[:, :], AF.Exp, [rstd.b], [rstd.b], scale=-0.5)
                yield
                for h in range(4):
                    k.stt("dve", omix[:, h * 128:(h + 1) * 128], og[:, h, :], rstd[:, h:h + 1], pb.zt[:, h * 128:(h + 1) * 128],
                          ALU.mult, ALU.mult, [ogb[h], rstd.b, pb.zt.b], [omix.b])
                if dbg:
                    P.dma(D["dbg"][t0:t0 + 128, :], omix[:, :], reads=[omix.b], writes=[D["dbg_b"]])
                yield
                for q in range(2):
                    bk = nb()
                    for j in range(4):
                        cc = q * 4 + j
                        k.tr(bk, bk[:, j * 128:(j + 1) * 128], omix[:, cc * 128:(cc + 1) * 128], ident[:, :], [omix.b, ident.b])
                    k.cp(omixT[:, q * 4:(q + 1) * 4, :], bk[:, :].rearrange("p (j c) -> p j c", j=4), [bk.b], [omixT.b])
                    yield
                for dh in range(2):
                    bk = nb()
                    for cc in range(8):
                        k.mm(bk, bk[:, :], omixT[:, cc, :], wout[:, cc, dh * 512:(dh + 1) * 512], [omixT.b, wout.b],
                             start=(cc == 0), stop=(cc == 7))
                    k.tt("dve", y[:, dh * 512:(dh + 1) * 512], bk[:, :], y[:, dh * 512:(dh + 1) * 512], ALU.add, [y.b, bk.b], [y.b])
                    yield
                layer_norm(P, [y], st, mv, rs, None, None, lnexp=True, affine=False)
                P.dma(D["x2"][t0:t0 + 128, :], y[:, :], reads=[y.b], writes=[D["x2_b"]])
                yield

            for ti in range(NT):
                pb = PB[ti % 2]
                gens = []
                if ti > 0:
                    gens.append(epilogue(ti - 1))
                gg = [gdn_stream(h, pb, ti % 2) for h in range(4)]
                gens += gg + [nsa_stream(ti, pb, ti % 2)]
                wts = {}
                pers = {}
                if GDN_SPREAD:
                    n_rounds = 2 * (ti + 1) + 2 * min(5, ti + 1) + 8
                    per = max(1, min(int(_os.environ.get('GDN_MAXP', '3')), n_rounds // int(_os.environ.get('GDN_DIV', '15'))))
                    for h, g in enumerate(gg):
                        pers[id(g)] = (per, h)
                if ti + 1 < NT:
                    pg = prologue(sq_i, ti + 1)
                    gens.append(pg)
                    wts[id(pg)] = PRO_W
                run_all(gens, wts, pers)
            run_all([epilogue(NT - 1)])
    P.barrier()


def _w_in_perm():
    o = {}
    off = 0
    for nm, sz in (("gq", 512), ("gk", 512), ("gv", 512), ("gz", 512), ("gb", 4), ("ga", 4), ("nq", 512), ("kc", 128),
                   ("vc", 128), ("ks", 128), ("vs", 128), ("kw", 128), ("vw", 128), ("gates", 24)):
        o[nm] = np.arange(off, off + sz)
        off += sz
    nq = o["nq"].reshape(2, 4, 64).transpose(1, 0, 2).reshape(-1)
    return np.concatenate([o["gq"], o["gk"], o["gv"], nq, o["gz"], o["gb"], o["ga"], o["gates"],
                           o["kc"], o["vc"], o["ks"], o["kw"], o["vs"], o["vw"]])


def host_consts(T_):
    NT = T_ // 128
    NC = (T_ - 32) // 16 + 1
    f = np.float32
    p = np.arange(128)[:, None]
    q = np.arange(128)[None, :]
    c = {}
    c["ident"] = np.eye(128, dtype=f)
    c["U"] = (p <= q).astype(f)
    c["NegU"] = -c["U"]
    c["ones"] = np.ones((128, 128), f)
    c["M1"] = np.where(q < p, 0.0, -1e30).astype(f)
    c["M2"] = np.where(p <= q, 0.0, -1e30).astype(f)
    c["ident4"] = np.tile(np.eye(128, dtype=f), (1, 4))
    c["CBT"] = np.where(q <= p, 0.0, -30000.0).astype(f)
    c["WBT"] = np.where(q > p, 0.0, -30000.0).astype(f)
    cs = np.arange(256) * 16
    ss = np.arange(64) * 64
    ov = np.clip(np.minimum(cs[:, None] + 32, ss[None, :] + 64) - np.maximum(cs[:, None], ss[None, :]), 0, None) / 32.0
    ov[NC:] = 0.0
    c["c2s"] = np.ascontiguousarray(ov.reshape(2, 128, 64).transpose(1, 0, 2)).astype(f)
    KF = np.zeros((NT, 128, 2, 64), f)
    CM = np.zeros((NT, 128, 2, 128), f)
    j = np.arange(64)[None, :]
    for ti in range(NT):
        t = ti * 128 + np.arange(128)[:, None]
        cur = t // 64
        visible = j <= cur
        forced = (j == 0) | (j == cur) | (j == cur - 1)
        KF[ti, :, 0, :] = (visible & ~forced)
        KF[ti, :, 1, :] = np.where(visible, np.where(forced, 1e4, 0.0), -1.0)
        n = np.arange(256)[None, :]
        valid = (16 * n + 31) <= t
        CM[ti] = np.where(valid, 0.0, -30000.0).reshape(128, 2, 128)
    c["KF"] = KF
    c["CMBT"] = CM
    c["Econst"] = (np.arange(T_)[None, :] // 64 == np.arange(64)[:, None]).astype(f)
    return c


def host_prep(inp, T_):
    f = np.float32
    d = dict(host_consts(T_))
    bc = lambda v, n=128: np.ascontiguousarray(np.broadcast_to(np.asarray(v, f).reshape(1, -1), (n, np.asarray(v).size)))
    d["w_in"] = np.ascontiguousarray(inp["w_in"][0][:, _w_in_perm()])
    d["w_out"] = np.ascontiguousarray(inp["w_out"][0])
    d["convw"] = np.ascontiguousarray(inp["gdn_conv_w"][0].reshape(4, 12, 128).transpose(2, 1, 0))
    d["alog"] = bc(inp["gdn_a_log"][0])
    d["dtb"] = bc(inp["gdn_dt_bias"][0])
    d["normw"] = bc(np.tile(inp["gdn_norm_w"][0], 4))
    d["ln1g"] = bc(inp["ln1_g"][0]); d["ln1b"] = bc(inp["ln1_b"][0])
    d["ln2g"] = bc(inp["ln2_g"][0]); d["ln2b"] = bc(inp["ln2_b"][0])
    d["ln3g"] = bc(inp["ln3_g"][0]); d["ln3b"] = bc(inp["ln3_b"][0])
    cw1 = []
    for nm in ("nsa_cmp_k_w1", "nsa_cmp_v_w1"):
        w = inp[nm][0].reshape(32, 64, 256).transpose(1, 0, 2)
        cw1.append(np.concatenate([w, w], 0))
    d["cw1"] = np.ascontiguousarray(np.stack(cw1))
    w2k = inp["nsa_cmp_k_w2"][0].reshape(2, 128, 64).transpose(1, 0, 2)
    d["cw2k"] = np.ascontiguousarray(np.concatenate([w2k, w2k], -1))
    d["cw2v"] = np.ascontiguousarray(inp["nsa_cmp_v_w2"][0].reshape(2, 128, 64).transpose(1, 0, 2))
    d["posT"] = np.ascontiguousarray(np.stack([np.concatenate([inp[nm][0].T] * 2, 0)
                                               for nm in ("nsa_cmp_pos_k", "nsa_cmp_pos_v")]))
    d["f1wg"] = np.ascontiguousarray(inp["ffn1_wg"][0])
    d["f1wu"] = np.ascontiguousarray(inp["ffn1_wu"][0])
    d["f1wd"] = np.ascontiguousarray(inp["ffn1_wd"][0])
    d["f2wg"] = np.ascontiguousarray(inp["ffn2_wg"][0])
    d["f2wu"] = np.ascontiguousarray(inp["ffn2_wu"][0])
    d["f2wd"] = np.ascontiguousarray(inp["ffn2_wd"][0])
    return d


CONST_SHAPES = None


def build_program(nseq, T_, phases=("ffn1", "b1", "b2", "ffn2"), dbg=False):
    nc = bass.Bass("TRN2", target_bir_lowering=False)
    ntok = nseq * T_
    NT = T_ // 128
    shp = {k_: v.shape for k_, v in host_consts(T_).items()}
    shp.update({"w_in": (D_MODEL, N_IN), "w_out": (D_MODEL, D_MODEL), "convw": (128, 12, 4), "alog": (128, 4), "dtb": (128, 4),
                "normw": (128, 512), "cw1": (2, 128, 32, 256), "cw2k": (128, 2, 128), "cw2v": (128, 2, 64), "posT": (2, 128, 32)})
    for i in (1, 2, 3):
        shp["ln%dg" % i] = (128, D_MODEL)
        shp["ln%db" % i] = (128, D_MODEL)
    for i in (1, 2):
        shp["f%dwg" % i] = (D_MODEL, D_FF)
        shp["f%dwu" % i] = (D_MODEL, D_FF)
        shp["f%dwd" % i] = (D_FF, D_MODEL)
    D = {}
    for nm, s in shp.items():
        D[nm] = nc.dram_tensor(nm, list(s), F32, kind="ExternalInput").ap()

    def act_tensor(nm, first_in, last_out):
        kind = "ExternalInput" if first_in else ("ExternalOutput" if last_out else "Internal")
        D[nm] = nc.dram_tensor(nm, [ntok, D_MODEL], F32, kind=kind).ap()
        D[nm + "_b"] = Buf(nm)
    act_tensor("x", "ffn1" in phases, False)
    act_tensor("x1", "ffn1" not in phases, phases[-1] == "ffn1")
    act_tensor("x2", False, phases[-1] == "b2")
    act_tensor("out", False, phases[-1] == "ffn2")
    for nm, s in (("ksT", [nseq, 128, T_]), ("ks1T", [nseq, 64, T_]), ("kwT", [nseq, 128, T_]), ("vs", [nseq, 128, NT, 2, 65]),
                  ("vw", [nseq, 128, NT, 2, 65]), ("kcmp", [nseq, 128, 256]), ("vcmp", [nseq, 128, 2, 2, 129])):
        D[nm] = nc.dram_tensor(nm, s, BF16, kind="Internal").ap()
    D["state_b"] = Buf("state")
    if dbg:
        D["dbg"] = nc.dram_tensor("dbg", [ntok, D_MODEL], F32, kind="ExternalOutput").ap()
        D["dbg_b"] = Buf("dbg")
    if phases[-1] == "b1":
        D["dummy"] = nc.dram_tensor("dummyo", [128, 128], F32, kind="ExternalOutput").ap()
    with ExitStack() as es:
        P = Prog(nc)
        banks = [T(es, nc, "bank%d" % i, [128, 512], F32, psum=True) for i in range(8)]
        if "ffn1" in phases:
            ffn_phase(P, D["x"], D["x_b"], D["x1"], D["x1_b"], D["f1wg"], D["f1wu"], D["f1wd"], D["ln1g"], D["ln1b"],
                      D["ident"], ntok, banks, pfx="f1_")
        if "b1" in phases:
            phase_b1(P, D, banks, nseq, T_)
        if phases[-1] == "b1":
            P.dma(D["dummy"], D["ident"])
        if "b2" in phases:
            phase_b2(P, D, banks, nseq, T_, dbg=dbg)
        if "ffn2" in phases:
            ffn_phase(P, D["x2"], D["x2_b"], D["out"], D["out_b"], D["f2wg"], D["f2wu"], D["f2wd"], D["ln3g"], D["ln3b"],
                      D["ident"], ntok, banks, pfx="f2_", pre_g=D["ln2g"], pre_b=D["ln2b"])
        P.emit(es)
    return nc


_NC_CACHE = {}


def kernel(**inputs):
    inputs = {k_: np.asarray(v) for k_, v in inputs.items()}
    T_ = SEQ
    nseq = SEQ_PER_CORE
    if "nc" not in _NC_CACHE:
        _NC_CACHE["nc"] = build_program(nseq, T_)
    nc = _NC_CACHE["nc"]
    d = host_prep(inputs, T_)
    x = inputs["x"].astype(np.float32, copy=False)
    in_maps = []
    for c in range(N_CORES):
        m = dict(d)
        m["x"] = np.ascontiguousarray(x[c * nseq:(c + 1) * nseq].reshape(nseq * T_, D_MODEL))
        in_maps.append(m)
    res = run_bass_kernel_spmd(nc, in_maps, core_ids=list(range(N_CORES)))
    out = np.concatenate([np.asarray(r["out"]).reshape(nseq, T_, D_MODEL) for r in res.results], axis=0)
    return out.astype(np.float32, copy=False)
```

```python
import numpy as np
from contextlib import ExitStack
import concourse.bass as bass
import concourse.mybir as mybir
from concourse.bass_utils import run_bass_kernel_spmd

F32 = mybir.dt.float32
BF16 = mybir.dt.bfloat16
AF = mybir.ActivationFunctionType
ALU = mybir.AluOpType
AX = mybir.AxisListType

D_MODEL = 1024
D_FF = 2816
SEQ = 4096
N_CORES = 8
SEQ_PER_CORE = 2
LN_EPS = 1e-5
DN_ALPHA = 2.0 ** 0.25


class Buf:
    __slots__ = ("name", "last_w", "readers", "psum")

    def __init__(self, name, psum=False):
        self.name = name
        self.last_w = None
        self.readers = {}
        self.psum = psum


class Op:
    __slots__ = ("eng", "fn", "is_dma", "waits", "inc", "count", "sem", "val", "idx", "presem")

    def __init__(self, eng, fn, is_dma):
        self.eng = eng
        self.fn = fn
        self.is_dma = is_dma
        self.waits = []
        self.inc = False
        self.count = 0
        self.sem = None
        self.val = 0
        self.presem = None


class Prog:
    ENGS = ("pe", "dve", "act", "pool", "sp")

    def __init__(self, nc, n_dma_sems=40):
        self.nc = nc
        self.ops = []
        self.n_dma_sems = n_dma_sems
        self.dma_slot = 0
        self.dma_slot_last = [None] * n_dma_sems
        self.last_op = {e: None for e in self.ENGS}
        self.pending_dmas = []
        self.nbar = 0
        self.same_engine_raw = True

    def eng_obj(self, e):
        nc = self.nc
        return {"pe": nc.tensor, "dve": nc.vector, "act": nc.scalar, "pool": nc.gpsimd, "sp": nc.sync}[e]

    def _add_dep(self, op, dep, kind):
        if dep is None or dep is op:
            return
        if not dep.is_dma and not op.is_dma and dep.eng == op.eng:
            if op.eng == "pe" or not self.same_engine_raw:
                return
        op.waits.append(dep)

    def op(self, eng, fn, reads=(), writes=(), dma=False):
        o = Op(eng, fn, dma)
        for r in reads:
            self._add_dep(o, r.last_w, "raw")
            if r.psum:
                for e_, rd in r.readers.items():
                    if e_ != eng:
                        self._add_dep(o, rd, "rar")
        for w in writes:
            self._add_dep(o, w.last_w, "waw")
            for rd in w.readers.values():
                self._add_dep(o, rd, "war")
        for r in reads:
            key = ("dma", len(self.ops)) if dma else eng
            r.readers[key] = o
        for w in writes:
            w.last_w = o
            w.readers = {}
        if dma:
            slot = self.dma_slot
            self.dma_slot = (slot + 1) % self.n_dma_sems
            prev = self.dma_slot_last[slot]
            o.presem = prev
            o.sem = slot
            o.val = (prev.val if prev is not None else 0) + 16
            self.dma_slot_last[slot] = o
            self.pending_dmas.append(o)
        else:
            self.last_op[eng] = o
        self.ops.append(o)
        return o

    def dma(self, out, in_, reads=(), writes=(), eng="sp"):
        e = self.eng_obj(eng)
        return self.op(eng, lambda: e.dma_start(out=out, in_=in_), reads, writes, dma=True)

    def barrier(self):
        self.nbar += 1
        o = Op("all", None, False)
        o.waits = [x for x in self.last_op.values() if x is not None and x.eng != "sp"]
        o.val = list(self.pending_dmas)
        o.count = self.nbar
        self.pending_dmas = []
        self.ops.append(o)

    def emit(self, es):
        nc = self.nc
        csem = {e: es.enter_context(nc.semaphore("c_" + e)) for e in ("pe", "dve", "act", "pool")}
        dsem = [es.enter_context(nc.semaphore("d%d" % i)) for i in range(self.n_dma_sems)]
        bsem = es.enter_context(nc.semaphore("bar"))
        for o in self.ops:
            for d in o.waits:
                if not d.is_dma:
                    d.inc = True
        cnt = {e: 0 for e in csem}
        for o in self.ops:
            if o.eng != "all" and not o.is_dma and o.inc:
                cnt[o.eng] += 1
                o.count = cnt[o.eng]
        waited = {e: {} for e in self.ENGS}

        def do_wait(eng, key, sem, val):
            w = waited[eng]
            if w.get(key, 0) >= val:
                return
            w[key] = val
            self.eng_obj(eng).wait_ge(sem, val)

        for o in self.ops:
            if o.eng == "all":
                for d in o.waits:
                    for e in self.ENGS:
                        if e != d.eng:
                            do_wait(e, d.eng, csem[d.eng], d.count)
                for d in o.val:
                    do_wait("sp", ("d", d.sem), dsem[d.sem], d.val)
                nc.sync.sem_inc(bsem, 1)
                for e in ("pe", "dve", "act", "pool"):
                    do_wait(e, "bar", bsem, o.count)
                continue
            for d in o.waits:
                if d.is_dma:
                    do_wait(o.eng, ("d", d.sem), dsem[d.sem], d.val)
                else:
                    do_wait(o.eng, d.eng, csem[d.eng], d.count)
            if o.is_dma:
                if o.presem is not None:
                    do_wait(o.eng, ("d", o.sem), dsem[o.sem], o.presem.val)
                o.fn().then_inc(dsem[o.sem], 16)
            else:
                ins = o.fn()
                if o.inc:
                    ins.then_inc(csem[o.eng], 1)
        for d in self.pending_dmas:
            do_wait("sp", ("d", d.sem), dsem[d.sem], d.val)


class T:
    def __init__(self, es, nc, name, shape, dtype, psum=False):
        if psum:
            self.t = es.enter_context(nc.psum_tensor(name, shape, dtype))
        else:
            self.t = es.enter_context(nc.sbuf_tensor(name, shape, dtype))
        self.b = Buf(name, psum=psum)

    def __getitem__(self, k):
        return self.t[k]


def ffn_phase(P, x_d, x_b, out_d, out_b, wg_d, wu_d, wd_d, lng_d, lnb_d, ident_d, ntok, banks, TT=256, pfx="f1_", pre_g=None, pre_b=None):
    nc = P.nc
    KC = D_MODEL // 128
    FC = D_FF // 128
    NS = TT // 128
    ntiles = ntok // TT
    with ExitStack() as es:
        wg = T(es, nc, pfx + "wg_sb", [128, KC, D_FF], BF16)
        wu = T(es, nc, pfx + "wu_sb", [128, KC, D_FF], BF16)
        wd = T(es, nc, pfx + "wd_sb", [128, FC, D_MODEL], BF16)
        xt = [T(es, nc, pfx + "x_tok%d" % i, [128, NS * D_MODEL], F32) for i in range(2)]
        xT = T(es, nc, pfx + "xT", [128, KC, TT], BF16)
        hT = T(es, nc, pfx + "hT", [128, FC, TT], BF16)
        sl = [T(es, nc, pfx + "silu%d" % i, [128, TT], F32) for i in range(2)]
        yt = [T(es, nc, pfx + "y_tok%d" % i, [128, D_MODEL], F32) for i in range(NS)]
        lng = T(es, nc, pfx + "lng", [128, D_MODEL], F32)
        lnb = T(es, nc, pfx + "lnb", [128, D_MODEL], F32)
        ident = T(es, nc, pfx + "identf", [128, 128], F32)
        st = T(es, nc, pfx + "bnst", [128, NS, 2, 6], F32)
        mv = T(es, nc, pfx + "bnmv", [128, NS, 2], F32)
        rs = T(es, nc, pfx + "rstd", [128, NS], F32)

        P.dma(ident[:], ident_d, writes=[ident.b])
        P.dma(lng[:], lng_d, writes=[lng.b])
        P.dma(lnb[:], lnb_d, writes=[lnb.b])
        if pre_g is not None:
            pg = T(es, nc, pfx + "pre_g", [128, D_MODEL], F32)
            pbt = T(es, nc, pfx + "pre_b", [128, D_MODEL], F32)
            P.dma(pg[:], pre_g, writes=[pg.b])
            P.dma(pbt[:], pre_b, writes=[pbt.b])
        HALF = D_FF // 2
        cast_engs = ["act", "dve", "pool"]
        ci = 0

        def cast(dst, src, rb, wb):
            nonlocal ci
            e = cast_engs[ci % 3]
            ci += 1
            if e == "act":
                P.op("act", lambda: nc.scalar.copy(out=dst, in_=src), [rb], [wb])
            elif e == "dve":
                P.op("dve", lambda: nc.vector.tensor_copy(out=dst, in_=src), [rb], [wb])
            else:
                P.op("pool", lambda: nc.gpsimd.tensor_copy(out=dst, in_=src), [rb], [wb])

        si = 0
        QW = D_FF // 4

        def stage():
            nonlocal si
            t_ = xt[(si // 2) % 2]
            h_ = si % 2
            si += 1
            return t_, t_[:, h_ * D_MODEL:(h_ + 1) * D_MODEL], stg_b[(si - 1) % 4]
        stg_b = [Buf(pfx + "stg%d" % i) for i in range(4)]
        for w_d, w_sb in ((wg_d, wg), (wu_d, wu)):
            for kc in range(KC):
                for q4 in range(4):
                    t_, sv, sb_ = stage()
                    P.dma(sv[:, 0:QW], w_d[kc * 128:(kc + 1) * 128, q4 * QW:(q4 + 1) * QW], writes=[sb_])
                    cast(w_sb[:, kc, q4 * QW:(q4 + 1) * QW], sv[:, 0:QW], sb_, w_sb.b)
        for j in range(FC):
            t_, sv, sb_ = stage()
            P.dma(sv[:, 0:D_MODEL], wd_d[j * 128:(j + 1) * 128, :], writes=[sb_])
            cast(wd[:, j, :], sv[:, 0:D_MODEL], sb_, wd.b)
        bi = 0

        def nb():
            nonlocal bi
            b = banks[bi % len(banks)]
            bi += 1
            return b

        def load(ti):
            x = xt[ti % 2]
            P.dma(x[:, :].rearrange("p (s d) -> p s d", s=NS),
                  x_d[ti * TT:(ti + 1) * TT, :].rearrange("(s p) d -> p s d", p=128),
                  reads=[x_b], writes=[x.b] + (stg_b[2 * (ti % 2):2 * (ti % 2) + 2] if ti < 2 else []))
            if pre_g is not None:
                for s_i in range(NS):
                    xs = x[:, s_i * D_MODEL:(s_i + 1) * D_MODEL]
                    P.op("pool", lambda xs=xs: nc.gpsimd.tensor_tensor(out=xs, in0=xs, in1=pg[:, :], op=ALU.mult), [x.b, pg.b], [x.b])
                    P.op("pool", lambda xs=xs: nc.gpsimd.tensor_tensor(out=xs, in0=xs, in1=pbt[:, :], op=ALU.add), [x.b, pbt.b], [x.b])

        def transposes(ti):
            x = xt[ti % 2]
            for s in range(NS):
                for q in range(KC // 4):
                    bk = nb()
                    for j in range(4):
                        kc = q * 4 + j
                        P.op("pe", lambda bk=bk, j=j, s=s, kc=kc, x=x: nc.tensor.transpose(
                            out=bk[:, j * 128:(j + 1) * 128],
                            in_=x[:, s * D_MODEL + kc * 128: s * D_MODEL + (kc + 1) * 128],
                            identity=ident[:]), [x.b, ident.b], [bk.b])
                    P.op("act", lambda bk=bk, q=q, s=s: nc.scalar.copy(
                        out=xT[:, q * 4:(q + 1) * 4, s * 128:(s + 1) * 128],
                        in_=bk[:, :].rearrange("p (j c) -> p j c", j=4)), [bk.b], [xT.b])
            P.op("pool", lambda x=x: nc.gpsimd.tensor_scalar_mul(out=x[:, :], in0=x[:, :], scalar1=DN_ALPHA),
                 [x.b], [x.b])

        load(0)
        transposes(0)
        for ti in range(ntiles):
            x = xt[ti % 2]
            if ti + 1 < ntiles:
                load(ti + 1)
            for fc in range(FC):
                bk = nb()
                for wi, w_sb in enumerate((wg, wu)):
                    for kc in range(KC):
                        P.op("pe", lambda bk=bk, wi=wi, w_sb=w_sb, kc=kc, fc=fc: nc.tensor.matmul(
                            out=bk[:, wi * TT:(wi + 1) * TT], lhsT=w_sb[:, kc, fc * 128:(fc + 1) * 128],
                            rhs=xT[:, kc, :], start=(kc == 0), stop=(kc == KC - 1)),
                            [w_sb.b, xT.b], [bk.b])
                s_ = sl[fc % 2]
                P.op("act", lambda bk=bk, s_=s_: nc.scalar.activation(out=s_[:, :], in_=bk[:, 0:TT], func=AF.Silu),
                     [bk.b], [s_.b])
                P.op("dve", lambda bk=bk, s_=s_, fc=fc: nc.vector.tensor_tensor(
                    out=hT[:, fc, :], in0=bk[:, TT:2 * TT], in1=s_[:, :], op=ALU.mult),
                    [bk.b, s_.b], [hT.b])
            if ti + 1 < ntiles:
                transposes(ti + 1)
            for s in range(NS):
                y = yt[s]
                DW = 256
                for dh in range(D_MODEL // DW):
                    bk = nb()
                    for fc in range(FC):
                        P.op("pe", lambda bk=bk, fc=fc, s=s, dh=dh: nc.tensor.matmul(
                            out=bk[:, 0:DW], lhsT=hT[:, fc, s * 128:(s + 1) * 128],
                            rhs=wd[:, fc, dh * DW:(dh + 1) * DW], start=(fc == 0), stop=(fc == FC - 1)),
                            [hT.b, wd.b], [bk.b])
                    P.op("dve", lambda bk=bk, y=y, x=x, s=s, dh=dh: nc.vector.scalar_tensor_tensor(
                        out=y[:, dh * DW:(dh + 1) * DW], in0=bk[:, 0:DW], scalar=0.5,
                        in1=x[:, s * D_MODEL + dh * DW: s * D_MODEL + (dh + 1) * DW],
                        op0=ALU.mult, op1=ALU.add), [bk.b, x.b], [y.b])
            layer_norm(P, yt, st, mv, rs, lng, lnb)
            for s in range(NS):
                t0 = ti * TT + s * 128
                P.dma(out_d[t0:t0 + 128, :], yt[s][:, :], reads=[yt[s].b], writes=[out_b])
    P.barrier()


def layer_norm(P, ys, st, mv, rs, lng, lnb, lnexp=False, affine=True):
    nc = P.nc
    n = len(ys)
    for s, y in enumerate(ys):
        for h in range(2):
            P.op("dve", lambda h=h, s=s, y=y: nc.vector.bn_stats(out=st[:, s, h, :], in_=y[:, h * 512:(h + 1) * 512]),
                 [y.b], [st.b])
    for s in range(n):
        P.op("dve", lambda s=s: nc.vector.bn_aggr(out=mv[:, s, :], in_=st[:, s, :, :].rearrange("p a b -> p (a b)")),
             [st.b], [mv.b])
    P.op("dve", lambda: nc.vector.tensor_scalar(out=rs[:, 0:n], in0=mv[:, 0:n, 1], scalar1=LN_EPS, scalar2=None,
                                                op0=ALU.add), [mv.b], [rs.b])
    if lnexp:
        P.op("act", lambda: nc.scalar.activation(out=rs[:, 0:n], in_=rs[:, 0:n], func=AF.Ln), [rs.b], [rs.b])
        P.op("act", lambda: nc.scalar.activation(out=rs[:, 0:n], in_=rs[:, 0:n], func=AF.Exp, scale=-0.5), [rs.b], [rs.b])
    else:
        P.op("act", lambda: nc.scalar.activation(out=rs[:, 0:n], in_=rs[:, 0:n], func=AF.Sqrt), [rs.b], [rs.b])
        P.op("dve", lambda: nc.vector.reciprocal(out=rs[:, 0:n], in_=rs[:, 0:n]), [rs.b], [rs.b])
    for s, y in enumerate(ys):
        P.op("dve", lambda s=s, y=y: nc.vector.tensor_scalar(out=y[:, :], in0=y[:, :], scalar1=mv[:, s, 0:1],
                                                         scalar2=rs[:, s:s + 1], op0=ALU.subtract, op1=ALU.mult),
             [y.b, mv.b, rs.b], [y.b])
        if not affine:
            continue
        P.op("pool", lambda y=y: nc.gpsimd.tensor_tensor(out=y[:, :], in0=y[:, :], in1=lng[:, :], op=ALU.mult),
             [y.b, lng.b], [y.b])
        P.op("pool", lambda y=y: nc.gpsimd.tensor_tensor(out=y[:, :], in0=y[:, :], in1=lnb[:, :], op=ALU.add),
             [y.b, lnb.b], [y.b])


class K:
    def __init__(self, P):
        self.P = P
        self.nc = P.nc
        self.ei = 0

    def mm(self, bk, out, lhsT, rhs, rb, start=True, stop=True, skip=False):
        nc = self.nc
        self.P.op("pe", lambda: nc.tensor.matmul(out=out, lhsT=lhsT, rhs=rhs, start=start, stop=stop,
                                                 skip_group_check=skip), rb, [bk.b])

    def tr(self, bk, out, in_, ident, rb):
        nc = self.nc
        self.P.op("pe", lambda: nc.tensor.transpose(out=out, in_=in_, identity=ident), rb, [bk.b])

    def act(self, out, in_, func, rb, wb, scale=1.0, bias=0.0, accum=None):
        nc = self.nc
        self.P.op("act", lambda: nc.scalar.activation(out=out, in_=in_, func=func, bias=bias, scale=scale,
                                                      accum_out=accum), rb, wb)

    def ts(self, eng, out, in0, s1, s2, op0, op1, rb, wb):
        e = self.P.eng_obj(eng)
        if s2 is None:
            self.P.op(eng, lambda: e.tensor_scalar(out=out, in0=in0, scalar1=s1, scalar2=None, op0=op0), rb, wb)
        else:
            self.P.op(eng, lambda: e.tensor_scalar(out=out, in0=in0, scalar1=s1, scalar2=s2, op0=op0, op1=op1), rb, wb)

    def tt(self, eng, out, in0, in1, op, rb, wb):
        e = self.P.eng_obj(eng)
        self.P.op(eng, lambda: e.tensor_tensor(out=out, in0=in0, in1=in1, op=op), rb, wb)

    def stt(self, eng, out, in0, s, in1, op0, op1, rb, wb):
        e = self.P.eng_obj(eng)
        self.P.op(eng, lambda: e.scalar_tensor_tensor(out=out, in0=in0, scalar=s, in1=in1, op0=op0, op1=op1), rb, wb)

    def cp(self, out, in_, rb, wb, eng=None):
        nc = self.nc
        if eng is None:
            eng = ("act", "dve")[self.ei % 2]
            self.ei += 1
        if eng == "act":
            self.P.op("act", lambda: nc.scalar.copy(out=out, in_=in_), rb, wb)
        elif eng == "dve":
            self.P.op("dve", lambda: nc.vector.tensor_copy(out=out, in_=in_), rb, wb)
        else:
            self.P.op("pool", lambda: nc.gpsimd.tensor_copy(out=out, in_=in_), rb, wb)

    def memset(self, eng, ap, val, wb):
        e = self.P.eng_obj(eng)
        self.P.op(eng, lambda: e.memset(ap, val), [], wb)


N_IN = 3360
import os as _os
SKIP = set(_os.environ.get('KSKIP', '').split(','))
FM_COLS = 2048
TM_Z = 2048
TM_S = 2560
P1_COLS = 2592


def phase_b1(P, D, banks, nseq, T_):
    nc = P.nc
    k = K(P)
    NT = T_ // 128
    NC = (T_ - 32) // 16 + 1
    bi = 0

    def nb():
        nonlocal bi
        b = banks[bi % 6]
        bi += 1
        return b

    with ExitStack() as es:
        w1p = T(es, nc, "b1_w1p", [128, 8, 768], BF16)
        stg = [T(es, nc, "b1_b1stg%d" % i, [128, 1024], F32) for i in range(2)]
        cw1 = [T(es, nc, "b1_cw1_%d" % i, [128, 32, 256], BF16) for i in range(2)]
        cw2k = T(es, nc, "b1_cw2k", [128, 2, 128], BF16)
        cw2v = T(es, nc, "b1_cw2v", [128, 2, 64], BF16)
        posT = [T(es, nc, "b1_posT%d" % i, [128, 32], BF16) for i in range(2)]
        pb = T(es, nc, "b1_posb", [128, 4], F32)
        ident = T(es, nc, "b1_b1ident", [128, 128], F32)
        xt = [T(es, nc, "b1_b1x%d" % i, [128, D_MODEL], F32) for i in range(3)]
        xTs = [T(es, nc, "b1_b1xT%d" % i, [128, 8, 128], BF16) for i in range(2)]
        kcT = T(es, nc, "b1_kcT", [128, T_], BF16)
        vcT = T(es, nc, "b1_vcT", [128, T_], BF16)
        ksT = T(es, nc, "b1_b1ksT", [128, T_], BF16)
        ks1T = T(es, nc, "b1_b1ks1T", [64, T_], BF16)
        kwT = T(es, nc, "b1_b1kwT", [128, T_], BF16)
        vst = T(es, nc, "b1_b1vs", [128, NT, 2, 65], BF16)
        vwt = T(es, nc, "b1_b1vw", [128, NT, 2, 65], BF16)
        kcmp = T(es, nc, "b1_b1kcmp", [128, 256], BF16)
        vcmp = T(es, nc, "b1_b1vcmp", [128, 2, 2, 129], BF16)
        c2s = T(es, nc, "b1_b1c2s", [128, 2, 64], F32)
        hid = [T(es, nc, "b1_hid%d" % i, [128, 2, 256], BF16) for i in range(2)]
        gx = [T(es, nc, "b1_gx%d" % i, [128, 256], F32) for i in range(2)]
        gu = [T(es, nc, "b1_gu%d" % i, [128, 256], F32) for i in range(2)]

        P.dma(ident[:], D["ident"], writes=[ident.b])
        P.dma(c2s[:], D["c2s"], writes=[c2s.b])
        si = 0
        for kc in range(8):
            s = stg[si % 2]; si += 1
            P.dma(s[:, 0:768], D["w_in"][kc * 128:(kc + 1) * 128, P1_COLS:N_IN], writes=[s.b])
            k.cp(w1p[:, kc, :], s[:, 0:768], [s.b], [w1p.b])
        for sq in range(nseq):
            k.memset("pool", vst[:, :, :, 64:65], 1.0, [vst.b])
            k.memset("pool", vwt[:, :, :, 64:65], 1.0, [vwt.b])
            k.memset("pool", vcmp[:, :, :, 0:65], 0.0, [vcmp.b])
            k.memset("pool", kcmp[:, :], 0.0, [kcmp.b])
            k.memset("pool", vcmp[:, :, :, 64:65], 1.0, [vcmp.b])
            for nch in range(2):
                for hk in range(2):
                    k.cp(vcmp[:, nch, hk, 65:129], c2s[:, nch, :], [c2s.b], [vcmp.b], eng="pool")

            def load(ti):
                x = xt[ti % 3]
                t0 = sq * T_ + ti * 128
                P.dma(x[:, :], D["x1"][t0:t0 + 128, :], reads=[D["x1_b"]], writes=[x.b])

            def tposes(ti):
                x = xt[ti % 3]
                xT_ = xTs[ti % 2]
                for q in range(2):
                    bk = nb()
                    for j in range(4):
                        kc = q * 4 + j
                        k.tr(bk, bk[:, j * 128:(j + 1) * 128], x[:, kc * 128:(kc + 1) * 128], ident[:], [x.b, ident.b])
                    k.cp(xT_[:, q * 4:(q + 1) * 4, :], bk[:, :].rearrange("p (j c) -> p j c", j=4), [bk.b], [xT_.b])
            load(0)
            if NT > 1:
                load(1)
            tposes(0)
            for ti in range(NT if 'proj' not in SKIP else 0):
                xT = xTs[ti % 2]
                if ti + 2 < NT:
                    load(ti + 2)
                if ti + 1 < NT:
                    tposes(ti + 1)
                bk = nb()
                for c in range(4):
                    for kc in range(8):
                        k.mm(bk, bk[:, c * 128:(c + 1) * 128], w1p[:, kc, c * 128:(c + 1) * 128], xT[:, kc, :],
                             [w1p.b, xT.b], start=(kc == 0), stop=(kc == 7))
                for c, dst in enumerate((kcT, vcT, ksT, kwT)):
                    k.cp(dst[:, ti * 128:(ti + 1) * 128], bk[:, c * 128:(c + 1) * 128], [bk.b], [dst.b])
                bk = nb()
                for kc in range(8):
                    k.mm(bk, bk[0:64, 0:128], w1p[:, kc, 320:384], xT[:, kc, :], [w1p.b, xT.b], start=(kc == 0), stop=(kc == 7))
                k.cp(ks1T[0:64, ti * 128:(ti + 1) * 128], bk[0:64, 0:128], [bk.b], [ks1T.b])
                bk = nb()
                for kc in range(8):
                    k.mm(bk, bk[:, 0:256], xT[:, kc, :], w1p[:, kc, 512:768], [w1p.b, xT.b], start=(kc == 0), stop=(kc == 7))
                k.cp(vst[:, ti, :, 0:64], bk[:, 0:128].rearrange("p (h d) -> p h d", h=2), [bk.b], [vst.b])
                k.cp(vwt[:, ti, :, 0:64], bk[:, 128:256].rearrange("p (h d) -> p h d", h=2), [bk.b], [vwt.b])
            if sq == 0:
                dsi = [0]
                for kv in range(2):
                    for q in range(8):
                        s = stg[dsi[0] % 2]; dsi[0] += 1
                        P.dma(s[:, :].rearrange("p (l j) -> p l j", l=4), D["cw1"][kv, :, q * 4:(q + 1) * 4, :], writes=[s.b])
                        k.cp(cw1[kv][:, q * 4:(q + 1) * 4, :], s[:, :].rearrange("p (l j) -> p l j", l=4), [s.b], [cw1[kv].b])
                s = stg[dsi[0] % 2]; dsi[0] += 1
                P.dma(s[:, 0:256].rearrange("p (c j) -> p c j", c=2), D["cw2k"], writes=[s.b])
                k.cp(cw2k[:, :, :], s[:, 0:256].rearrange("p (c j) -> p c j", c=2), [s.b], [cw2k.b])
                s = stg[dsi[0] % 2]; dsi[0] += 1
                P.dma(s[:, 0:128].rearrange("p (c j) -> p c j", c=2), D["cw2v"], writes=[s.b])
                k.cp(cw2v[:, :, :], s[:, 0:128].rearrange("p (c j) -> p c j", c=2), [s.b], [cw2v.b])
                for kv in range(2):
                    s = stg[dsi[0] % 2]; dsi[0] += 1
                    P.dma(s[:, 0:32], D["posT"][kv], writes=[s.b])
                    k.cp(posT[kv][:, :], s[:, 0:32], [s.b], [posT[kv].b])
                bk = nb()
                for kv in range(2 if 'pb' not in SKIP else 0):
                    for jc in range(2):
                        col = kv * 2 + jc
                        for l in range(32):
                            k.mm(bk, bk[:, col:col + 1], cw1[kv][0:64, l, jc * 128:(jc + 1) * 128], posT[kv][0:64, l:l + 1],
                                 [cw1[kv].b, posT[kv].b], start=(l == 0), stop=(l == 31), skip=True)
                if 'pb' not in SKIP:
                    k.cp(pb[:, :], bk[:, 0:4], [bk.b], [pb.b], eng="dve")
                else:
                    k.memset('dve', pb[:, :], 0.0, [pb.b])

            for kv, src in enumerate((kcT, vcT) if 'cmp' not in SKIP else ()):
                for hk in range(2):
                    h_ = hid[hk]
                    for jc in range(2):
                        bk = nb()
                        for l in range(32):
                            k.mm(bk, bk[:, 0:NC], cw1[kv][hk * 64:(hk + 1) * 64, l, jc * 128:(jc + 1) * 128],
                                 src[hk * 64:(hk + 1) * 64, l:l + 16 * (NC - 1) + 1:16], [cw1[kv].b, src.b],
                                 start=(l == 0), stop=(l == 31))
                        x_ = gx[jc]; u_ = gu[jc]
                        col = kv * 2 + jc
                        k.ts("dve", x_[:, 0:NC], bk[:, 0:NC], pb[:, col:col + 1], None, ALU.add, None, [bk.b, pb.b], [x_.b])
                        k.tt("dve", u_[:, 0:NC], x_[:, 0:NC], x_[:, 0:NC], ALU.mult, [x_.b], [u_.b])
                        k.ts("dve", u_[:, 0:NC], u_[:, 0:NC], 0.044715, 1.0, ALU.mult, ALU.add, [u_.b], [u_.b])
                        k.tt("dve", u_[:, 0:NC], u_[:, 0:NC], x_[:, 0:NC], ALU.mult, [u_.b, x_.b], [u_.b])
                        k.act(u_[:, 0:NC], u_[:, 0:NC], AF.Exp, [u_.b], [u_.b], scale=-1.5957691216)
                        k.ts("dve", u_[:, 0:NC], u_[:, 0:NC], 1.0, None, ALU.add, None, [u_.b], [u_.b])
                        P.op("dve", lambda u_=u_: nc.vector.reciprocal(out=u_[:, 0:NC], in_=u_[:, 0:NC]), [u_.b], [u_.b])
                        k.tt("dve", h_[:, jc, 0:NC], u_[:, 0:NC], x_[:, 0:NC], ALU.mult, [u_.b, x_.b], [h_.b])
                    if kv == 0:
                        bk = nb()
                        for jc in range(2):
                            k.mm(bk, bk[:, 0:NC], cw2k[:, jc, :], h_[:, jc, 0:NC], [cw2k.b, h_.b], start=(jc == 0), stop=(jc == 1))
                        k.cp(kcmp[hk * 64:(hk + 1) * 64, 0:NC], bk[hk * 64:(hk + 1) * 64, 0:NC], [bk.b], [kcmp.b])
                    else:
                        for nch in range(2):
                            rows = min(NC - nch * 128, 128)
                            if rows <= 0:
                                continue
                            bk = nb()
                            for jc in range(2):
                                k.mm(bk, bk[0:rows, 0:64], h_[:, jc, nch * 128:nch * 128 + rows], cw2v[:, jc, :],
                                     [cw2v.b, h_.b], start=(jc == 0), stop=(jc == 1))
                            k.cp(vcmp[0:rows, nch, hk, 0:64], bk[0:rows, 0:64], [bk.b], [vcmp.b])
            if 'state' in SKIP:
                continue
            sb = D["state_b"]
            P.dma(D["ksT"][sq], ksT[:, :], reads=[ksT.b], writes=[sb])
            P.dma(D["kwT"][sq], kwT[:, :], reads=[kwT.b], writes=[sb])
            P.dma(D["ks1T"][sq], ks1T[0:64, :], reads=[ks1T.b], writes=[sb])
            P.dma(D["vs"][sq], vst[:, :, :, :], reads=[vst.b], writes=[sb])
            P.dma(D["vw"][sq], vwt[:, :, :, :], reads=[vwt.b], writes=[sb])
            P.dma(D["kcmp"][sq], kcmp[:, :], reads=[kcmp.b], writes=[sb])
            P.dma(D["vcmp"][sq], vcmp[:, :, :, :], reads=[vcmp.b], writes=[sb])
    P.barrier()


def phase_b2(P, D, banks, nseq, T_, dbg=False):
    nc = P.nc
    k = K(P)
    NT = T_ // 128
    bi = 0

    def nb():
        nonlocal bi
        b = banks[bi % 5]
        bi += 1
        return b
    bselAB, bwin = (banks[5], banks[6]), banks[7]

    with ExitStack() as es:
        def S(name, shape, dt=F32):
            return T(es, nc, "b2" + name, shape, dt)
        win = S("win", [128, 8, P1_COLS], BF16)
        wout = S("wout", [128, 8, D_MODEL], BF16)
        ident, U, NegU, ones, M1, M2 = [S(n, [128, 128]) for n in ("ident", "U", "NegU", "ones", "M1", "M2")]
        ident4 = S("ident4", [128, 512], BF16)
        CBT = S("CBT", [128, 128], BF16)
        WBT = S("WBT", [128, 128], BF16)
        convw = S("convw", [128, 12, 4])
        alog = S("alog", [128, 4]); dtb = S("dtb", [128, 4]); negA = S("negA", [128, 4])
        normw = S("normw", [128, 512])
        ksE = [S("ksE%d" % i, [128, T_], BF16) for i in range(2)]
        NT2 = S("NT2", [128, 128])
        vs = S("vs", [128, NT, 2, 65], BF16)
        kcmp = S("kcmp", [128, 256], BF16); vcmp = S("vcmp", [128, 2, 2, 129], BF16)
        xt = [S("x%d" % i, [128, D_MODEL]) for i in range(2)]
        xT = S("xT", [128, 8, 128], BF16)
        raw = S("raw", [128, 12, 131])
        cacc = [S("cacc%d" % i, [128, 128]) for i in range(4)]
        sil = S("sil", [128, 8, 128])
        silb = [Buf("silb%d" % i) for i in range(8)]
        sq = [S("sq%d" % i, [128, 128]) for i in range(2)]
        rn = [S("rn%d" % i, [128, 128]) for i in range(2)]
        sm = S("sm", [128, 32]); smt = S("smt", [128, 32])
        gcs = S("gcs", [128, 8])
        class NS:
            pass
        PB = []
        for par in range(2):
            pb = NS()
            sfx = "_p%d" % par
            pb.qnb = S("qnb" + sfx, [128, 4, 128], BF16); pb.knb = S("knb" + sfx, [128, 4, 128], BF16)
            pb.kn = S("kn" + sfx, [128, 4, 128]); pb.silv = S("silv" + sfx, [128, 4, 128])
            pb.silvb = [Buf("silvb%d" % i + sfx) for i in range(4)]
            pb.nqT = S("nqT" + sfx, [128, 4, 128], BF16); pb.zt = S("zt" + sfx, [128, 512])
            pb.QN = [S("QN%d" % i + sfx, [128, 512], BF16) for i in range(2)]
            pb.beta = S("beta" + sfx, [128, 4]); pb.nbeta = S("nbeta" + sfx, [128, 4]); pb.g_ = S("g" + sfx, [128, 4])
            pb.egc = S("egc" + sfx, [128, 4]); pb.egl = S("egl" + sfx, [128, 4]); pb.eglm = S("eglm" + sfx, [128, 4])
            pb.gcp = S("gcp" + sfx, [128, 8]); pb.sg = S("sg" + sfx, [128, 24]); pb.cmbtb = S("cmbtb" + sfx, [128, 2, 128], BF16); pb.kf = S("kf" + sfx, [128, 2, 64])
            pb.Gm = [S("Gm%d" % h + sfx, [128, 128]) for h in range(4)]
            pb.kwin = S("kwin" + sfx, [128, 640], BF16); pb.vwin = S("vwin" + sfx, [128, 5, 2, 65], BF16)
            PB.append(pb)
        HB = []
        for h in range(4):
            HB.append((S("DecS%d" % h, [128, 128]), S("DecTi%d" % h, [128, 128]),
                       [S("Lb%d_%d" % (h, i), [128, 3, 128]) for i in range(2)], S("TT%d" % h, [128, 128]),
                       S("QKdT%d" % h, [128, 128], BF16), S("vtok%d" % h, [128, 128]), S("kd%d" % h, [128, 128], BF16),
                       S("t1_%d" % h, [128, 128]), S("t2_%d" % h, [128, 128]), S("vnb%d" % h, [128, 128], BF16)))
        junk = [S("junk%d" % h, [128, 128], BF16) for h in range(4)]
        ogs = [S("og%d" % i, [128, 4, 128]) for i in range(2)]
        mss = [S("ms%d" % i, [128, 4]) for i in range(2)]
        ogbs = [[Buf("ogb%d_%d" % (i, h)) for h in range(4)] for i in range(2)]
        msbs = [[Buf("msb%d_%d" % (i, h)) for h in range(4)] for i in range(2)]
        rstd = S("rstd", [128, 4])
        St = [S("S%d" % h, [128, 128]) for h in range(4)]
        Sb = [S("Sb%d" % h, [128, 128], BF16) for h in range(4)]
        casb = [S("casb%d" % i, [128, 260]) for i in range(2)]
        wsb = [S("wsb%d" % i, [128, 260]) for i in range(2)]
        omixs = [S("omix%d" % i, [128, D_MODEL]) for i in range(2)]; omixT = S("omixT", [128, 8, 128], BF16)
        y = S("y", [128, D_MODEL])
        st = S("bnst", [128, 1, 2, 6]); mv = S("bnmv", [128, 1, 2]); rs = S("rs", [128, 1])
        SKEW = int(_os.environ.get("SKEW", "2"))
        NPT = SKEW + 2
        pT = [S("pT%d" % i, [128, 512], BF16) for i in range(NPT)]
        cmbt = S("cmbt", [128, 2, 128])
        lc = S("lc", [128, 4]); rl = S("rl", [128, 4]); imp = S("imp", [128, 64]); impt = S("impt", [128, 64])
        m8a = S("m8a", [128, 8]); m8b = S("m8b", [128, 8])
        Lg = S("Lg", [128, 4, 3]); coef = S("coef", [128, 4, 3]); tn = S("tn", [128, 64])

        for t_, nm in ((ident, "ident"), (U, "U"), (NegU, "NegU"), (ones, "ones"), (M1, "M1"), (M2, "M2"),
                       (convw, "convw"), (alog, "alog"), (dtb, "dtb"), (normw, "normw")):
            P.dma(t_.t[tuple(slice(None) for _ in t_.t.shape)], D[nm], writes=[t_.b])
        si = 0
        for t_, nm, w_ in ((ident4, "ident4", 512), (CBT, "CBT", 128), (WBT, "WBT", 128)):
            s = xt[si % 2]; si += 1
            P.dma(s[:, 0:w_], D[nm], writes=[s.b])
            k.cp(t_[:, :], s[:, 0:w_], [s.b], [t_.b])
        for kc in range(8):
            for c3 in range(3):
                s = xt[si % 2]; si += 1
                P.dma(s[:, 0:864], D["w_in"][kc * 128:(kc + 1) * 128, c3 * 864:(c3 + 1) * 864], writes=[s.b])
                k.cp(win[:, kc, c3 * 864:(c3 + 1) * 864], s[:, 0:864], [s.b], [win.b])
        k.memset("pool", NT2[:, :], 0.0, [NT2.b])
        k.act(negA[:, :], alog[:, :], AF.Exp, [alog.b], [negA.b])
        k.ts("dve", negA[:, :], negA[:, :], -1.0, None, ALU.mult, None, [negA.b], [negA.b])


        def gdn_stream(h, pb, par):
            og = ogs[par]; ogb = ogbs[par]; ms = mss[par]; msb = msbs[par]
            DecS, DecTi, Lb, TT_, QKdT, vtok, kd, t1, t2, vnb = HB[h]
            Gm = pb.Gm[h]
            bd = nb()
            k.mm(bd, bd[:, 0:128], Gm[:, :], NegU[:, :], [NegU.b, Gm.b])
            k.mm(bd, bd[:, 128:256], pb.knb[:, h, :], pb.knb[:, h, :], [pb.knb.b])
            k.mm(bd, bd[:, 256:384], pb.knb[:, h, :], pb.qnb[:, h, :], [pb.knb.b, pb.qnb.b])
            k.tt("dve", DecS[:, :], bd[:, 0:128], M1[:, :], ALU.add, [bd.b, M1.b], [DecS.b])
            k.stt("dve", DecTi[:, :], bd[:, 0:128], -1.0, M2[:, :], ALU.mult, ALU.add, [bd.b, M2.b], [DecTi.b])
            k.act(DecS[:, :], DecS[:, :], AF.Exp, [DecS.b, pb.gcp.b], [DecS.b], bias=pb.gcp[:, h:h + 1])
            k.act(DecTi[:, :], DecTi[:, :], AF.Exp, [DecTi.b, pb.gcp.b], [DecTi.b], bias=pb.gcp[:, 4 + h:5 + h])
            k.stt("dve", Lb[0][:, 0, :], bd[:, 128:256], pb.beta[:, h:h + 1], DecS[:, :], ALU.mult, ALU.mult,
                  [bd.b, pb.beta.b, DecS.b], [Lb[0].b])
            k.tt("dve", QKdT[:, :], bd[:, 256:384], DecTi[:, :], ALU.mult, [bd.b, DecTi.b], [QKdT.b])
            yield
            bt = nb()
            k.tr(bt, bt[:, 0:128], Lb[0][:, 0, :], ident[:, :], [Lb[0].b, ident.b])
            k.cp(Lb[0][:, 1, :], bt[:, 0:128], [bt.b], [Lb[0].b], eng="act")
            k.stt("dve", Lb[1][:, 2, :], bt[:, 0:128], -1.0, ident[:, :], ALU.mult, ALU.add, [bt.b, ident.b], [Lb[1].b])
            yield
            for lvl in range(1, 8):
                cur = Lb[(lvl - 1) % 2]
                nxt = Lb[lvl % 2]
                bk = nb()
                if lvl <= 6:
                    k.mm(bk, bk[:, 0:128], cur[:, 1, :], cur[:, 0, :], [cur.b])
                    k.mm(bk, bk[:, 128:256], cur[:, 0, :], cur[:, 1, :], [cur.b])
                if lvl >= 2:
                    k.mm(bk, bk[:, 256:384], cur[:, 0, :], cur[:, 2, :], [cur.b])
                if lvl <= 6:
                    k.cp(nxt[:, 0:2, :], bk[:, 0:256].rearrange("p (a c) -> p a c", a=2), [bk.b], [nxt.b], eng="act")
                if lvl >= 2:
                    dst = nxt[:, 2, :] if lvl <= 6 else TT_[:, :]
                    dstb = nxt.b if lvl <= 6 else TT_.b
                    k.tt("dve", dst, bk[:, 256:384], cur[:, 2, :], ALU.add, [bk.b, cur.b], [dstb])
                yield
            bq = nb()
            k.mm(bq, bq[:, 0:128], pb.knb[:, h, :], Sb[h][:, :], [pb.knb.b, Sb[h].b])
            k.mm(bq, bq[:, 128:256], pb.qnb[:, h, :], Sb[h][:, :], [pb.qnb.b, Sb[h].b])
            k.tr(bq, bq[:, 256:384], pb.silv[:, h, :], ident[:, :], [pb.silv.b, ident.b])
            k.tr(bq, bq[:, 384:512], pb.kn[:, h, :], ident[:, :], [pb.kn.b, ident.b])
            k.cp(vtok[:, :], bq[:, 256:384], [bq.b], [vtok.b], eng="act")
            k.ts("dve", kd[:, :], bq[:, 384:512], pb.eglm[:, h:h + 1], None, ALU.mult, None, [bq.b, pb.eglm.b], [kd.b])
            k.stt("dve", t1[:, :], bq[:, 0:128], pb.egc[:, h:h + 1], vtok[:, :], ALU.mult, ALU.subtract,
                  [bq.b, pb.egc.b, vtok.b], [t1.b])
            k.ts("dve", t1[:, :], t1[:, :], pb.nbeta[:, h:h + 1], None, ALU.mult, None, [t1.b, pb.nbeta.b], [t1.b])
            k.ts("dve", t2[:, :], bq[:, 128:256], pb.egc[:, h:h + 1], None, ALU.mult, None, [bq.b, pb.egc.b], [t2.b])
            yield
            bv = nb()
            k.mm(bv, bv[:, 0:128], TT_[:, :], t1[:, :], [TT_.b, t1.b])
            k.cp(vnb[:, :], bv[:, 0:128], [bv.b], [vnb.b], eng="act")
            yield
            bw = nb()
            k.mm(bw, bw[:, 128:256], QKdT[:, :], vnb[:, :], [QKdT.b, vnb.b])
            k.mm(bw, bw[:, 256:384], kd[:, :], vnb[:, :], [kd.b, vnb.b])
            k.tt("dve", og[:, h, :], bw[:, 128:256], t2[:, :], ALU.add, [bw.b, t2.b], [ogb[h]])
            k.stt("dve", St[h][:, :], St[h][:, :], pb.egl[:, h:h + 1], bw[:, 256:384], ALU.mult, ALU.add,
                  [St[h].b, pb.egl.b, bw.b], [St[h].b])
            k.cp(Sb[h][:, :], St[h][:, :], [St[h].b], [Sb[h].b], eng="pool")
            k.act(junk[h][:, :], og[:, h, :], AF.Square, [ogb[h]], [junk[h].b, msb[h]], accum=ms[:, h:h + 1])
            yield

        def nsa_stream(ti, pb, par):
            omix = omixs[par]
            nchunks = 1 if ti < 16 else 2
            pi = 0
            k0 = max(0, ti - 4)
            for hk in range(2):
                hs = slice(hk * 64, (hk + 1) * 64)
                qrhs = pb.nqT[hs, :, :].rearrange("p g q -> p (g q)")
                bCa = nb()
                bCb = nb()
                for nch in range(nchunks):
                    bk = nb()
                    k.mm(bk, bk[:, :], kcmp[hs, nch * 128:(nch + 1) * 128], qrhs, [kcmp.b, pb.nqT.b], start=True, stop=False)
                    k.mm(bk, bk[:, :], pb.cmbtb[:, nch, :], ident4[:, :], [pb.cmbtb.b, ident4.b], start=False, stop=True)
                    p_ = pT[pi % NPT]; pi += 1
                    k.act(p_[:, :], bk[:, :], AF.Exp, [bk.b], [p_.b], scale=0.125)
                    for g in range(4):
                        k.mm(bCa, bCa[:, g * 65:(g + 1) * 65], p_[:, g * 128:(g + 1) * 128], vcmp[:, nch, hk, 0:65],
                             [p_.b, vcmp.b], start=(nch == 0 and g == 0), stop=(nch == nchunks - 1), skip=True)
                    for g in range(4):
                        k.mm(bCb, bCb[:, g * 64:(g + 1) * 64], p_[:, g * 128:(g + 1) * 128], vcmp[:, nch, hk, 65:129],
                             [p_.b, vcmp.b], start=(nch == 0 and g == 0), stop=(nch == nchunks - 1), skip=True)
                cs = casb[hk]
                k.cp(cs[:, :], bCa[:, 0:260], [bCa.b], [cs.b], eng="act")
                k.ts("dve", rl[:, :], cs[:, 64:260:65], 1e-30, None, ALU.max, None, [cs.b], [rl.b])
                P.op("dve", lambda: nc.vector.reciprocal(out=rl[:, :], in_=rl[:, :]), [rl.b], [rl.b])
                k.ts("dve", imp[:, :], bCb[:, 0:64], rl[:, 0:1], None, ALU.mult, None, [bCb.b, rl.b], [imp.b])
                for g in range(1, 4):
                    k.stt("dve", imp[:, :], bCb[:, g * 64:(g + 1) * 64], rl[:, g:g + 1], imp[:, :], ALU.mult, ALU.add,
                          [bCb.b, rl.b, imp.b], [imp.b])
                yield
                k.tt("dve", imp[:, :], imp[:, :], pb.kf[:, 0, :], ALU.mult, [imp.b, pb.kf.b], [imp.b])
                k.tt("dve", imp[:, :], imp[:, :], pb.kf[:, 1, :], ALU.add, [imp.b, pb.kf.b], [imp.b])
                P.op("dve", lambda: nc.vector.max(out=m8a[:, :], in_=imp[:, :]), [imp.b], [m8a.b])
                P.op("dve", lambda: nc.vector.match_replace(out=impt[:, :], in_to_replace=m8a[:, :], in_values=imp[:, :],
                                                            imm_value=-1e9), [imp.b, m8a.b], [impt.b])
                P.op("dve", lambda: nc.vector.max(out=m8b[:, :], in_=impt[:, :]), [impt.b], [m8b.b])
                k.ts("dve", NT2[:, 64:128], imp[:, :], m8b[:, 7:8], -30000.0, ALU.is_lt, ALU.mult, [imp.b, m8b.b], [NT2.b])
                bt = nb()
                k.tr(bt, bt[:, 0:128], NT2[:, :], ident[:, :], [NT2.b, ident.b])
                k.cp(pb.QN[hk][64:128, :].rearrange("p (g q) -> p g q", g=4),
                     bt[64:128, 0:128].rearrange("p (o q) -> p o q", o=1).to_broadcast([64, 4, 128]), [bt.b], [pb.QN[hk].b], eng="act")
                yield
            for hk in range(2):
                hs = slice(hk * 64, (hk + 1) * 64)
                qrhs = pb.nqT[hs, :, :].rearrange("p g q -> p (g q)")
                pend = []
                for kc in range(k0, ti + 1):
                    bk = nb()
                    last_extra = (kc == ti) or (kc == ti - 4)
                    k.mm(bk, bk[:, :], pb.kwin[hs, (kc - k0) * 128:(kc - k0 + 1) * 128], qrhs, [pb.kwin.b, pb.nqT.b],
                         start=True, stop=not last_extra)
                    if kc == ti:
                        k.mm(bk, bk[:, :], CBT[:, :], ident4[:, :], [CBT.b, ident4.b], start=False, stop=True)
                    elif kc == ti - 4:
                        k.mm(bk, bk[:, :], WBT[:, :], ident4[:, :], [WBT.b, ident4.b], start=False, stop=True)
                    p_ = pT[pi % NPT]; pi += 1
                    k.act(p_[:, :], bk[:, :], AF.Exp, [bk.b], [p_.b], scale=0.125)
                    if len(pend) >= SKEW:
                        pend.pop(0)()
                    def pv(p_=p_, kc=kc, hk=hk):
                        for g in range(4):
                            k.mm(bwin, bwin[:, g * 65:(g + 1) * 65], p_[:, g * 128:(g + 1) * 128], pb.vwin[:, kc - k0, hk, :],
                                 [p_.b, pb.vwin.b], start=(kc == k0 and g == 0), stop=(kc == ti), skip=True)
                    pend.append(pv)
                    yield
                while pend:
                    pend.pop(0)()
                k.cp(wsb[hk][:, :], bwin[:, 0:260], [bwin.b], [wsb[hk].b], eng="dve")
                yield
            for hk in range(2):
                bsel = bselAB[hk]
                pend = []
                for kc in range(ti + 1):
                    bk = nb()
                    k.mm(bk, bk[:, :], ksE[hk][:, kc * 128:(kc + 1) * 128], pb.QN[hk][:, :], [ksE[hk].b, pb.QN[hk].b],
                         start=True, stop=(kc != ti))
                    if kc == ti:
                        k.mm(bk, bk[:, :], CBT[:, :], ident4[:, :], [CBT.b, ident4.b], start=False, stop=True)
                    p_ = pT[pi % NPT]; pi += 1
                    k.act(p_[:, :], bk[:, :], AF.Exp, [bk.b], [p_.b], scale=0.125)
                    if len(pend) >= SKEW:
                        pend.pop(0)()
                    def pv(p_=p_, kc=kc, hk=hk, bsel=bsel):
                        for g in range(4):
                            k.mm(bsel, bsel[:, g * 65:(g + 1) * 65], p_[:, g * 128:(g + 1) * 128], vs[:, kc, hk, :],
                                 [p_.b, vs.b], start=(kc == 0 and g == 0), stop=(kc == ti), skip=True)
                    pend.append(pv)
                    yield
                while pend:
                    pend.pop(0)()
                cs = casb[hk]; ws = wsb[hk]
                k.cp(Lg[:, :, 0], cs[:, 64:260:65], [cs.b], [Lg.b], eng="dve")
                k.cp(Lg[:, :, 1], bsel[:, 64:260:65], [bsel.b], [Lg.b], eng="dve")
                k.cp(Lg[:, :, 2], ws[:, 64:260:65], [ws.b], [Lg.b], eng="dve")
                k.ts("dve", coef[:, :, :], Lg[:, :, :], 1e-30, None, ALU.max, None, [Lg.b], [coef.b])
                P.op("dve", lambda: nc.vector.reciprocal(out=coef[:, :, :], in_=coef[:, :, :]), [coef.b], [coef.b])
                k.tt("dve", coef[:, :, :], coef[:, :, :], pb.sg[:, hk * 12:(hk + 1) * 12].rearrange("p (g b) -> p g b", b=3),
                     ALU.mult, [coef.b, pb.sg.b], [coef.b])
                for g in range(4):
                    col = 512 + (hk * 4 + g) * 64
                    k.ts("dve", tn[:, :], cs[:, g * 65:g * 65 + 64], coef[:, g, 0:1], None, ALU.mult, None, [cs.b, coef.b], [tn.b])
                    k.stt("dve", tn[:, :], ws[:, g * 65:g * 65 + 64], coef[:, g, 2:3], tn[:, :], ALU.mult, ALU.add,
                          [ws.b, coef.b, tn.b], [tn.b])
                    k.stt("dve", omix[:, col:col + 64], bsel[:, g * 65:g * 65 + 64], coef[:, g, 1:2], tn[:, :], ALU.mult, ALU.add,
                          [bsel.b, coef.b, tn.b], [omix.b])
                yield

        def prologue(sq_i, ti):
            pb = PB[ti % 2]
            x = xt[ti % 2]
            t0 = sq_i * T_ + ti * 128
            P.dma(x[:, :], D["x1"][t0:t0 + 128, :], reads=[D["x1_b"]], writes=[x.b])
            P.dma(pb.kf[:, :, :], D["KF"][ti], writes=[pb.kf.b])
            P.dma(cmbt[:, :, :], D["CMBT"][ti], writes=[cmbt.b])
            k0 = max(0, ti - 4)
            nk = ti + 1 - k0
            P.dma(pb.kwin[:, 0:nk * 128], D["kwT"][sq_i][:, k0 * 128:(ti + 1) * 128], reads=[D["state_b"]], writes=[pb.kwin.b])
            P.dma(pb.vwin[:, 0:nk, :, :], D["vw"][sq_i][:, k0:ti + 1, :, :], reads=[D["state_b"]], writes=[pb.vwin.b])
            k.cp(pb.cmbtb[:, :, :], cmbt[:, :, :], [cmbt.b], [pb.cmbtb.b], eng="pool")
            yield
            yield
            yield
            for q in range(2):
                bk = nb()
                for j in range(4):
                    kc = q * 4 + j
                    k.tr(bk, bk[:, j * 128:(j + 1) * 128], x[:, kc * 128:(kc + 1) * 128], ident[:, :], [x.b, ident.b])
                k.cp(xT[:, q * 4:(q + 1) * 4, :], bk[:, :].rearrange("p (j c) -> p j c", j=4), [bk.b], [xT.b])
                yield
            for q in range(4):
                bk = nb()
                for j in range(4):
                    c = q * 4 + j
                    for kc in range(8):
                        k.mm(bk, bk[:, j * 128:(j + 1) * 128], win[:, kc, c * 128:(c + 1) * 128], xT[:, kc, :],
                             [win.b, xT.b], start=(kc == 0), stop=(kc == 7))
                if q < 3:
                    k.cp(raw[:, q * 4:(q + 1) * 4, 3:131], bk[:, :].rearrange("p (j c) -> p j c", j=4), [bk.b], [raw.b])
                else:
                    k.cp(pb.nqT[:, :, :], bk[:, :].rearrange("p (j c) -> p j c", j=4), [bk.b], [pb.nqT.b])
                    k.cp(pb.QN[0][0:64, :], bk[0:64, :], [bk.b], [pb.QN[0].b])
                yield
            bk = nb()
            for g in range(4):
                for kc in range(8):
                    k.mm(bk, bk[0:64, g * 128:(g + 1) * 128], win[:, kc, 1536 + g * 128 + 64:1536 + (g + 1) * 128], xT[:, kc, :],
                         [win.b, xT.b], start=(kc == 0), stop=(kc == 7))
            k.cp(pb.QN[1][0:64, :], bk[0:64, :], [bk.b], [pb.QN[1].b])
            yield
            bz = nb()
            for kc in range(8):
                k.mm(bz, bz[:, :], xT[:, kc, :], win[:, kc, TM_Z:TM_Z + 512], [win.b, xT.b], start=(kc == 0), stop=(kc == 7))
            k.cp(pb.zt[:, :], bz[:, :], [bz.b], [pb.zt.b], eng="act")
            bs_ = nb()
            for kc in range(8):
                k.mm(bs_, bs_[:, 0:32], xT[:, kc, :], win[:, kc, TM_S:TM_S + 32], [win.b, xT.b], start=(kc == 0), stop=(kc == 7))
            k.cp(sm[:, :], bs_[:, 0:32], [bs_.b], [sm.b], eng="dve")
            yield
            def cdst(c):
                return (sil[:, c, :], silb[c]) if c < 8 else (pb.silv[:, c - 8, :], pb.silvb[c - 8])
            for kk in (3, 2, 1, 0):
                for half in range(2):
                    for c in range(half * 6, half * 6 + 6):
                        a_, ab = cdst(c)
                        if kk == 3:
                            k.ts("dve", a_, raw[:, c, 3:131], convw[:, c, 3:4], None, ALU.mult, None, [raw.b, convw.b],
                                 [ab] if c < 8 else [ab, pb.silv.b])
                        else:
                            k.stt("dve", a_, raw[:, c, kk:kk + 128], convw[:, c, kk:kk + 1], a_, ALU.mult, ALU.add,
                                  [raw.b, convw.b, ab], [ab])
                    yield
            k.cp(raw[:, :, 0:3], raw[:, :, 128:131], [raw.b], [raw.b], eng="pool")
            yield
            yield
            k.act(sil[:, :, :], sil[:, :, :], AF.Silu, silb, silb)
            k.act(pb.silv[:, :, :], pb.silv[:, :, :], AF.Silu, pb.silvb, pb.silvb + [pb.silv.b])
            k.act(pb.zt[:, :], pb.zt[:, :], AF.Silu, [pb.zt.b], [pb.zt.b])
            yield
            yield
            yield
            k.tt("pool", pb.zt[:, :], pb.zt[:, :], normw[:, :], ALU.mult, [pb.zt.b, normw.b], [pb.zt.b])
            k.tt("dve", sq[0][:, :], sil[:, 0, :], sil[:, 0, :], ALU.mult, [silb[0]], [sq[0].b])
            yield
            for c in range(8):
                h = c % 4
                s_ = sq[c % 2]; r_ = rn[c % 2]
                bk = nb()
                k.mm(bk, bk[:, 0:128], ones[:, :], s_[:, :], [ones.b, s_.b])
                k.act(r_[:, :], bk[:, 0:128], AF.Ln, [bk.b], [r_.b], bias=1e-6)
                k.act(r_[:, :], r_[:, :], AF.Exp, [r_.b], [r_.b], scale=-0.5, bias=(-0.5 * float(np.log(128.0)) if c < 4 else 0.0))
                if c + 1 < 8:
                    k.tt("dve", sq[(c + 1) % 2][:, :], sil[:, c + 1, :], sil[:, c + 1, :], ALU.mult, [silb[c + 1]], [sq[(c + 1) % 2].b])
                yield
                if c < 4:
                    k.tt("dve", pb.qnb[:, h, :], sil[:, c, :], r_[:, :], ALU.mult, [silb[c], r_.b], [pb.qnb.b])
                else:
                    k.tt("dve", pb.kn[:, h, :], sil[:, c, :], r_[:, :], ALU.mult, [silb[c], r_.b], [pb.kn.b])
                    k.cp(pb.knb[:, h, :], pb.kn[:, h, :], [pb.kn.b], [pb.knb.b], eng="pool")
            yield
            k.act(smt[:, 0:4], sm[:, 0:4], AF.Exp, [sm.b], [smt.b], scale=-1.0)
            k.act(smt[:, 8:32], sm[:, 8:32], AF.Exp, [sm.b], [smt.b], scale=-1.0)
            k.tt("dve", smt[:, 4:8], sm[:, 4:8], dtb[:, :], ALU.add, [sm.b, dtb.b], [smt.b])
            k.act(smt[:, 4:8], smt[:, 4:8], AF.Exp, [smt.b], [smt.b])
            k.ts("dve", smt[:, :], smt[:, :], 1.0, None, ALU.add, None, [smt.b], [smt.b])
            k.act(pb.g_[:, :], smt[:, 4:8], AF.Ln, [smt.b], [pb.g_.b])
            k.tt("dve", pb.g_[:, :], pb.g_[:, :], negA[:, :], ALU.mult, [pb.g_.b, negA.b], [pb.g_.b])
            P.op("dve", lambda: nc.vector.reciprocal(out=pb.beta[:, :], in_=smt[:, 0:4]), [smt.b], [pb.beta.b])
            k.ts("dve", pb.nbeta[:, :], pb.beta[:, :], -1.0, None, ALU.mult, None, [pb.beta.b], [pb.nbeta.b])
            P.op("dve", lambda: nc.vector.reciprocal(out=pb.sg[:, :], in_=smt[:, 8:32]), [smt.b], [pb.sg.b])
            yield
            bk = nb()
            k.mm(bk, bk[:, 0:4], U[:, :], pb.g_[:, :], [U.b, pb.g_.b])
            k.mm(bk, bk[:, 4:8], ones[:, :], pb.g_[:, :], [ones.b, pb.g_.b])
            k.cp(gcs[:, :], bk[:, 0:8], [bk.b], [gcs.b], eng="dve")
            k.cp(pb.gcp[:, 0:4], gcs[:, 0:4], [gcs.b], [pb.gcp.b], eng="dve")
            k.ts("dve", pb.gcp[:, 4:8], gcs[:, 0:4], -1.0, None, ALU.mult, None, [gcs.b], [pb.gcp.b])
            k.act(pb.egc[:, :], gcs[:, 0:4], AF.Exp, [gcs.b], [pb.egc.b])
            k.act(pb.egl[:, :], gcs[:, 4:8], AF.Exp, [gcs.b], [pb.egl.b])
            k.tt("dve", pb.eglm[:, :], gcs[:, 4:8], gcs[:, 0:4], ALU.subtract, [gcs.b], [pb.eglm.b])
            k.act(pb.eglm[:, :], pb.eglm[:, :], AF.Exp, [pb.eglm.b], [pb.eglm.b])
            for h in range(4):
                k.ts("dve", pb.Gm[h][:, :], ones[:, :], pb.g_[:, h:h + 1], None, ALU.mult, None, [ones.b, pb.g_.b], [pb.Gm[h].b])
            yield

        PRO_W = int(_os.environ.get("PRO_W", "1"))

        def run_all(gens, weights=None, periods=None):
            gens = list(gens)
            weights = dict(weights or {})
            periods = dict(periods or {})
            r = 0
            while gens:
                long_alive = any(id(g) not in periods for g in gens)
                for s_ in list(gens):
                    per, ph = periods.get(id(s_), (1, 0))
                    if long_alive and per > 1 and r % per != ph % per:
                        continue
                    for _ in range(weights.get(id(s_), 1)):
                        try:
                            next(s_)
                        except StopIteration:
                            gens.remove(s_)
                            break
                r += 1

        GDN_SPREAD = int(_os.environ.get("GDN_SPREAD", "1"))

        for sq_i in range(nseq):
            sb = D["state_b"]
            P.dma(ksE[0][0:64, :], D["ksT"][sq_i][0:64, :], reads=[sb], writes=[ksE[0].b])
            P.dma(ksE[1][0:64, :], D["ks1T"][sq_i], reads=[sb], writes=[ksE[1].b])
            P.dma(vs[:, :, :, :], D["vs"][sq_i], reads=[sb], writes=[vs.b])
            P.dma(kcmp[:, :], D["kcmp"][sq_i], reads=[sb], writes=[kcmp.b])
            P.dma(vcmp[:, :, :, :], D["vcmp"][sq_i], reads=[sb], writes=[vcmp.b])
            k.memset("pool", raw[:, :, 0:3], 0.0, [raw.b])
            for h in range(4):
                k.memset("pool", St[h][:, :], 0.0, [St[h].b])
                k.memset("pool", Sb[h][:, :], 0.0, [Sb[h].b])
            run_all([prologue(sq_i, 0)])
            if sq_i == 0:
                for kc in range(8):
                    s = xt[1]
                    P.dma(s[:, :], D["w_out"][kc * 128:(kc + 1) * 128, :], writes=[s.b])
                    k.cp(wout[:, kc, :], s[:, :], [s.b], [wout.b])
                for c4 in range(T_ // 1024 if T_ >= 1024 else 1):
                    w_ = min(1024, T_)
                    s = xt[1]
                    P.dma(s[64:128, 0:w_], D["Econst"][:, c4 * w_:(c4 + 1) * w_], writes=[s.b])
                    for i in range(2):
                        k.cp(ksE[i][64:128, c4 * w_:(c4 + 1) * w_], s[64:128, 0:w_], [s.b], [ksE[i].b])

            def epilogue(ti):
                par = ti % 2
                pb = PB[par]; x = xt[par]; omix = omixs[par]; og = ogs[par]; ogb = ogbs[par]; ms = mss[par]; msb = msbs[par]
                t0 = sq_i * T_ + ti * 128
                k.ts("dve", y[:, :], x[:, :], DN_ALPHA, None, ALU.mult, None, [x.b], [y.b])
                k.act(rstd[:, :], ms[:, :], AF.Ln, msb, [rstd.b], scale=1.0 / 128.0, bias=1e-6)
                k.act(rstd[:, :], rstd[:, :], AF.Exp, [rstd.b], [rstd.b], scale=-0.5)
                yield
                for h in range(4):
                    k.stt("dve", omix[:, h * 128:(h + 1) * 128], og[:, h, :], rstd[:, h:h + 1], pb.zt[:, h * 128:(h + 1) * 128],
                          ALU.mult, ALU.mult, [ogb[h], rstd.b, pb.zt.b], [omix.b])
                if dbg:
                    P.dma(D["dbg"][t0:t0 + 128, :], omix[:, :], reads=[omix.b], writes=[D["dbg_b"]])
                yield
                for q in range(2):
                    bk = nb()
                    for j in range(4):
                        cc = q * 4 + j
                        k.tr(bk, bk[:, j * 128:(j + 1) * 128], omix[:, cc * 128:(cc + 1) * 128], ident[:, :], [omix.b, ident.b])
                    k.cp(omixT[:, q * 4:(q + 1) * 4, :], bk[:, :].rearrange("p (j c) -> p j c", j=4), [bk.b], [omixT.b])
                    yield
                for dh in range(2):
                    bk = nb()
                    for cc in range(8):
                        k.mm(bk, bk[:, :], omixT[:, cc, :], wout[:, cc, dh * 512:(dh + 1) * 512], [omixT.b, wout.b],
                             start=(cc == 0), stop=(cc == 7))
                    k.tt("dve", y[:, dh * 512:(dh + 1) * 512], bk[:, :], y[:, dh * 512:(dh + 1) * 512], ALU.add, [y.b, bk.b], [y.b])
                    yield
                layer_norm(P, [y], st, mv, rs, None, None, lnexp=True, affine=False)
                P.dma(D["x2"][t0:t0 + 128, :], y[:, :], reads=[y.b], writes=[D["x2_b"]])
                yield

            for ti in range(NT):
                pb = PB[ti % 2]
                gens = []
                gg = [gdn_stream(h, pb, ti % 2) for h in range(4)]
                gens += [nsa_stream(ti, pb, ti % 2)] + gg
                if ti > 0:
                    gens.append(epilogue(ti - 1))
                wts = {}
                pers = {}
                if GDN_SPREAD:
                    n_rounds = 2 * (ti + 1) + 2 * min(5, ti + 1) + 8
                    per = max(1, min(int(_os.environ.get('GDN_MAXP', '3')), n_rounds // int(_os.environ.get('GDN_DIV', '15'))))
                    for h, g in enumerate(gg):
                        pers[id(g)] = (per, h)
                if ti + 1 < NT:
                    pg = prologue(sq_i, ti + 1)
                    gens.append(pg)
                    wts[id(pg)] = PRO_W
                run_all(gens, wts, pers)
            run_all([epilogue(NT - 1)])
    P.barrier()


def _w_in_perm():
    o = {}
    off = 0
    for nm, sz in (("gq", 512), ("gk", 512), ("gv", 512), ("gz", 512), ("gb", 4), ("ga", 4), ("nq", 512), ("kc", 128),
                   ("vc", 128), ("ks", 128), ("vs", 128), ("kw", 128), ("vw", 128), ("gates", 24)):
        o[nm] = np.arange(off, off + sz)
        off += sz
    nq = o["nq"].reshape(2, 4, 64).transpose(1, 0, 2).reshape(-1)
    return np.concatenate([o["gq"], o["gk"], o["gv"], nq, o["gz"], o["gb"], o["ga"], o["gates"],
                           o["kc"], o["vc"], o["ks"], o["kw"], o["vs"], o["vw"]])


def host_consts(T_):
    NT = T_ // 128
    NC = (T_ - 32) // 16 + 1
    f = np.float32
    p = np.arange(128)[:, None]
    q = np.arange(128)[None, :]
    c = {}
    c["ident"] = np.eye(128, dtype=f)
    c["U"] = (p <= q).astype(f)
    c["NegU"] = -c["U"]
    c["ones"] = np.ones((128, 128), f)
    c["M1"] = np.where(q < p, 0.0, -1e30).astype(f)
    c["M2"] = np.where(p <= q, 0.0, -1e30).astype(f)
    c["ident4"] = np.tile(np.eye(128, dtype=f), (1, 4))
    c["CBT"] = np.where(q <= p, 0.0, -30000.0).astype(f)
    c["WBT"] = np.where(q > p, 0.0, -30000.0).astype(f)
    cs = np.arange(256) * 16
    ss = np.arange(64) * 64
    ov = np.clip(np.minimum(cs[:, None] + 32, ss[None, :] + 64) - np.maximum(cs[:, None], ss[None, :]), 0, None) / 32.0
    ov[NC:] = 0.0
    c["c2s"] = np.ascontiguousarray(ov.reshape(2, 128, 64).transpose(1, 0, 2)).astype(f)
    KF = np.zeros((NT, 128, 2, 64), f)
    CM = np.zeros((NT, 128, 2, 128), f)
    j = np.arange(64)[None, :]
    for ti in range(NT):
        t = ti * 128 + np.arange(128)[:, None]
        cur = t // 64
        visible = j <= cur
        forced = (j == 0) | (j == cur) | (j == cur - 1)
        KF[ti, :, 0, :] = (visible & ~forced)
        KF[ti, :, 1, :] = np.where(visible, np.where(forced, 1e4, 0.0), -1.0)
        n = np.arange(256)[None, :]
        valid = (16 * n + 31) <= t
        CM[ti] = np.where(valid, 0.0, -30000.0).reshape(128, 2, 128)
    c["KF"] = KF
    c["CMBT"] = CM
    c["Econst"] = (np.arange(T_)[None, :] // 64 == np.arange(64)[:, None]).astype(f)
    return c


def host_prep(inp, T_):
    f = np.float32
    d = dict(host_consts(T_))
    bc = lambda v, n=128: np.ascontiguousarray(np.broadcast_to(np.asarray(v, f).reshape(1, -1), (n, np.asarray(v).size)))
    d["w_in"] = np.ascontiguousarray(inp["w_in"][0][:, _w_in_perm()])
    d["w_out"] = np.ascontiguousarray(inp["w_out"][0])
    d["convw"] = np.ascontiguousarray(inp["gdn_conv_w"][0].reshape(4, 12, 128).transpose(2, 1, 0))
    d["alog"] = bc(inp["gdn_a_log"][0])
    d["dtb"] = bc(inp["gdn_dt_bias"][0])
    d["normw"] = bc(np.tile(inp["gdn_norm_w"][0], 4))
    d["ln1g"] = bc(inp["ln1_g"][0]); d["ln1b"] = bc(inp["ln1_b"][0])
    d["ln2g"] = bc(inp["ln2_g"][0]); d["ln2b"] = bc(inp["ln2_b"][0])
    d["ln3g"] = bc(inp["ln3_g"][0]); d["ln3b"] = bc(inp["ln3_b"][0])
    cw1 = []
    for nm in ("nsa_cmp_k_w1", "nsa_cmp_v_w1"):
        w = inp[nm][0].reshape(32, 64, 256).transpose(1, 0, 2)
        cw1.append(np.concatenate([w, w], 0))
    d["cw1"] = np.ascontiguousarray(np.stack(cw1))
    w2k = inp["nsa_cmp_k_w2"][0].reshape(2, 128, 64).transpose(1, 0, 2)
    d["cw2k"] = np.ascontiguousarray(np.concatenate([w2k, w2k], -1))
    d["cw2v"] = np.ascontiguousarray(inp["nsa_cmp_v_w2"][0].reshape(2, 128, 64).transpose(1, 0, 2))
    d["posT"] = np.ascontiguousarray(np.stack([np.concatenate([inp[nm][0].T] * 2, 0)
                                               for nm in ("nsa_cmp_pos_k", "nsa_cmp_pos_v")]))
    d["f1wg"] = np.ascontiguousarray(inp["ffn1_wg"][0])
    d["f1wu"] = np.ascontiguousarray(inp["ffn1_wu"][0])
    d["f1wd"] = np.ascontiguousarray(inp["ffn1_wd"][0])
    d["f2wg"] = np.ascontiguousarray(inp["ffn2_wg"][0])
    d["f2wu"] = np.ascontiguousarray(inp["ffn2_wu"][0])
    d["f2wd"] = np.ascontiguousarray(inp["ffn2_wd"][0])
    return d


CONST_SHAPES = None


def build_program(nseq, T_, phases=("ffn1", "b1", "b2", "ffn2"), dbg=False):
    nc = bass.Bass("TRN2", target_bir_lowering=False)
    ntok = nseq * T_
    NT = T_ // 128
    shp = {k_: v.shape for k_, v in host_consts(T_).items()}
    shp.update({"w_in": (D_MODEL, N_IN), "w_out": (D_MODEL, D_MODEL), "convw": (128, 12, 4), "alog": (128, 4), "dtb": (128, 4),
                "normw": (128, 512), "cw1": (2, 128, 32, 256), "cw2k": (128, 2, 128), "cw2v": (128, 2, 64), "posT": (2, 128, 32)})
    for i in (1, 2, 3):
        shp["ln%dg" % i] = (128, D_MODEL)
        shp["ln%db" % i] = (128, D_MODEL)
    for i in (1, 2):
        shp["f%dwg" % i] = (D_MODEL, D_FF)
        shp["f%dwu" % i] = (D_MODEL, D_FF)
        shp["f%dwd" % i] = (D_FF, D_MODEL)
    D = {}
    for nm, s in shp.items():
        D[nm] = nc.dram_tensor(nm, list(s), F32, kind="ExternalInput").ap()

    def act_tensor(nm, first_in, last_out):
        kind = "ExternalInput" if first_in else ("ExternalOutput" if last_out else "Internal")
        D[nm] = nc.dram_tensor(nm, [ntok, D_MODEL], F32, kind=kind).ap()
        D[nm + "_b"] = Buf(nm)
    act_tensor("x", "ffn1" in phases, False)
    act_tensor("x1", "ffn1" not in phases, phases[-1] == "ffn1")
    act_tensor("x2", False, phases[-1] == "b2")
    act_tensor("out", False, phases[-1] == "ffn2")
    for nm, s in (("ksT", [nseq, 128, T_]), ("ks1T", [nseq, 64, T_]), ("kwT", [nseq, 128, T_]), ("vs", [nseq, 128, NT, 2, 65]),
                  ("vw", [nseq, 128, NT, 2, 65]), ("kcmp", [nseq, 128, 256]), ("vcmp", [nseq, 128, 2, 2, 129])):
        D[nm] = nc.dram_tensor(nm, s, BF16, kind="Internal").ap()
    D["state_b"] = Buf("state")
    if dbg:
        D["dbg"] = nc.dram_tensor("dbg", [ntok, D_MODEL], F32, kind="ExternalOutput").ap()
        D["dbg_b"] = Buf("dbg")
    if phases[-1] == "b1":
        D["dummy"] = nc.dram_tensor("dummyo", [128, 128], F32, kind="ExternalOutput").ap()
    with ExitStack() as es:
        P = Prog(nc)
        banks = [T(es, nc, "bank%d" % i, [128, 512], F32, psum=True) for i in range(8)]
        if "ffn1" in phases:
            ffn_phase(P, D["x"], D["x_b"], D["x1"], D["x1_b"], D["f1wg"], D["f1wu"], D["f1wd"], D["ln1g"], D["ln1b"],
                      D["ident"], ntok, banks, pfx="f1_")
        if "b1" in phases:
            phase_b1(P, D, banks, nseq, T_)
        if phases[-1] == "b1":
            P.dma(D["dummy"], D["ident"])
        if "b2" in phases:
            phase_b2(P, D, banks, nseq, T_, dbg=dbg)
        if "ffn2" in phases:
            ffn_phase(P, D["x2"], D["x2_b"], D["out"], D["out_b"], D["f2wg"], D["f2wu"], D["f2wd"], D["ln3g"], D["ln3b"],
                      D["ident"], ntok, banks, pfx="f2_", pre_g=D["ln2g"], pre_b=D["ln2b"])
        P.emit(es)
    return nc


_NC_CACHE = {}


def kernel(**inputs):
    inputs = {k_: np.asarray(v) for k_, v in inputs.items()}
    T_ = SEQ
    nseq = SEQ_PER_CORE
    if "nc" not in _NC_CACHE:
        _NC_CACHE["nc"] = build_program(nseq, T_)
    nc = _NC_CACHE["nc"]
    d = host_prep(inputs, T_)
    x = inputs["x"].astype(np.float32, copy=False)
    in_maps = []
    for c in range(N_CORES):
        m = dict(d)
        m["x"] = np.ascontiguousarray(x[c * nseq:(c + 1) * nseq].reshape(nseq * T_, D_MODEL))
        in_maps.append(m)
    res = run_bass_kernel_spmd(nc, in_maps, core_ids=list(range(N_CORES)))
    out = np.concatenate([np.asarray(r["out"]).reshape(nseq, T_, D_MODEL) for r in res.results], axis=0)
    return out.astype(np.float32, copy=False)
```
